# Optimizing a Trainium2 kernel written in Bass

```python
import math
import jax
import jax.numpy as jnp
from jax import lax
import numpy as np

D_MODEL = 2048
BATCH = 8
SEQ = 4096
DEPTH = 4

ATT_PATTERNS = ((128, 1), (512, 4), (2048, 16))
N_GROUPS_A = len(ATT_PATTERNS)
HEADS_PER_GROUP = 8
HEAD_DIM = 64
N_HEADS_A = N_GROUPS_A * HEADS_PER_GROUP
QKV_WIDTH_A = N_HEADS_A * HEAD_DIM
WIDTH_A = HEADS_PER_GROUP * HEAD_DIM
ATT_BLOCK = 128
N_REL_BUCKETS = 32
REL_MAX_DIST = 2048
NEG_INF = -1e30
CHUNK = 128
WIDTH_B = 768
N_GROUPS_B = 6
GROUP_B = WIDTH_B // N_GROUPS_B
WIDTH_C = 768
SSM_GROUP = 16
N_GROUPS_C = WIDTH_C // SSM_GROUP
SSM_STATE = 64
DT_MIN = 1e-3
DT_MAX = 1e-1
N_BRANCH = 3
D_FF = -(-8 * D_MODEL // (3 * 256)) * 256
ALPHA = (2 * DEPTH) ** 0.25
BETA = (8 * DEPTH) ** -0.25
IN_SPLIT = (QKV_WIDTH_A, QKV_WIDTH_A, QKV_WIDTH_A, 2 * WIDTH_B, WIDTH_C, N_BRANCH * D_MODEL)
IN_COLS = sum(IN_SPLIT)

kernel_name = 'hybrid_dilated_attn_gmlp_s5_deepnorm'


def _layer_norm(x, g, b, eps=1e-5):
    xf = x.astype(jnp.float32)
    mu = jnp.mean(xf, axis=-1, keepdims=True)
    var = jnp.mean(jnp.square(xf - mu), axis=-1, keepdims=True)
    return ((xf - mu) * lax.rsqrt(var + eps) * g + b).astype(x.dtype)


def _t5_bucket(dist):
    max_exact = N_REL_BUCKETS // 2
    d = np.maximum(dist, 1).astype(np.float32)
    scale = (N_REL_BUCKETS - max_exact) / math.log(REL_MAX_DIST / max_exact)
    large = max_exact + (np.log(d / max_exact) * scale).astype(np.int32)
    large = np.minimum(large, N_REL_BUCKETS - 1)
    return np.where(dist < max_exact, dist, large).astype(np.int32)


def _band_steps():
    i = np.arange(ATT_BLOCK)[:, None]
    kk = np.arange(2 * ATT_BLOCK)[None, :]
    return ATT_BLOCK + i - kk


def _group_rel_bias(rel_bias, g, dilation):
    bucket = _t5_bucket(np.maximum(_band_steps(), 0) * dilation)
    cols = rel_bias[:, g * HEADS_PER_GROUP:(g + 1) * HEADS_PER_GROUP]
    return jnp.transpose(cols[bucket], (2, 0, 1)).astype(jnp.float32)


def _dilated_window_attention(q, k, v, bias, dilation, n_steps):
    bsz, s, h, hd = q.shape
    L = s // dilation
    nb = -(-L // ATT_BLOCK)
    lp = nb * ATT_BLOCK

    def to_streams(t):
        t = t.reshape(bsz, L, dilation, h, hd).transpose(0, 2, 1, 3, 4)
        return jnp.pad(t, ((0, 0), (0, 0), (0, lp - L), (0, 0), (0, 0)))

    def to_band(t):
        t = jnp.pad(t, ((0, 0), (0, 0), (ATT_BLOCK, 0), (0, 0), (0, 0)))
        t = t.reshape(bsz, dilation, nb + 1, ATT_BLOCK, h, hd)
        return jnp.concatenate([t[:, :, :-1], t[:, :, 1:]], axis=3)

    qb = to_streams(q).reshape(bsz, dilation, nb, ATT_BLOCK, h, hd)
    kb = to_band(to_streams(k))
    vb = to_band(to_streams(v))
    steps = _band_steps()
    key_idx = (np.arange(nb)[:, None] - 1) * ATT_BLOCK + np.arange(2 * ATT_BLOCK)[None, :]
    mask = ((steps >= 0) & (steps <= n_steps))[None, None] & (key_idx >= 0)[:, None, None, :]
    logits = jnp.einsum('brnqhd,brnkhd->brnhqk', qb, kb, preferred_element_type=jnp.float32)
    logits = jnp.where(mask, logits * (hd ** -0.5) + bias, NEG_INF)
    m = jnp.max(logits, axis=-1, keepdims=True)
    p = jnp.exp(logits - m)
    den = jnp.sum(p, axis=-1, keepdims=True)
    o = jnp.einsum('brnhqk,brnkhd->brnqhd', p, vb.astype(jnp.float32)) / jnp.swapaxes(den, 3, 4)
    lse = jnp.swapaxes((m + jnp.log(den))[..., 0], 3, 4)
    o = o.reshape(bsz, dilation, lp, h, hd)[:, :, :L].transpose(0, 2, 1, 3, 4).reshape(bsz, s, h, hd)
    lse = lse.reshape(bsz, dilation, lp, h)[:, :, :L].transpose(0, 2, 1, 3).reshape(bsz, s, h)
    return o, lse


def _spatial_gating(z, ln_g, ln_b, w_s, b_s):
    bsz, s, _ = z.shape
    u, v = jnp.split(z, 2, axis=-1)
    v = _layer_norm(v, ln_g, ln_b)
    vc = v.reshape(bsz, s // CHUNK, CHUNK, N_GROUPS_B, GROUP_B)
    w = jnp.tril(w_s)
    mixed = jnp.einsum('gts,bnsgc->bntgc', w, vc) + jnp.transpose(b_s)[:, :, None]
    return u * mixed.reshape(bsz, s, WIDTH_B)


def _ssm_combine(e1, e2):
    a1r, a1i, b1r, b1i = e1
    a2r, a2i, b2r, b2i = e2
    return (a2r * a1r - a2i * a1i,
            a2r * a1i + a2i * a1r,
            a2r * b1r - a2i * b1i + b2r,
            a2r * b1i + a2i * b1r + b2i)


def _s5(u, lam_re, lam_im, log_dt, b_re, b_im, c_re, c_im, d_skip):
    bsz, s, _ = u.shape
    f32 = jnp.float32
    uf = u.astype(f32)
    ug = uf.reshape(bsz, s, N_GROUPS_C, SSM_GROUP)
    lr = lam_re.astype(f32)
    li = lam_im.astype(f32)
    dt = jnp.exp(log_dt.astype(f32))[:, None]
    mag = jnp.exp(lr * dt)
    ab_re = mag * jnp.cos(li * dt)
    ab_im = mag * jnp.sin(li * dt)
    nrm = lr * lr + li * li
    cr = ((ab_re - 1.0) * lr + ab_im * li) / nrm
    ci = (ab_im * lr - (ab_re - 1.0) * li) / nrm
    bb_re = cr[..., None] * b_re - ci[..., None] * b_im
    bb_im = cr[..., None] * b_im + ci[..., None] * b_re
    bu_re = jnp.einsum('bsgh,gph->bsgp', ug, bb_re.astype(f32))
    bu_im = jnp.einsum('bsgh,gph->bsgp', ug, bb_im.astype(f32))
    a_re = jnp.broadcast_to(ab_re[None, None], (1, s, N_GROUPS_C, SSM_STATE))
    a_im = jnp.broadcast_to(ab_im[None, None], (1, s, N_GROUPS_C, SSM_STATE))
    _, _, xr, xi = lax.associative_scan(_ssm_combine, (a_re, a_im, bu_re, bu_im), axis=1)
    y = (jnp.einsum('bsgp,ghp->bsgh', xr, c_re.astype(f32))
         - jnp.einsum('bsgp,ghp->bsgh', xi, c_im.astype(f32)))
    return y.reshape(bsz, s, WIDTH_C) + d_skip.astype(f32) * uf


def setup_inputs(seed: int = 0) -> dict:
    key = jax.random.key(seed)
    ks = iter(jax.random.split(key, 32))
    nrm = lambda shape, scale: jax.random.normal(next(ks), shape, jnp.float32) * scale
    gain = lambda shape: 1.0 + nrm(shape, 0.02)
    n_idx = jnp.arange(SSM_STATE, dtype=jnp.float32)
    return {
        'x': nrm((BATCH, SEQ, D_MODEL), 1.0),
        'w_in': nrm((DEPTH, D_MODEL, IN_COLS), D_MODEL ** -0.5),
        'b_in': nrm((DEPTH, IN_COLS), 0.02),
        'rel_bias': nrm((N_REL_BUCKETS, N_HEADS_A), 0.1),
        'sgu_ln_g': gain((DEPTH, WIDTH_B)),
        'sgu_ln_b': nrm((DEPTH, WIDTH_B), 0.02),
        'w_s': nrm((DEPTH, N_GROUPS_B, CHUNK, CHUNK), CHUNK ** -0.5),
        'b_s': gain((DEPTH, N_GROUPS_B, CHUNK)),
        'lam_re': -0.5 + nrm((DEPTH, N_GROUPS_C, SSM_STATE), 0.01),
        'lam_im': math.pi * n_idx + nrm((DEPTH, N_GROUPS_C, SSM_STATE), 0.01),
        'log_dt': jax.random.uniform(next(ks), (DEPTH, N_GROUPS_C), jnp.float32,
                                     minval=math.log(DT_MIN), maxval=math.log(DT_MAX)),
        'b_re': nrm((DEPTH, N_GROUPS_C, SSM_STATE, SSM_GROUP), (2 * SSM_GROUP) ** -0.5),
        'b_im': nrm((DEPTH, N_GROUPS_C, SSM_STATE, SSM_GROUP), (2 * SSM_GROUP) ** -0.5),
        'c_re': nrm((DEPTH, N_GROUPS_C, SSM_GROUP, SSM_STATE), SSM_STATE ** -0.5),
        'c_im': nrm((DEPTH, N_GROUPS_C, SSM_GROUP, SSM_STATE), SSM_STATE ** -0.5),
        'd_skip': nrm((DEPTH, WIDTH_C), 1.0),
        'w_glu': nrm((DEPTH, WIDTH_C, WIDTH_C), WIDTH_C ** -0.5),
        'b_glu': nrm((DEPTH, WIDTH_C), 0.02),
        'w_pa': nrm((DEPTH, WIDTH_A, D_MODEL), WIDTH_A ** -0.5),
        'w_pb': nrm((DEPTH, WIDTH_B, D_MODEL), WIDTH_B ** -0.5),
        'w_pc': nrm((DEPTH, WIDTH_C, D_MODEL), WIDTH_C ** -0.5),
        'w_o': nrm((DEPTH, D_MODEL, D_MODEL), BETA * D_MODEL ** -0.5),
        'ln1_g': gain((DEPTH, D_MODEL)),
        'ln1_b': nrm((DEPTH, D_MODEL), 0.02),
        'w_ffn_in': nrm((DEPTH, D_MODEL, 2 * D_FF), D_MODEL ** -0.5),
        'w_ffn_out': nrm((DEPTH, D_FF, D_MODEL), BETA * D_FF ** -0.5),
        'ln2_g': gain((DEPTH, D_MODEL)),
        'ln2_b': nrm((DEPTH, D_MODEL), 0.02),
    }


def reference(x, w_in, b_in, rel_bias, sgu_ln_g, sgu_ln_b, w_s, b_s, lam_re, lam_im, log_dt,
              b_re, b_im, c_re, c_im, d_skip, w_glu, b_glu, w_pa, w_pb, w_pc, w_o,
              ln1_g, ln1_b, w_ffn_in, w_ffn_out, ln2_g, ln2_b):
    dt = x.dtype
    bsz, s, _ = x.shape
    offs = np.cumsum(IN_SPLIT)[:-1].tolist()
    group_bias = [_group_rel_bias(rel_bias, g, dil) for g, (_, dil) in enumerate(ATT_PATTERNS)]
    for l in range(DEPTH):
        proj = x @ w_in[l] + b_in[l]
        q, k, v, zb, uc, gl = jnp.split(proj, offs, axis=-1)
        q = q.reshape(bsz, s, N_HEADS_A, HEAD_DIM)
        k = k.reshape(bsz, s, N_HEADS_A, HEAD_DIM)
        v = v.reshape(bsz, s, N_HEADS_A, HEAD_DIM)
        outs, lses = [], []
        for g, (window, dil) in enumerate(ATT_PATTERNS):
            sl = slice(g * HEADS_PER_GROUP, (g + 1) * HEADS_PER_GROUP)
            o_g, lse_g = _dilated_window_attention(q[:, :, sl], k[:, :, sl], v[:, :, sl],
                                                   group_bias[g], dil, window // dil)
            outs.append(o_g)
            lses.append(lse_g)
        wts = jax.nn.softmax(jnp.stack(lses, axis=0), axis=0)
        ya = jnp.sum(wts[..., None] * jnp.stack(outs, axis=0), axis=0)
        ya = ya.reshape(bsz, s, WIDTH_A).astype(dt)
        yb = _spatial_gating(jax.nn.gelu(zb), sgu_ln_g[l], sgu_ln_b[l], w_s[l], b_s[l]).astype(dt)
        yc = jax.nn.gelu(_s5(uc, lam_re[l], lam_im[l], log_dt[l], b_re[l], b_im[l],
                             c_re[l], c_im[l], d_skip[l]))
        yc = (yc * jax.nn.sigmoid(yc @ w_glu[l] + b_glu[l])).astype(dt)
        gates = jax.nn.sigmoid(gl.reshape(bsz, s, N_BRANCH, D_MODEL))
        merged = (gates[:, :, 0] * (ya @ w_pa[l]) + gates[:, :, 1] * (yb @ w_pb[l])
                  + gates[:, :, 2] * (yc @ w_pc[l]))
        x = _layer_norm(ALPHA * x + merged @ w_o[l], ln1_g[l], ln1_b[l]).astype(dt)
        gate_f, up = jnp.split(x @ w_ffn_in[l], 2, axis=-1)
        f = (jax.nn.silu(gate_f) * up) @ w_ffn_out[l]
        x = _layer_norm(ALPHA * x + f, ln2_g[l], ln2_b[l]).astype(dt)
    return x
```

```python
import math
import os
from contextlib import ExitStack

import numpy as np
import ml_dtypes

import concourse.bass as bass
import concourse.mybir as mybir
from concourse.bass_utils import run_bass_kernel_spmd

F32 = mybir.dt.float32
F32R = mybir.dt.float32r
BF16 = mybir.dt.bfloat16
I32 = mybir.dt.int32
AF = mybir.ActivationFunctionType
ALU = mybir.AluOpType

S = 4096
D = 2048
TT = 512
NTT = S // TT
DEPTH = 4
NCC = 102
Q_OFF, K_OFF, V_OFF, U_OFF, VG_OFF, UC_OFF, GL_OFF = 0, 1536, 3072, 4608, 5376, 6144, 6912
DFF = 5632
ALPHA = (2 * DEPTH) ** 0.25
DILS = (1, 4, 16)
EW = 383
NEG = -30000.0
TWO_PI = 2.0 * math.pi

O_BA, O_SG, O_SB, O_LR, O_LI, O_LD = 0, 102, 108, 114, 138, 162
O_BRE, O_BIM, O_CRE, O_CIM = 186, 570, 954, 1338
O_DSK, O_BGLU, O_L1G, O_L1B, O_L2G, O_L2B = 1722, 1728, 1734, 1750, 1766, 1782
NSP = 1798

SAME_ENGINE_SYNC = bool(int(os.environ.get("K_SES", "1")))


class Res:
    __slots__ = ("name", "w", "r")

    def __init__(self, name):
        self.name = name
        self.w = {}
        self.r = {}


class KB:
    def __init__(self, nc, es):
        self.nc = nc
        self.es = es
        self.eng = {"pe": nc.tensor, "act": nc.scalar, "dve": nc.vector, "pool": nc.gpsimd, "sp": nc.sync}
        self.sem = {}
        self.cnt = {}
        self.waited = {}
        self.resd = {}
        for e in ("pe", "act", "dve", "pool"):
            self._mksem(e)

    def _mksem(self, key):
        if key not in self.sem:
            self.sem[key] = self.es.enter_context(self.nc.semaphore("s_" + key))
            self.cnt[key] = 0
        return self.sem[key]

    def R(self, *key):
        r = self.resd.get(key)
        if r is None:
            r = Res(str(key))
            self.resd[key] = r
        return r

    def _wait(self, e, evs):
        for key, val in evs.items():
            if key == e and not SAME_ENGINE_SYNC:
                continue
            if key == "pe" and e == "pe":
                continue
            if self.waited.get((e, key), 0) >= val:
                continue
            self.eng[e].wait_ge(self.sem[key], val)
            self.waited[(e, key)] = val

    @staticmethod
    def _merge(d, s):
        for k, v in s.items():
            if d.get(k, 0) < v:
                d[k] = v

    def _deps(self, reads, writes):
        evs = {}
        for r in reads:
            self._merge(evs, r.w)
        for w in writes:
            self._merge(evs, w.w)
            self._merge(evs, w.r)
        return evs

    def _commit(self, ev, reads, writes):
        k, v = ev
        for r in reads:
            if r.r.get(k, 0) < v:
                r.r[k] = v
        for w in writes:
            w.w = {k: v}
            w.r = {}

    def op(self, e, fn, reads=(), writes=()):
        self._wait(e, self._deps(reads, writes))
        ins = fn()
        self.cnt[e] += 1
        ins.then_inc(self.sem[e], 1)
        self._commit((e, self.cnt[e]), reads, writes)

    def mm(self, out, pairs, reads, writes, transpose=False):
        self._wait("pe", self._deps(reads, writes))
        n = len(pairs)
        ins = None
        for i, (a, b) in enumerate(pairs):
            ins = self.nc.tensor.matmul(out, lhsT=a, rhs=b, start=(i == 0), stop=(i == n - 1))
        self.cnt["pe"] += 1
        ins.then_inc(self.sem["pe"], 1)
        self._commit(("pe", self.cnt["pe"]), reads, writes)

    def tr(self, out, in_, ident, reads, writes):
        self._wait("pe", self._deps(reads, writes))
        ins = self.nc.tensor.transpose(out, in_, ident)
        self.cnt["pe"] += 1
        ins.then_inc(self.sem["pe"], 1)
        self._commit(("pe", self.cnt["pe"]), reads, writes)

    def dma(self, q, out, in_, reads, writes, semkey, **kw):
        self._mksem(semkey)
        self._wait(q, self._deps(reads, writes))
        ins = self.eng[q].dma_start(out=out, in_=in_, **kw)
        self.cnt[semkey] += 16
        ins.then_inc(self.sem[semkey], 16)
        self._commit((semkey, self.cnt[semkey]), reads, writes)

    def barrier(self):
        evs = {k: v for k, v in self.cnt.items() if v > 0}
        for e in ("pe", "act", "dve", "pool", "sp"):
            for key, val in evs.items():
                if self.waited.get((e, key), 0) >= val:
                    continue
                self.eng[e].wait_ge(self.sem[key], val)
                self.waited[(e, key)] = val


def bc_mid(ap, n):
    a = ap.ap
    return bass.AP(tensor=ap.tensor, offset=ap.offset, ap=[list(a[0]), [0, n], list(a[1])])


def bc_last(ap, n):
    a = ap.ap
    return bass.AP(tensor=ap.tensor, offset=ap.offset, ap=[list(a[0]), list(a[1]), [0, n]])


class WStream:
    def __init__(self, kb, es, nslots, slot_elems):
        self.kb = kb
        self.n = nslots
        self.tiles = [es.enter_context(kb.nc.sbuf_tensor("wsl%d" % i, [128, slot_elems], BF16)) for i in range(nslots)]
        self.res = [Res("wsl%d" % i) for i in range(nslots)]
        self.sched = []
        self.issued = 0
        self.consumed = 0

    def plan(self, tag, dram_ap, nelem, dres):
        self.sched.append((tag, dram_ap, nelem, dres))

    def get(self, tag):
        i = self.consumed
        assert self.sched[i][0] == tag, (self.sched[i][0], tag)
        while self.issued < min(len(self.sched), i + self.n - 1):
            k = self.issued
            _, dap, ne, dres = self.sched[k]
            sl = k % self.n
            self.kb.dma("sp", self.tiles[sl][:, 0:ne], dap, reads=dres, writes=[self.res[sl]], semkey="wsl%d" % sl)
            self.issued += 1
        self.consumed += 1
        sl = i % self.n
        return self.tiles[sl], self.res[sl]


class Prog:
    def __init__(self, nlayers=DEPTH, phases="ABCDEF", debug=False):
        self.nlayers = nlayers
        self.phases = phases
        self.debug = debug
        self.nc = bass.Bass("TRN2", target_bir_lowering=False)
        self.build()

    def dram(self, name, shape, dt, kind):
        return self.nc.dram_tensor(name, list(shape), dt, kind=kind)

    def build(self):
        nc = self.nc
        dk = "ExternalOutput" if self.debug else "Internal"
        self.d_xT = self.dram("xT", [D, S], F32, "ExternalInput")
        self.d_w_in = self.dram("w_in", [DEPTH, D, NCC * 128], F32, "ExternalInput")
        self.d_w_pa = self.dram("w_pa", [DEPTH, 512, D], F32, "ExternalInput")
        self.d_w_pb = self.dram("w_pb", [DEPTH, 768, D], F32, "ExternalInput")
        self.d_w_pc = self.dram("w_pc", [DEPTH, 768, D], F32, "ExternalInput")
        self.d_w_o = self.dram("w_o", [DEPTH, D, D], F32, "ExternalInput")
        self.d_w_glu = self.dram("w_glu", [DEPTH, 768, 768], F32, "ExternalInput")
        self.d_w_f1 = self.dram("w_ffn_in", [DEPTH, D, 2 * DFF], F32, "ExternalInput")
        self.d_w_f2 = self.dram("w_ffn_out", [DEPTH, DFF, D], F32, "ExternalInput")
        self.d_w_s = self.dram("w_s", [DEPTH, 6, 128, 128], F32, "ExternalInput")
        self.d_b_s = self.dram("b_s", [DEPTH, 768], F32, "ExternalInput")
        self.d_sp = self.dram("smallp", [DEPTH, 128, NSP], F32, "ExternalInput")
        self.d_relb = self.dram("rel_bias", [32, 24], F32, "ExternalInput")
        self.d_cf = self.dram("constf", [128, 1664], F32, "ExternalInput")
        self.d_oh = self.dram("onehot", [3, 33, EW], F32, "ExternalInput")
        self.d_out = self.dram("outT", [D, S], F32, "ExternalOutput")
        self.d_P = self.dram("P", [NCC * 128, S], BF16, dk)
        self.d_YA = self.dram("YA", [512, S], BF16, dk)
        self.d_YB = self.dram("YB", [768, S], BF16, dk)
        self.d_YC0 = self.dram("YC0", [768, S], BF16, dk)
        self.d_XT = [self.dram("XTa", [D, S], F32, dk), self.dram("XTb", [D, S], F32, dk)]
        self.d_E = self.dram("Ed", [24, EW], F32, "Internal")
        self.d_Z = self.dram("Zd", [24, 128 * EW], F32, "Internal")
        self.d_WBin = [self.dram("WBin%d" % i, [NCC, 128, 2048], BF16, "Internal") for i in range(2)]
        self.d_WBm = [self.dram("WBm%d" % i, [16, 128, 2048], BF16, "Internal") for i in range(2)]
        self.d_WBo = [self.dram("WBo%d" % i, [16, 128, 2048], BF16, "Internal") for i in range(2)]
        self.d_WBf1 = [self.dram("WBf1%d" % i, [88, 128, 2048], BF16, "Internal") for i in range(2)]
        self.d_WBf2 = [self.dram("WBf2%d" % i, [32, 128, 2816], BF16, "Internal") for i in range(2)]
        self.d_WBg = [self.dram("WBg%d" % i, [2, 128, 2304], BF16, "Internal") for i in range(2)]

        with ExitStack() as es:
            self.es = es
            kb = self.kb = KB(nc, es)
            blk = es.enter_context(nc.Block())

            @blk.sync
            def _(sync):
                self.emit()

    def sb(self, es, name, shape, dt):
        self._uid = getattr(self, "_uid", 0) + 1
        t = es.enter_context(self.nc.sbuf_tensor("%s_%d" % (name, self._uid), list(shape), dt))
        sz = int(np.prod(shape[1:])) * (2 if dt == BF16 else 4)
        self._cur = getattr(self, "_cur", 0) + sz
        self._peak = max(getattr(self, "_peak", 0), self._cur)
        if os.environ.get("K_MEM"):
            print("SB alloc", name, sz, "cur", self._cur)

        def _free():
            self._cur -= sz
        es.callback(_free)
        return t

    def psum(self):
        i = self.ps_i % 8
        self.ps_i += 1
        return self.ps_tiles[i], self.ps_res[i]

    def V(self, fn, reads=(), writes=()):
        self.kb.op("dve", fn, reads, writes)

    def A(self, fn, reads=(), writes=()):
        self.kb.op("act", fn, reads, writes)

    def emit(self):
        nc, kb, es = self.nc, self.kb, self.es
        self.ps_tiles = [es.enter_context(nc.psum_tensor("ps%d" % i, [128, 512], F32)) for i in range(8)]
        self.ps_res = [Res("ps%d" % i) for i in range(8)]
        self.ps_i = 0
        self.ws = WStream(kb, es, 6, 2816)
        self.cf = self.sb(es, "cf", [128, 1664], F32)
        self.r_cf = Res("cf")
        kb.dma("sp", self.cf[:], self.d_cf.ap(), [], [self.r_cf], "cst")
        self.identF = self.cf[:, 0:128]
        self.tril = self.cf[:, 128:256]
        self.iota = self.cf[:, 256:768]
        self.sel = self.cf[:, 768:896]
        self.onesF = self.cf[:, 1024:1152]
        self.onesR = self.cf[:, 1024:1152].bitcast(F32R)
        self.identB = self.sb(es, "identB", [128, 128], BF16)
        self.onesB = self.sb(es, "onesB", [128, 128], BF16)
        self.r_ib = Res("identB")
        self.V(lambda: nc.vector.tensor_copy(self.identB[:], self.identF), [self.r_cf], [self.r_ib])
        self.V(lambda: nc.vector.tensor_copy(self.onesB[:], self.onesF), [self.r_cf], [self.r_ib])
        self.sp = self.sb(es, "sp", [128, NSP], F32)
        self.r_sp = Res("sp")

        self.cast_q = []
        self.plan_weights()
        if "B" in self.phases:
            self.bias_setup()
        self.cast_weights(0)
        self.pump_casts(NCC)
        for l in range(self.nlayers):
            self.l = l
            self.par = l % 2
            self.x_in = self.d_xT if l == 0 else self.d_XT[(l - 1) % 2]
            self.x_out = self.d_out if l == self.nlayers - 1 else self.d_XT[l % 2]
            kb.dma("sp", self.sp[:], self.d_sp.ap()[l], [], [self.r_sp], "spl")
            if l + 1 < self.nlayers and l > 0:
                self.cast_weights(l + 1)
                if "E" not in self.phases:
                    self.pump_casts()
            if "A" in self.phases:
                ag = self.phase_A_gen()
                for _ in range(self.NQ * 6):
                    next(ag)
                self.a_emitted = self.NQ * 6
                self.a_done = False
                if "D" in self.phases:
                    self.phase_D(ag)
                while self.a_emitted < self.NQ * (6 + 48):
                    self.pumpA(ag)
                kb.barrier()
                if "B" in self.phases:
                    self.phase_B(ag)
                    kb.barrier()
                if "C" in self.phases:
                    self.phase_C(ag)
                for _ in ag:
                    pass
                kb.barrier()
            if l == 0:
                self.pump_casts()
                if l + 1 < self.nlayers:
                    self.cast_weights(l + 1)
            if "E" in self.phases:
                self.phase_EF()
                self.pump_casts()
                kb.barrier()
        kb.barrier()

    def cast_weights(self, l):
        kb = self.kb
        par = l % 2
        q = self.cast_q

        def cast(dst_t, tile, src_t, src_off, row_stride, kch, ncol_tile, resname, kofs=0):
            dst_elems = dst_t.ap().shape[2]
            src = bass.AP(tensor=src_t, offset=src_off, ap=[[row_stride, 128], [128 * row_stride, kch], [1, ncol_tile]])
            dst = bass.AP(tensor=dst_t, offset=tile * 128 * dst_elems + kofs * ncol_tile,
                          ap=[[dst_elems, 128], [ncol_tile, kch], [1, ncol_tile]])
            q.append((dst, src, (resname, par), "cast_%s_%d" % (resname, par)))

        seen = set()
        for (q_, cc) in self.a_order():
            if cc in seen:
                continue
            seen.add(cc)
            cast(self.d_WBin[par], cc, self.d_w_in, l * D * 13056 + cc * 128, 13056, 16, 128, "win%d" % self.win_group(cc))
        for h in range(2):
            cast(self.d_WBg[par], h, self.d_w_glu, l * 768 * 768 + h * 3 * 128 * 768, 768, 3, 768, "wglu")
        for dc in range(16):
            cast(self.d_WBm[par], dc, self.d_w_pa, l * 512 * D + dc * 128, D, 4, 128, "wm", kofs=0)
            cast(self.d_WBm[par], dc, self.d_w_pb, l * 768 * D + dc * 128, D, 6, 128, "wm", kofs=4)
            cast(self.d_WBm[par], dc, self.d_w_pc, l * 768 * D + dc * 128, D, 6, 128, "wm", kofs=10)
        for dc in range(16):
            cast(self.d_WBo[par], dc, self.d_w_o, l * D * D + dc * 128, D, 16, 128, "wo")
        for j in range(88):
            cast(self.d_WBf1[par], j, self.d_w_f1, l * D * 2 * DFF + j * 128, 2 * DFF, 16, 128, "wf1")
        for dc in range(16):
            for h in range(2):
                cast(self.d_WBf2[par], dc * 2 + h, self.d_w_f2, l * DFF * D + h * 22 * 128 * D + dc * 128, D, 22, 128, "wf2")

    def win_group(self, cc):
        if not hasattr(self, "_wing"):
            order = []
            for (q_, c) in self.a_order():
                if c not in order:
                    order.append(c)
            self._wing = {c: i // 9 for i, c in enumerate(order)}
        return self._wing[cc]

    def pump_casts(self, n=None):
        kb = self.kb
        while self.cast_q and (n is None or n > 0):
            dst, src, rkey, sem = self.cast_q.pop(0)
            kb.dma("pool", dst, src, [], [kb.R(*rkey)], sem)
            if n is not None:
                n -= 1

    def plan_weights(self):
        kb = self.kb
        ws = self.ws
        for l in range(self.nlayers):
            par = l % 2
            if "A" in self.phases:
                for (q, cc) in self.a_order():
                    ws.plan(("A", l, q, cc), self.d_WBin[par].ap()[cc], 2048, [kb.R("win%d" % self.win_group(cc), par)])
            if "E" in self.phases:
                for tt in range(NTT):
                    for h in range(2):
                        ws.plan(("G", l, tt, h), self.d_WBg[par].ap()[h], 2304, [kb.R("wglu", par)])
                    for dc in range(16):
                        ws.plan(("M", l, tt, dc), self.d_WBm[par].ap()[dc], 2048, [kb.R("wm", par)])
                    for dc in range(16):
                        ws.plan(("O", l, tt, dc), self.d_WBo[par].ap()[dc], 2048, [kb.R("wo", par)])
                    for j in range(44):
                        ws.plan(("F1g", l, tt, j), self.d_WBf1[par].ap()[j], 2048, [kb.R("wf1", par)])
                        ws.plan(("F1u", l, tt, j), self.d_WBf1[par].ap()[44 + j], 2048, [kb.R("wf1", par)])
                    for dc in range(16):
                        for h in range(2):
                            ws.plan(("F2", l, tt, dc, h), self.d_WBf2[par].ap()[dc * 2 + h], 2816, [kb.R("wf2", par)])

    NQ = 4

    def a_groups(self):
        ucs = list(range(UC_OFF // 128, UC_OFF // 128 + 6))
        mix = list(range(0, UC_OFF // 128))
        gates = list(range(GL_OFF // 128, NCC))
        return [ucs, mix, gates]

    def a_order(self):
        out = []
        for gi, grp in enumerate(self.a_groups()):
            out += [(q, cc) for q in range(self.NQ) for cc in grp]
        return out

    def pumpA(self, ag, n=1):
        for _ in range(n):
            if self.a_done:
                return
            if next(ag) == "hold":
                self.a_done = True
            else:
                self.a_emitted += 1

    def phase_A_gen(self):
        nc, kb, l = self.nc, self.kb, self.l
        QS = S // self.NQ
        with ExitStack() as es:
            xb = self.sb(es, "xb", [128, 16, QS], BF16)
            r_xb = Res("xb")
            oA = [self.sb(es, "oA%d" % i, [128, QS], BF16) for i in range(2)]
            r_oA = [Res("oA%d" % i) for i in range(2)]
            xin = self.x_in.ap().rearrange("(k p) s -> p k s", p=128)
            cur = None
            it = 0
            first_rest = True
            for (q, cc) in self.a_order():
                gid = 0 if UC_OFF // 128 <= cc < UC_OFF // 128 + 6 else (1 if cc < UC_OFF // 128 else 2)
                if cur != (q, gid):
                    cur = (q, gid)
                    for k4 in range(4):
                        kb.dma("pool", xb[:, k4 * 4:(k4 + 1) * 4, :], xin[:, k4 * 4:(k4 + 1) * 4, q * QS:(q + 1) * QS],
                               [kb.R("X", l, t) for t in range(NTT)], [r_xb], "xbld")
                wt, r_w = self.ws.get(("A", l, q, cc))
                w3 = wt[:, 0:2048].rearrange("p (k c) -> p k c", c=128)
                if cc < 36 or 48 <= cc < 54:
                    fn = AF.Identity
                elif cc < 48:
                    fn = AF.Gelu_apprx_tanh
                else:
                    fn = AF.Sigmoid
                sl = it % 2
                it += 1
                for t in range(QS // TT):
                    ps, r_ps = self.psum()
                    kb.mm(ps[:], [(w3[:, k, :], xb[:, k, t * TT:(t + 1) * TT]) for k in range(16)],
                          [r_w, r_xb], [r_ps])
                    self.A(lambda: nc.scalar.activation(out=oA[sl][:, t * TT:(t + 1) * TT], in_=ps[:], func=fn,
                                                        bias=self.sp[:, O_BA + cc:O_BA + cc + 1], scale=1.0),
                           [r_ps, self.r_sp], [r_oA[sl]])
                kb.dma("pool", self.d_P.ap()[cc * 128:(cc + 1) * 128, q * QS:(q + 1) * QS], oA[sl][:],
                       [r_oA[sl]], [kb.R("P", cc)], "oAst%d" % sl)
                yield
            yield "hold"

    def bias_setup(self):
        nc, kb = self.nc, self.kb
        with ExitStack() as es:
            relb = self.sb(es, "relb", [33, 24], F32)
            oh = self.sb(es, "oh", [33, 3, EW], F32)
            eo = self.sb(es, "eo", [8, 3, EW], F32)
            r1, r2, r3 = Res("relb"), Res("oh"), Res("eo")
            self.V(lambda: nc.vector.memset(relb[:], 1.0), [], [r1])
            kb.dma("sp", relb[0:32, :], self.d_relb.ap(), [], [r1], "bs1")
            kb.dma("sp", oh[:], self.d_oh.ap().rearrange("g b s -> b g s"), [], [r2], "bs2")
            for g in range(3):
                ps, r_ps = self.psum()
                kb.mm(ps[0:8, 0:EW], [(relb[:, g * 8:(g + 1) * 8], oh[:, g, :])], [r1, r2], [r_ps])
                self.V(lambda: nc.vector.tensor_copy(eo[:, g, :], ps[0:8, 0:EW]), [r_ps], [r3])
                kb.dma("sp", self.d_E.ap()[g * 8:(g + 1) * 8, :], eo[:, g, :], [r3], [kb.R("E")], "bs3")
            src = bass.AP(tensor=self.d_E, offset=0, ap=[[EW, 24], [0, 128], [1, EW]])
            dst = bass.AP(tensor=self.d_Z, offset=0, ap=[[128 * EW, 24], [EW, 128], [1, EW]])
            kb.dma("sp", dst, src, [kb.R("E")], [kb.R("Z")], "bs4")
            kb.barrier()

    def phase_B(self, ag):
        nc, kb, l = self.nc, self.kb, self.l
        with ExitStack() as es:
            qkv = [self.sb(es, "qkv%d" % i, [64, 3, S], BF16) for i in range(2)]
            r_qkv = [Res("qkv%d" % i) for i in range(2)]
            va = [self.sb(es, "va%d" % i, [128, 32, 128], BF16) for i in range(2)]
            r_va = [Res("va%d" % i) for i in range(2)]
            bm = [self.sb(es, "bm%d" % i, [128, 2, 128], F32) for i in range(2)]
            r_bm = [Res("bm%d" % i) for i in range(2)]
            acc = [self.sb(es, "acc%d" % i, [128, S], F32) for i in range(1)] * 2
            r_acc = [Res("acc%d" % i) for i in range(1)] * 2
            tq = [self.sb(es, "tq%d" % i, [128, 2, 128], F32) for i in range(4)]
            r_tq = [Res("tq%d" % i) for i in range(4)]
            NPM = 12
            LAG = 9
            pm = [self.sb(es, "pm%d" % i, [128, 2, 128], BF16) for i in range(NPM)]
            r_pm = [Res("pm%d" % i) for i in range(NPM)]
            yo = [self.sb(es, "yo%d" % i, [64, S], BF16) for i in range(1)] * 2
            r_yo = [Res("yo%d" % i) for i in range(1)] * 2
            rd = [self.sb(es, "rd%d" % i, [64, TT], F32) for i in range(2)]
            r_rd = [Res("rd%d" % i) for i in range(2)]
            for i in range(2):
                self.V(lambda: nc.vector.memset(va[i][:, :, 64:128], 1.0), [], [r_va[i]])

            def load(it):
                hl, g = divmod(it, 3)
                head = g * 8 + hl
                sl = it % 2
                for j, off in enumerate((Q_OFF, K_OFF, V_OFF)):
                    row = off + head * 64
                    kb.dma("sp", qkv[sl][:, j, :], self.d_P.ap()[row:row + 64, :], [kb.R("P", row // 128)], [r_qkv[sl]],
                           "qkvld%d" % sl)
                src = bass.AP(tensor=self.d_Z, offset=head * 128 * EW + 127, ap=[[EW - 1, 128], [128, 2], [1, 128]])
                kb.dma("sp", bm[sl][:], src, [kb.R("Z")], [r_bm[sl]], "bmld%d" % sl)

            load(0)
            blk_i = 0
            for it in range(24):
                hl, g = divmod(it, 3)
                r = DILS[g]
                nb = 32 // r
                sl = it % 2
                if it + 1 < 24:
                    load(it + 1)
                q = qkv[sl]
                a = acc[hl % 2]
                r_a = r_acc[hl % 2]

                def tok(c, n):
                    st = 128 * n * r + c
                    return slice(st, st + 127 * r + 1, r)

                for b8 in range(4):
                    ps, r_ps = self.psum()
                    psb = ps[:].bitcast(BF16)
                    for j in range(8):
                        b = b8 * 8 + j
                        c, n = divmod(b, nb)
                        kb.tr(psb[:, j * 64:(j + 1) * 64], q[:, 2, tok(c, n)], self.identB[0:64, 0:64],
                              [r_qkv[sl], self.r_ib], [r_ps])
                    self.V(lambda: nc.vector.tensor_copy(va[sl][:, b8 * 8:(b8 + 1) * 8, 0:64],
                                                         psb[:, 0:512].rearrange("p (j d) -> p j d", d=64)),
                           [r_ps], [r_va[sl]])
                pend = []

                def emit_pv(b, pi):
                    c, n = divmod(b, nb)
                    ps2, r_ps2 = self.psum()
                    pairs = [(va[sl][:, b, :], pm[pi][:, 0, :])]
                    if n > 0:
                        pairs.append((va[sl][:, b - 1, :], pm[pi][:, 1, :]))
                    kb.mm(ps2[:, 0:128], pairs, [r_va[sl], r_pm[pi]], [r_ps2])
                    if g == 0:
                        self.V(lambda: nc.vector.tensor_copy(a[:, tok(c, n)], ps2[:, 0:128]), [r_ps2], [r_a])
                    else:
                        self.V(lambda: nc.vector.tensor_tensor(out=a[:, tok(c, n)], in0=a[:, tok(c, n)], in1=ps2[:, 0:128],
                                                               op=ALU.add), [r_ps2, r_a], [r_a])

                for b in range(32):
                    if blk_i % 20 == 0:
                        self.pumpA(ag, 2)
                    c, n = divmod(b, nb)
                    np_ = 1 if n == 0 else 2
                    ps, r_ps = self.psum()
                    sc = ps[:, 0:256].rearrange("p (a q) -> p a q", q=128)
                    kb.mm(sc[:, 0, :], [(q[:, 1, tok(c, n)], q[:, 0, tok(c, n)])], [r_qkv[sl]], [r_ps])
                    if n > 0:
                        kb.mm(sc[:, 1, :], [(q[:, 1, tok(c, n - 1)], q[:, 0, tok(c, n)])], [r_qkv[sl]], [r_ps])
                    ti = blk_i % 4
                    pi = blk_i % NPM
                    blk_i += 1
                    self.V(lambda: nc.vector.scalar_tensor_tensor(out=tq[ti][:, 0:np_, :], in0=sc[:, 0:np_, :], scalar=0.125,
                                                                  in1=bm[sl][:, 0:np_, :], op0=ALU.mult, op1=ALU.add),
                           [r_ps, r_bm[sl]], [r_tq[ti]])
                    self.A(lambda: nc.scalar.activation(out=pm[pi][:, 0:np_, :], in_=tq[ti][:, 0:np_, :], func=AF.Exp),
                           [r_tq[ti]], [r_pm[pi]])
                    pend.append((b, pi))
                    if len(pend) > LAG:
                        emit_pv(*pend.pop(0))
                while pend:
                    emit_pv(*pend.pop(0))
                if g == 2:
                    ysl = hl % 2
                    for t in range(NTT):
                        ps, r_ps = self.psum()
                        kb.mm(ps[:, :], [(self.sel, a[:, t * TT:(t + 1) * TT])], [self.r_cf, r_a], [r_ps])
                        di = t % 2
                        self.A(lambda: nc.scalar.activation(out=rd[di][:], in_=ps[0:64, :], func=AF.Ln), [r_ps], [r_rd[di]])
                        self.A(lambda: nc.scalar.activation(out=rd[di][:], in_=rd[di][:], func=AF.Exp, scale=-1.0), [r_rd[di]], [r_rd[di]])
                        self.V(lambda: nc.vector.tensor_tensor(out=yo[ysl][:, t * TT:(t + 1) * TT], in0=a[0:64, t * TT:(t + 1) * TT],
                                                               in1=rd[di][:], op=ALU.mult), [r_a, r_rd[di]], [r_yo[ysl]])
                    kb.dma("pool", self.d_YA.ap()[hl * 64:(hl + 1) * 64, :], yo[ysl][:], [r_yo[ysl]], [kb.R("YA", hl // 2)],
                           "yast")

    def ln_stats(self, es_tiles, src_chunks, r_src, nfeat, bf_chunks=None, r_bf=None, sq_chunks=None, r_sqc=None):
        nc, kb = self.nc, self.kb
        sq, r_sq, st, r_st = es_tiles
        ps1, r_ps1 = self.psum()
        kb.mm(ps1[:], [(self.onesB[:], c) for c in bf_chunks], [self.r_ib] + r_bf, [r_ps1])
        ps2, r_ps2 = self.psum()
        n = len(src_chunks)
        if sq_chunks is not None:
            kb.mm(ps2[:], [(self.onesB[:], c) for c in sq_chunks], [self.r_ib] + r_sqc, [r_ps2])
            src_chunks = []
        else:
            kb._wait("pe", kb._deps([self.r_ib], [r_ps2]))
        for i, c in enumerate(src_chunks):
            k = i % len(sq)
            self.A(lambda: nc.scalar.activation(out=sq[k][:], in_=c, func=AF.Square), r_src, [r_sq[k]])
            kb._wait("pe", kb._deps([r_sq[k]], []))
            ins = nc.tensor.matmul(ps2[:], lhsT=self.onesB[:], rhs=sq[k][:], start=(i == 0), stop=(i == n - 1))
            kb.cnt["pe"] += 1
            ins.then_inc(kb.sem["pe"], 1)
            kb._commit(("pe", kb.cnt["pe"]), [r_sq[k]], [r_ps2] if i == n - 1 else [])
        mean, msq, rstd, mr = st
        inv = 1.0 / nfeat
        self.V(lambda: nc.vector.tensor_scalar(out=mean[:], in0=ps1[:], scalar1=inv, scalar2=None, op0=ALU.mult), [r_ps1], [r_st[0]])
        self.V(lambda: nc.vector.tensor_tensor(out=msq[:], in0=mean[:], in1=mean[:], op=ALU.mult), [r_st[0]], [r_st[1]])
        self.V(lambda: nc.vector.scalar_tensor_tensor(out=msq[:], in0=ps2[:], scalar=inv, in1=msq[:], op0=ALU.mult, op1=ALU.subtract),
               [r_ps2, r_st[1]], [r_st[1]])
        self.V(lambda: nc.vector.tensor_scalar(out=msq[:], in0=msq[:], scalar1=1e-5, scalar2=None, op0=ALU.add), [r_st[1]], [r_st[1]])
        self.A(lambda: nc.scalar.activation(out=msq[:], in_=msq[:], func=AF.Sqrt), [r_st[1]], [r_st[1]])
        self.V(lambda: nc.vector.reciprocal(out=rstd[:], in_=msq[:]), [r_st[1]], [r_st[2]])
        self.V(lambda: nc.vector.tensor_tensor(out=mr[:], in0=mean[:], in1=rstd[:], op=ALU.mult), [r_st[0], r_st[2]], [r_st[3]])
        return rstd, r_st[2], mr, r_st[3]

    def ln_tiles(self, es, pfx):
        sq = [self.sb(es, pfx + "sq%d" % i, [128, TT], BF16) for i in range(4)]
        r_sq = [Res(pfx + "sq%d" % i) for i in range(4)]
        st = [self.sb(es, pfx + "st%d" % i, [128, TT], F32) for i in range(4)]
        r_st = [Res(pfx + "st%d" % i) for i in range(4)]
        return sq, r_sq, st, r_st

    def phase_C(self, ag):
        nc, kb, l = self.nc, self.kb, self.l
        with ExitStack() as es:
            wsl = self.sb(es, "wsl", [128, 6, 128], F32)
            wsm = self.sb(es, "wsm", [128, 6, 128], BF16)
            wsT = self.sb(es, "wsT", [128, 6, 128], BF16)
            bsb = self.sb(es, "bsb", [128, 768], F32)
            r_wsl, r_wsm, r_wsT, r_bsb = Res("wsl"), Res("wsm"), Res("wsT"), Res("bsb")
            kb.dma("sp", wsl[:], self.d_w_s.ap()[l].rearrange("g t s -> t g s"), [], [r_wsl], "cws")
            kb.dma("sp", bsb[:], self.d_b_s.ap()[l].partition_broadcast(128), [], [r_bsb], "cbs")
            self.V(lambda: nc.vector.tensor_tensor(out=wsm[:], in0=wsl[:], in1=bc_mid(self.tril, 6), op=ALU.mult),
                   [r_wsl, self.r_cf], [r_wsm])
            ps, r_ps = self.psum()
            psb = ps[:].bitcast(BF16)
            for g in range(6):
                kb.tr(psb[:, g * 128:(g + 1) * 128], wsm[:, g, :], self.identB[:], [r_wsm, self.r_ib], [r_ps])
            self.V(lambda: nc.vector.tensor_copy(wsT[:], psb[:, 0:768].rearrange("p (g t) -> p g t", t=128)), [r_ps], [r_wsT])

            uv = [self.sb(es, "uv%d" % i, [128, 12, TT], BF16) for i in range(2)]
            r_uv = [Res("uv%d" % i) for i in range(2)]
            vf = self.sb(es, "vf", [128, 6, TT], F32)
            r_vf = Res("vf")
            vn = self.sb(es, "vn", [128, 6, TT], BF16)
            r_vn = Res("vn")
            vnT = self.sb(es, "vnT", [128, 6, 4, 128], BF16)
            r_vnT = Res("vnT")
            tmp = [self.sb(es, "ctmp%d" % i, [128, TT], F32) for i in range(2)]
            r_tmp = [Res("ctmp%d" % i) for i in range(2)]
            yb = [self.sb(es, "ybo%d" % i, [128, 6, TT], BF16) for i in range(2)]
            r_yb = [Res("ybo%d" % i) for i in range(2)]
            lnt = self.ln_tiles(es, "c")

            def load(tt):
                sl = tt % 2
                src = self.d_P.ap()[U_OFF:U_OFF + 1536, tt * TT:(tt + 1) * TT].rearrange("(c p) s -> p c s", p=128)
                kb.dma("sp", uv[sl][:], src, [kb.R("P", U_OFF // 128 + c) for c in range(12)], [r_uv[sl]], "uvld%d" % sl)

            load(0)
            for tt in range(NTT):
                sl = tt % 2
                if tt + 1 < NTT:
                    load(tt + 1)
                self.pumpA(ag, 3)
                self.V(lambda: nc.vector.tensor_copy(vf[:], uv[sl][:, 6:12, :]), [r_uv[sl]], [r_vf])
                rstd, r_rstd, mr, r_mr = self.ln_stats(lnt, [vf[:, c, :] for c in range(6)], [r_vf], 768.0,
                                                       [uv[sl][:, 6 + c, :] for c in range(6)], [r_uv[sl]])
                for c in range(6):
                    k = c % 2
                    self.V(lambda: nc.vector.tensor_tensor(out=tmp[k][:], in0=vf[:, c, :], in1=rstd[:], op=ALU.mult),
                           [r_vf, r_rstd], [r_tmp[k]])
                    self.V(lambda: nc.vector.tensor_tensor(out=tmp[k][:], in0=tmp[k][:], in1=mr[:], op=ALU.subtract),
                           [r_tmp[k], r_mr], [r_tmp[k]])
                    self.A(lambda: nc.scalar.activation(out=vn[:, c, :], in_=tmp[k][:], func=AF.Identity,
                                                        scale=self.sp[:, O_SG + c:O_SG + c + 1], bias=self.sp[:, O_SB + c:O_SB + c + 1]),
                           [r_tmp[k], self.r_sp], [r_vn])
                for c in range(6):
                    ps, r_ps = self.psum()
                    psb = ps[:].bitcast(BF16)
                    for j in range(4):
                        kb.tr(psb[:, j * 128:(j + 1) * 128], vn[:, c, j * 128:(j + 1) * 128], self.identB[:], [r_vn, self.r_ib], [r_ps])
                    self.V(lambda: nc.vector.tensor_copy(vnT[:, c, :, :], psb[:, 0:512].rearrange("p (j d) -> p j d", d=128)),
                           [r_ps], [r_vnT])
                for c in range(6):
                    ps, r_ps = self.psum()
                    for j in range(4):
                        kb.mm(ps[:, j * 128:(j + 1) * 128], [(vnT[:, c, j, :], wsT[:, c, :])], [r_vnT, r_wsT], [r_ps])
                    k = c % 2
                    self.V(lambda: nc.vector.tensor_tensor(out=tmp[k][:].rearrange("p (j t) -> p j t", t=128),
                                                           in0=ps[:].rearrange("p (j t) -> p j t", t=128),
                                                           in1=bc_mid(bsb[:, c * 128:(c + 1) * 128], 4), op=ALU.add),
                           [r_ps, r_bsb], [r_tmp[k]])
                    self.V(lambda: nc.vector.tensor_tensor(out=yb[sl][:, c, :], in0=tmp[k][:], in1=uv[sl][:, c, :], op=ALU.mult),
                           [r_tmp[k], r_uv[sl]], [r_yb[sl]])
                dst = self.d_YB.ap()[:, tt * TT:(tt + 1) * TT].rearrange("(c p) s -> p c s", p=128)
                kb.dma("pool", dst, yb[sl][:], [r_yb[sl]], [kb.R("YB", tt)], "ybst%d" % sl)

    def range_reduce(self, x, r_x, tmpf, tmpi, r_t, shape_ap=None):
        nc = self.nc
        C1 = 6.28125
        C2 = TWO_PI - C1
        self.V(lambda: nc.vector.tensor_scalar(out=tmpf, in0=x, scalar1=1.0 / TWO_PI, scalar2=None, op0=ALU.mult), [r_x], [r_t])
        self.V(lambda: nc.vector.tensor_copy(tmpi, tmpf), [r_t], [r_t])
        self.V(lambda: nc.vector.tensor_copy(tmpf, tmpi), [r_t], [r_t])
        self.V(lambda: nc.vector.scalar_tensor_tensor(out=x, in0=tmpf, scalar=-C1, in1=x, op0=ALU.mult, op1=ALU.add), [r_t, r_x], [r_x])
        self.V(lambda: nc.vector.scalar_tensor_tensor(out=x, in0=tmpf, scalar=-C2, in1=x, op0=ALU.mult, op1=ALU.add), [r_t, r_x], [r_x])
        self.V(lambda: nc.vector.tensor_scalar(out=tmpf, in0=x, scalar1=math.pi, scalar2=-TWO_PI, op0=ALU.is_gt, op1=ALU.mult), [r_x], [r_t])
        self.V(lambda: nc.vector.tensor_tensor(out=x, in0=x, in1=tmpf, op=ALU.add), [r_t, r_x], [r_x])
        self.V(lambda: nc.vector.tensor_scalar(out=tmpf, in0=x, scalar1=-math.pi, scalar2=TWO_PI, op0=ALU.is_lt, op1=ALU.mult), [r_x], [r_t])
        self.V(lambda: nc.vector.tensor_tensor(out=x, in0=x, in1=tmpf, op=ALU.add), [r_t, r_x], [r_x])
        self.V(lambda: nc.vector.tensor_scalar(out=x, in0=x, scalar1=math.pi, scalar2=-math.pi, op0=ALU.min, op1=ALU.max), [r_x], [r_x])

    def phase_D(self, ag):
        nc, kb, l = self.nc, self.kb, self.l
        sp = self.sp
        with ExitStack() as es:
            NS = 24
            pp = self.sb(es, "s5p", [128, 20, NS], F32)
            ppi = self.sb(es, "s5pi", [128, 2, NS], I32)
            r_pp = Res("s5p")
            lr, li, ld = sp[:, O_LR:O_LR + NS], sp[:, O_LI:O_LI + NS], sp[:, O_LD:O_LD + NS]
            (DT, MAG, TH, SN, CS, T0, T1, SH, EM1, ABI, AR1, INV, CR, CI, TH2, C5, S5, T2, T3, T4) = [pp[:, i, :] for i in range(20)]
            RS = [self.r_sp, r_pp]

            def v(fn):
                self.V(fn, RS, [r_pp])

            def a(fn):
                self.A(fn, RS, [r_pp])

            a(lambda: nc.scalar.activation(out=DT, in_=ld, func=AF.Exp))
            v(lambda: nc.vector.tensor_tensor(out=T0, in0=lr, in1=DT, op=ALU.mult))
            a(lambda: nc.scalar.activation(out=MAG, in_=T0, func=AF.Exp))
            v(lambda: nc.vector.tensor_scalar(out=EM1, in0=T0, scalar1=1.0 / 6.0, scalar2=1.0, op0=ALU.mult, op1=ALU.add))
            for dv in (5.0, 4.0, 3.0, 2.0):
                v(lambda: nc.vector.tensor_tensor(out=EM1, in0=EM1, in1=T0, op=ALU.mult))
                v(lambda: nc.vector.tensor_scalar(out=EM1, in0=EM1, scalar1=1.0 / dv, scalar2=1.0, op0=ALU.mult, op1=ALU.add))
            v(lambda: nc.vector.tensor_tensor(out=EM1, in0=EM1, in1=T0, op=ALU.mult))
            v(lambda: nc.vector.tensor_tensor(out=TH, in0=li, in1=DT, op=ALU.mult))
            v(lambda: nc.vector.tensor_copy(T1, TH))
            self.range_reduce(T1, r_pp, T2, ppi[:, 0, :], r_pp)
            a(lambda: nc.scalar.activation(out=SN, in_=T1, func=AF.Sin))
            v(lambda: nc.vector.tensor_scalar(out=T1, in0=TH, scalar1=math.pi / 2, scalar2=None, op0=ALU.add))
            self.range_reduce(T1, r_pp, T2, ppi[:, 0, :], r_pp)
            a(lambda: nc.scalar.activation(out=CS, in_=T1, func=AF.Sin))
            v(lambda: nc.vector.tensor_scalar(out=T1, in0=TH, scalar1=0.5, scalar2=None, op0=ALU.mult))
            self.range_reduce(T1, r_pp, T2, ppi[:, 0, :], r_pp)
            a(lambda: nc.scalar.activation(out=SH, in_=T1, func=AF.Sin))
            v(lambda: nc.vector.tensor_tensor(out=ABI, in0=MAG, in1=SN, op=ALU.mult))
            v(lambda: nc.vector.tensor_tensor(out=AR1, in0=EM1, in1=CS, op=ALU.mult))
            v(lambda: nc.vector.tensor_tensor(out=T1, in0=SH, in1=SH, op=ALU.mult))
            v(lambda: nc.vector.scalar_tensor_tensor(out=AR1, in0=T1, scalar=-2.0, in1=AR1, op0=ALU.mult, op1=ALU.add))
            v(lambda: nc.vector.tensor_tensor(out=T1, in0=lr, in1=lr, op=ALU.mult))
            v(lambda: nc.vector.tensor_tensor(out=T2, in0=li, in1=li, op=ALU.mult))
            v(lambda: nc.vector.tensor_tensor(out=T1, in0=T1, in1=T2, op=ALU.add))
            v(lambda: nc.vector.reciprocal(out=INV, in_=T1))
            v(lambda: nc.vector.tensor_tensor(out=T1, in0=AR1, in1=lr, op=ALU.mult))
            v(lambda: nc.vector.tensor_tensor(out=T2, in0=ABI, in1=li, op=ALU.mult))
            v(lambda: nc.vector.tensor_tensor(out=T1, in0=T1, in1=T2, op=ALU.add))
            v(lambda: nc.vector.tensor_tensor(out=CR, in0=T1, in1=INV, op=ALU.mult))
            v(lambda: nc.vector.tensor_tensor(out=T1, in0=ABI, in1=lr, op=ALU.mult))
            v(lambda: nc.vector.tensor_tensor(out=T2, in0=AR1, in1=li, op=ALU.mult))
            v(lambda: nc.vector.tensor_tensor(out=T1, in0=T1, in1=T2, op=ALU.subtract))
            v(lambda: nc.vector.tensor_tensor(out=CI, in0=T1, in1=INV, op=ALU.mult))
            v(lambda: nc.vector.tensor_scalar(out=TH2, in0=TH, scalar1=float(TT), scalar2=None, op0=ALU.mult))
            self.range_reduce(TH2, r_pp, T2, ppi[:, 0, :], r_pp)
            a(lambda: nc.scalar.activation(out=S5, in_=TH2, func=AF.Sin))
            v(lambda: nc.vector.tensor_scalar(out=T1, in0=TH2, scalar1=math.pi / 2, scalar2=None, op0=ALU.add))
            self.range_reduce(T1, r_pp, T2, ppi[:, 0, :], r_pp)
            a(lambda: nc.scalar.activation(out=C5, in_=T1, func=AF.Sin))

            bbr = self.sb(es, "bbr", [128, NS, 16], F32)
            bbi = self.sb(es, "bbi", [128, NS, 16], F32)
            bt = self.sb(es, "bbt", [128, NS, 16], F32)
            r_bb = Res("bb")
            bre = sp[:, O_BRE:O_BRE + 384].rearrange("p (s h) -> p s h", h=16)
            bim = sp[:, O_BIM:O_BIM + 384].rearrange("p (s h) -> p s h", h=16)
            cre = sp[:, O_CRE:O_CRE + 384].rearrange("p (s h) -> p s h", h=16)
            cim = sp[:, O_CIM:O_CIM + 384].rearrange("p (s h) -> p s h", h=16)
            crb, cib = bc_last(CR, 16), bc_last(CI, 16)
            RB = [self.r_sp, r_pp, r_bb]
            self.V(lambda: nc.vector.tensor_tensor(out=bbr[:], in0=bre, in1=crb, op=ALU.mult), RB, [r_bb])
            self.V(lambda: nc.vector.tensor_tensor(out=bt[:], in0=bim, in1=cib, op=ALU.mult), RB, [r_bb])
            self.V(lambda: nc.vector.tensor_tensor(out=bbr[:], in0=bbr[:], in1=bt[:], op=ALU.subtract), RB, [r_bb])
            self.V(lambda: nc.vector.tensor_tensor(out=bbi[:], in0=bim, in1=crb, op=ALU.mult), RB, [r_bb])
            self.V(lambda: nc.vector.tensor_tensor(out=bt[:], in0=bre, in1=cib, op=ALU.mult), RB, [r_bb])
            self.V(lambda: nc.vector.tensor_tensor(out=bbi[:], in0=bbi[:], in1=bt[:], op=ALU.add), RB, [r_bb])

            bwr = self.sb(es, "bwr", [128, NS, 128], BF16)
            bwi = self.sb(es, "bwi", [128, NS, 128], BF16)
            cwr = self.sb(es, "cwr", [128, NS, 128], BF16)
            cwi = self.sb(es, "cwi", [128, NS, 128], BF16)
            r_bw, r_cw = Res("bw"), Res("cw")
            stg = [self.sb(es, "stg%d" % i, [128, 128], F32) for i in range(2)]
            r_stg = [Res("stg%d" % i) for i in range(2)]
            self.V(lambda: nc.vector.memset(cwr[:], 0.0), [], [r_cw])
            self.V(lambda: nc.vector.memset(cwi[:], 0.0), [], [r_cw])
            for sc in range(NS):
                c0 = (sc % 4) * 32
                for hf in range(2):
                    ps_ = slice(hf * 64, hf * 64 + 64)
                    cs_ = slice(c0 + hf * 16, c0 + hf * 16 + 16)
                    self.V(lambda: nc.vector.tensor_copy(cwr[ps_, sc, cs_], cre[ps_, sc, :]), [self.r_sp, r_cw], [r_cw])
                    self.V(lambda: nc.vector.tensor_scalar(out=cwi[ps_, sc, cs_], in0=cim[ps_, sc, :], scalar1=-1.0, scalar2=None,
                                                           op0=ALU.mult), [self.r_sp, r_cw], [r_cw])
            ti = 0
            for sc in range(NS):
                c0 = (sc % 4) * 32
                for src, dstw in ((bbr, bwr), (bbi, bwi)):
                    k = ti % 2
                    ti += 1
                    self.V(lambda: nc.vector.memset(stg[k][:], 0.0), [], [r_stg[k]])
                    for hf in range(2):
                        ps_ = slice(hf * 64, hf * 64 + 64)
                        cs_ = slice(c0 + hf * 16, c0 + hf * 16 + 16)
                        self.V(lambda: nc.vector.tensor_copy(stg[k][ps_, cs_], src[ps_, sc, :]), [r_bb, r_stg[k]], [r_stg[k]])
                    ps, r_ps = self.psum()
                    kb.tr(ps[:, 0:128], stg[k][:], self.identF, [r_stg[k], self.r_cf], [r_ps])
                    self.V(lambda: nc.vector.tensor_copy(dstw[:, sc, :], ps[:, 0:128]), [r_ps], [r_bw])

            cosT = self.sb(es, "cosT", [128, 4, TT], F32)
            sinT = self.sb(es, "sinT", [128, 4, TT], F32)
            rho = self.sb(es, "rho", [128, 4, TT], F32)
            r_tab = Res("tab")
            phs = self.sb(es, "phs", [128, TT], F32)
            phf = self.sb(es, "phf", [128, TT], F32)
            phi = self.sb(es, "phi", [128, TT], I32)
            r_ph, r_pht = Res("phs"), Res("pht")
            ut = [self.sb(es, "ut%d" % i, [128, TT], BF16) for i in range(4)]
            r_ut = [Res("ut%d" % i) for i in range(4)]
            ut2, r_ut2 = ut, r_ut
            pend_y = []
            NT = 4
            tmp = [self.sb(es, "dtmp%d" % i, [128, TT], F32) for i in range(NT)]
            r_tmp = [Res("dtmp%d" % i) for i in range(NT)]
            dre = [self.sb(es, "dre%d" % i, [128, TT], F32) for i in range(2)]
            dim = [self.sb(es, "dim%d" % i, [128, TT], F32) for i in range(2)]
            wre = [self.sb(es, "wre%d" % i, [128, TT], F32) for i in range(3)]
            wim = [self.sb(es, "wim%d" % i, [128, TT], F32) for i in range(3)]
            r_d = [Res("dd%d" % i) for i in range(2)]
            r_w = [Res("ww%d" % i) for i in range(3)]
            xre = [self.sb(es, "xre%d" % i, [128, 4, TT], BF16) for i in range(2)]
            xim = [self.sb(es, "xim%d" % i, [128, 4, TT], BF16) for i in range(2)]
            r_x = [Res("xx%d" % i) for i in range(2)]
            car = self.sb(es, "car", [128, 4, 4], F32)
            r_car = Res("car")
            so = [self.sb(es, "so%d" % i, [128, TT], F32) for i in range(2)]
            r_so = [Res("so%d" % i) for i in range(2)]
            yo = [self.sb(es, "dyo%d" % i, [128, TT], BF16) for i in range(2)]
            r_yo = [Res("dyo%d" % i) for i in range(2)]
            tix = 0
            wi = 0
            ptix = 0
            ptmp = [self.sb(es, "ptmp%d" % i, [128, TT], F32) for i in range(4)]
            r_ptmp = [Res("ptmp%d" % i) for i in range(4)]

            def load(i):
                uc_, tt_ = divmod(i, NTT)
                sl = i % 4
                kb.dma("sp", ut[sl][:], self.d_P.ap()[UC_OFF + uc_ * 128:UC_OFF + (uc_ + 1) * 128, tt_ * TT:(tt_ + 1) * TT],
                       [kb.R("P", UC_OFF // 128 + uc_)], [r_ut[sl]], "utld%d" % (sl % 2))

            load(0)
            for uc in range(6):
                for s4 in range(4):
                    sc = uc * 4 + s4
                    th_ap = TH[:, sc:sc + 1]
                    self.V(lambda: nc.vector.tensor_scalar(out=phs[:], in0=self.iota, scalar1=th_ap, scalar2=None, op0=ALU.mult),
                           [self.r_cf, r_pp], [r_ph])
                    self.V(lambda: nc.vector.tensor_copy(phf[:], phs[:]), [r_ph], [r_pht])
                    self.range_reduce(phf[:], r_pht, phs[:], phi[:], r_ph)
                    self.A(lambda: nc.scalar.activation(out=sinT[:, s4, :], in_=phf[:], func=AF.Sin), [r_pht, r_tab], [r_tab])
                    self.V(lambda: nc.vector.tensor_scalar(out=phs[:], in0=self.iota, scalar1=th_ap, scalar2=None, op0=ALU.mult),
                           [self.r_cf, r_pp, r_ph], [r_ph])
                    self.V(lambda: nc.vector.tensor_scalar(out=phf[:], in0=phs[:], scalar1=math.pi / 2, scalar2=None, op0=ALU.add),
                           [r_ph, r_pht], [r_pht])
                    self.range_reduce(phf[:], r_pht, phs[:], phi[:], r_ph)
                    self.A(lambda: nc.scalar.activation(out=cosT[:, s4, :], in_=phf[:], func=AF.Sin), [r_pht, r_tab], [r_tab])
                    self.A(lambda: nc.scalar.activation(out=rho[:, s4, :], in_=self.iota, func=AF.Identity, scale=0.0,
                                                        bias=MAG[:, sc:sc + 1]), [self.r_cf, r_pp, r_tab], [r_tab])
                self.V(lambda: nc.vector.memset(car[:], 0.0), [r_car], [r_car])
                for tt in range(NTT):
                    xs = (uc * NTT + tt) % 2
                    usl = (uc * NTT + tt) % 4
                    if uc * NTT + tt + 1 < 6 * NTT:
                        load(uc * NTT + tt + 1)
                    tsl = slice(tt * TT, (tt + 1) * TT)
                    for s4 in range(4):
                        sc = uc * 4 + s4
                        self.pumpA(ag, 1 + (s4 % 2))
                        if l == 0:
                            self.pump_casts(1)
                        pr, r_pr = self.psum()
                        kb.mm(pr[:], [(bwr[:, sc, :], ut[usl][:])], [r_bw, r_ut[usl]], [r_pr])
                        pi_, r_pi = self.psum()
                        kb.mm(pi_[:], [(bwi[:, sc, :], ut[usl][:])], [r_bw, r_ut[usl]], [r_pi])
                        if s4 == 1 and pend_y:
                            pend_y.pop(0)()
                        c_, s_ = cosT[:, s4, :], sinT[:, s4, :]
                        k = wi % 2
                        kw = wi % 3
                        wi += 1
                        t = [tmp[(tix + i) % NT] for i in range(2)]
                        rt = [r_tmp[(tix + i) % NT] for i in range(2)]
                        tix += 2
                        self.V(lambda: nc.vector.tensor_tensor(out=t[0][:], in0=pr[:], in1=c_, op=ALU.mult), [r_pr, r_tab], [rt[0]])
                        self.V(lambda: nc.vector.tensor_tensor(out=t[1][:], in0=pi_[:], in1=s_, op=ALU.mult), [r_pi, r_tab], [rt[1]])
                        self.V(lambda: nc.vector.tensor_tensor(out=dre[k][:], in0=t[0][:], in1=t[1][:], op=ALU.add), [rt[0], rt[1]], [r_d[k]])
                        self.V(lambda: nc.vector.tensor_tensor(out=t[0][:], in0=pi_[:], in1=c_, op=ALU.mult), [r_pi, r_tab, rt[0]], [rt[0]])
                        self.V(lambda: nc.vector.tensor_tensor(out=t[1][:], in0=pr[:], in1=s_, op=ALU.mult), [r_pr, r_tab, rt[1]], [rt[1]])
                        self.V(lambda: nc.vector.tensor_tensor(out=dim[k][:], in0=t[0][:], in1=t[1][:], op=ALU.subtract), [rt[0], rt[1]], [r_d[k]])
                        self.V(lambda: nc.vector.tensor_tensor_scan(out=wre[kw][:], data0=rho[:, s4, :], data1=dre[k][:],
                                                                    initial=car[:, s4, 0:1], op0=ALU.mult, op1=ALU.add),
                               [r_tab, r_d[k], r_car], [r_w[kw]])
                        self.V(lambda: nc.vector.tensor_tensor_scan(out=wim[kw][:], data0=rho[:, s4, :], data1=dim[k][:],
                                                                    initial=car[:, s4, 1:2], op0=ALU.mult, op1=ALU.add),
                               [r_tab, r_d[k], r_car], [r_w[kw]])
                        if tt + 1 < NTT:
                            c5, s5 = C5[:, sc:sc + 1], S5[:, sc:sc + 1]
                            wl_r, wl_i = wre[kw][:, TT - 1:TT], wim[kw][:, TT - 1:TT]
                            self.V(lambda: nc.vector.tensor_scalar(out=car[:, s4, 2:3], in0=wl_i, scalar1=s5, scalar2=None, op0=ALU.mult),
                                   [r_w[kw], r_pp, r_car], [r_car])
                            self.V(lambda: nc.vector.tensor_scalar(out=car[:, s4, 3:4], in0=wl_r, scalar1=s5, scalar2=None, op0=ALU.mult),
                                   [r_w[kw], r_pp, r_car], [r_car])
                            self.V(lambda: nc.vector.scalar_tensor_tensor(out=car[:, s4, 0:1], in0=wl_r, scalar=c5, in1=car[:, s4, 2:3],
                                                                          op0=ALU.mult, op1=ALU.subtract), [r_w[kw], r_pp, r_car], [r_car])
                            self.V(lambda: nc.vector.scalar_tensor_tensor(out=car[:, s4, 1:2], in0=wl_i, scalar=c5, in1=car[:, s4, 3:4],
                                                                          op0=ALU.mult, op1=ALU.add), [r_w[kw], r_pp, r_car], [r_car])
                        t = [ptmp[(ptix + i) % 4] for i in range(2)]
                        rt = [r_ptmp[(ptix + i) % 4] for i in range(2)]
                        ptix += 2
                        P_ = lambda fn, rd, wr: kb.op("pool", fn, rd, wr)
                        P_(lambda: nc.gpsimd.tensor_tensor(out=t[0][:], in0=wre[kw][:], in1=c_, op=ALU.mult), [r_w[kw], r_tab], [rt[0]])
                        P_(lambda: nc.gpsimd.tensor_tensor(out=t[1][:], in0=wim[kw][:], in1=s_, op=ALU.mult), [r_w[kw], r_tab], [rt[1]])
                        P_(lambda: nc.gpsimd.tensor_tensor(out=xre[xs][:, s4, :], in0=t[0][:], in1=t[1][:], op=ALU.subtract),
                           [rt[0], rt[1]], [r_x[xs]])
                        P_(lambda: nc.gpsimd.tensor_tensor(out=t[0][:], in0=wre[kw][:], in1=s_, op=ALU.mult), [r_w[kw], r_tab, rt[0]], [rt[0]])
                        P_(lambda: nc.gpsimd.tensor_tensor(out=t[1][:], in0=wim[kw][:], in1=c_, op=ALU.mult), [r_w[kw], r_tab, rt[1]], [rt[1]])
                        P_(lambda: nc.gpsimd.tensor_tensor(out=xim[xs][:, s4, :], in0=t[0][:], in1=t[1][:], op=ALU.add),
                           [rt[0], rt[1]], [r_x[xs]])
                    def emit_y(uc=uc, tt=tt, xs=xs, usl=usl, tsl=tsl):
                        py, r_py = self.psum()
                        pairs = []
                        for s4 in range(4):
                            sc = uc * 4 + s4
                            pairs.append((cwr[:, sc, :], xre[xs][:, s4, :]))
                            pairs.append((cwi[:, sc, :], xim[xs][:, s4, :]))
                        kb.mm(py[:], pairs, [r_cw, r_x[xs]], [r_py])
                        os_ = tt % 2
                        self.V(lambda: nc.vector.scalar_tensor_tensor(out=so[os_][:], in0=ut2[usl][:], scalar=sp[:, O_DSK + uc:O_DSK + uc + 1],
                                                                      in1=py[:], op0=ALU.mult, op1=ALU.add),
                               [r_ut2[usl], self.r_sp, r_py], [r_so[os_]])
                        self.A(lambda: nc.scalar.activation(out=yo[os_][:], in_=so[os_][:], func=AF.Gelu_apprx_tanh), [r_so[os_]], [r_yo[os_]])
                        kb.dma("pool", self.d_YC0.ap()[uc * 128:(uc + 1) * 128, tsl], yo[os_][:], [r_yo[os_]], [kb.R("YC0", tt)],
                               "ycst%d" % os_)
                    pend_y.append(emit_y)
            while pend_y:
                pend_y.pop(0)()

    def phase_EF(self):
        nc, kb, l = self.nc, self.kb, self.l
        sp = self.sp
        with ExitStack() as es:
            arena = self.sb(es, "arena", [128, 44 * TT], BF16)
            hT = arena[:, :].rearrange("p (k t) -> p k t", t=TT)
            yaT = hT[:, 0:4, :]
            ybT = hT[:, 4:10, :]
            y0T = hT[:, 10:16, :]
            ycT = hT[:, 16:22, :]
            mT = hT[:, 22:38, :]
            r_in = Res("ef_in")
            r_yc, r_mT, r_hT = Res("ycT"), Res("mT"), Res("hT")
            rr = self.sb(es, "rr", [128, 16, TT], F32)
            r_rrc = [Res("rr%d" % i) for i in range(16)]
            x1b = self.sb(es, "x1b", [128, 16, TT], BF16)
            r_x1b = Res("x1b")
            sqb = self.sb(es, "sqb", [128, 16, TT], BF16)
            r_sqb = Res("sqb")
            gt = [self.sb(es, "gt%d" % i, [128, 3, TT], BF16) for i in range(2)]
            r_gt = [Res("gt%d" % i) for i in range(2)]
            xr = [self.sb(es, "xr%d" % i, [128, TT], F32) for i in range(2)]
            r_xr = [Res("xr%d" % i) for i in range(2)]
            tmp = [self.sb(es, "etmp%d" % i, [128, TT], F32) for i in range(6)]
            r_tmp = [Res("etmp%d" % i) for i in range(6)]
            lnt = self.ln_tiles(es, "e")
            tix = 0
            X_in = self.x_in.ap()
            X_out = self.x_out.ap()

            def layer_norm(goff, boff, final_store, tt):
                nonlocal tix
                rstd, r_rstd, mr, r_mr = self.ln_stats(lnt, [rr[:, c, :] for c in range(16)], r_rrc, float(D),
                                                       [x1b[:, c, :] for c in range(16)], [r_x1b],
                                                       [sqb[:, c, :] for c in range(16)], [r_sqb])
                for c in range(16):
                    k = tix % 6
                    tix += 1
                    self.V(lambda: nc.vector.tensor_tensor(out=tmp[k][:], in0=rr[:, c, :], in1=rstd[:], op=ALU.mult),
                           [r_rrc[c], r_rstd], [r_tmp[k]])
                    self.V(lambda: nc.vector.tensor_tensor(out=tmp[k][:], in0=tmp[k][:], in1=mr[:], op=ALU.subtract),
                           [r_tmp[k], r_mr], [r_tmp[k]])
                    self.A(lambda: nc.scalar.activation(out=rr[:, c, :], in_=tmp[k][:], func=AF.Identity,
                                                        scale=sp[:, goff + c:goff + c + 1], bias=sp[:, boff + c:boff + c + 1]),
                           [r_tmp[k], self.r_sp], [r_rrc[c]])
                    if not final_store:
                        self.A(lambda: nc.scalar.activation(out=x1b[:, c, :], in_=tmp[k][:], func=AF.Identity,
                                                            scale=sp[:, goff + c:goff + c + 1], bias=sp[:, boff + c:boff + c + 1]),
                               [r_tmp[k], self.r_sp], [r_x1b])
                if final_store:
                    dst = X_out[:, tt * TT:(tt + 1) * TT].rearrange("(c p) s -> p c s", p=128)
                    kb.dma("pool", dst, rr[:], r_rrc, [kb.R("X", l + 1, tt)], "xst")

            for tt in range(NTT):
                tsl = slice(tt * TT, (tt + 1) * TT)
                kb.dma("sp", yaT, self.d_YA.ap()[:, tsl].rearrange("(c p) s -> p c s", p=128), [kb.R("YA", i) for i in range(4)],
                       [r_in, r_hT], "efld")
                kb.dma("sp", ybT, self.d_YB.ap()[:, tsl].rearrange("(c p) s -> p c s", p=128), [kb.R("YB", tt)], [r_in, r_hT], "efld")
                kb.dma("sp", y0T, self.d_YC0.ap()[:, tsl].rearrange("(c p) s -> p c s", p=128), [kb.R("YC0", tt)], [r_in, r_hT], "efld")
                wg = []
                for h in range(2):
                    wt, r_w = self.ws.get(("G", l, tt, h))
                    wg.append((wt[:, 0:2304].rearrange("p (k c) -> p k c", c=768), r_w))
                for oc in range(6):
                    ps, r_ps = self.psum()
                    pairs = [(wg[k // 3][0][:, k % 3, oc * 128:(oc + 1) * 128], y0T[:, k, :]) for k in range(6)]
                    kb.mm(ps[:], pairs, [wg[0][1], wg[1][1], r_in], [r_ps])
                    k = tix % 6
                    tix += 1
                    self.A(lambda: nc.scalar.activation(out=tmp[k][:], in_=ps[:], func=AF.Sigmoid, bias=sp[:, O_BGLU + oc:O_BGLU + oc + 1],
                                                        scale=1.0), [r_ps, self.r_sp], [r_tmp[k]])
                    self.V(lambda: nc.vector.tensor_tensor(out=ycT[:, oc, :], in0=y0T[:, oc, :], in1=tmp[k][:], op=ALU.mult),
                           [r_in, r_tmp[k]], [r_yc])
                for dc in range(16):
                    gs = dc % 2
                    src = bass.AP(tensor=self.d_P, offset=(GL_OFF + dc * 128) * S + tt * TT, ap=[[S, 128], [D * S, 3], [1, TT]])
                    kb.dma("sp", gt[gs][:], src, [kb.R("P", GL_OFF // 128 + br * 16 + dc) for br in range(3)], [r_gt[gs]], "gtld%d" % gs)
                    wt, r_w = self.ws.get(("M", l, tt, dc))
                    w3 = wt[:, 0:2048].rearrange("p (k c) -> p k c", c=128)
                    pa, r_pa = self.psum()
                    kb.mm(pa[:], [(w3[:, k, :], yaT[:, k, :]) for k in range(4)], [r_w, r_in], [r_pa])
                    pb, r_pb = self.psum()
                    kb.mm(pb[:], [(w3[:, 4 + k, :], ybT[:, k, :]) for k in range(6)], [r_w, r_in], [r_pb])
                    pc, r_pc = self.psum()
                    kb.mm(pc[:], [(w3[:, 10 + k, :], ycT[:, k, :]) for k in range(6)], [r_w, r_yc], [r_pc])
                    k0, k1 = tix % 6, (tix + 1) % 6
                    tix += 2
                    self.V(lambda: nc.vector.tensor_tensor(out=tmp[k0][:], in0=pa[:], in1=gt[gs][:, 0, :], op=ALU.mult), [r_pa, r_gt[gs]], [r_tmp[k0]])
                    self.V(lambda: nc.vector.tensor_tensor(out=tmp[k1][:], in0=pb[:], in1=gt[gs][:, 1, :], op=ALU.mult), [r_pb, r_gt[gs]], [r_tmp[k1]])
                    k2 = tix % 6
                    tix += 1
                    self.V(lambda: nc.vector.tensor_tensor(out=tmp[k2][:], in0=pc[:], in1=gt[gs][:, 2, :], op=ALU.mult), [r_pc, r_gt[gs]], [r_tmp[k2]])
                    kb.op("pool", lambda: nc.gpsimd.tensor_tensor(out=tmp[k0][:], in0=tmp[k0][:], in1=tmp[k1][:], op=ALU.add), [r_tmp[k0], r_tmp[k1]], [r_tmp[k0]])
                    kb.op("pool", lambda: nc.gpsimd.tensor_tensor(out=mT[:, dc, :], in0=tmp[k0][:], in1=tmp[k2][:], op=ALU.add), [r_tmp[k0], r_tmp[k2]], [r_mT])
                for dc in range(16):
                    xs = dc % 2
                    kb.dma("sp", xr[xs][:], X_in[dc * 128:(dc + 1) * 128, tsl], [kb.R("X", l, tt)], [r_xr[xs]], "xrld%d" % xs)
                    wt, r_w = self.ws.get(("O", l, tt, dc))
                    w3 = wt[:, 0:2048].rearrange("p (k c) -> p k c", c=128)
                    ps, r_ps = self.psum()
                    kb.mm(ps[:], [(w3[:, k, :], mT[:, k, :]) for k in range(16)], [r_w, r_mT], [r_ps])
                    self.V(lambda: nc.vector.scalar_tensor_tensor(out=rr[:, dc, :], in0=xr[xs][:], scalar=float(ALPHA), in1=ps[:],
                                                                  op0=ALU.mult, op1=ALU.add), [r_xr[xs], r_ps], [r_rrc[dc]])
                    kb.op("pool", lambda: nc.gpsimd.tensor_copy(x1b[:, dc, :], rr[:, dc, :]), [r_rrc[dc]], [r_x1b])
                    self.A(lambda: nc.scalar.activation(out=sqb[:, dc, :], in_=rr[:, dc, :], func=AF.Square), [r_rrc[dc]], [r_sqb])
                layer_norm(O_L1G, O_L1B, False, tt)
                for j in range(44):
                    self.pump_casts(1)
                    wtg, r_wg = self.ws.get(("F1g", l, tt, j))
                    wtu, r_wu = self.ws.get(("F1u", l, tt, j))
                    g3 = wtg[:, 0:2048].rearrange("p (k c) -> p k c", c=128)
                    u3 = wtu[:, 0:2048].rearrange("p (k c) -> p k c", c=128)
                    pg, r_pg = self.psum()
                    kb.mm(pg[:], [(g3[:, k, :], x1b[:, k, :]) for k in range(16)], [r_wg, r_x1b], [r_pg])
                    pu, r_pu = self.psum()
                    kb.mm(pu[:], [(u3[:, k, :], x1b[:, k, :]) for k in range(16)], [r_wu, r_x1b], [r_pu])
                    k = tix % 6
                    tix += 1
                    self.A(lambda: nc.scalar.activation(out=tmp[k][:], in_=pg[:], func=AF.Silu), [r_pg], [r_tmp[k]])
                    self.V(lambda: nc.vector.tensor_tensor(out=hT[:, j, :], in0=pu[:], in1=tmp[k][:], op=ALU.mult),
                           [r_pu, r_tmp[k]], [r_hT, r_in, r_yc, r_mT])
                for dc in range(16):
                    ps, r_ps = self.psum()
                    wa, r_wa = self.ws.get(("F2", l, tt, dc, 0))
                    wb, r_wb = self.ws.get(("F2", l, tt, dc, 1))
                    a3 = wa[:, 0:2816].rearrange("p (k c) -> p k c", c=128)
                    b3 = wb[:, 0:2816].rearrange("p (k c) -> p k c", c=128)
                    pairs = [(a3[:, k, :], hT[:, k, :]) for k in range(22)] + [(b3[:, k, :], hT[:, 22 + k, :]) for k in range(22)]
                    kb.mm(ps[:], pairs, [r_wa, r_wb, r_hT], [r_ps])
                    self.V(lambda: nc.vector.scalar_tensor_tensor(out=rr[:, dc, :], in0=rr[:, dc, :], scalar=float(ALPHA), in1=ps[:],
                                                                  op0=ALU.mult, op1=ALU.add), [r_ps], [r_rrc[dc]])
                    kb.op("pool", lambda: nc.gpsimd.tensor_copy(x1b[:, dc, :], rr[:, dc, :]), [r_rrc[dc]], [r_x1b])
                    self.A(lambda: nc.scalar.activation(out=sqb[:, dc, :], in_=rr[:, dc, :], func=AF.Square), [r_rrc[dc]], [r_sqb])
                layer_norm(O_L2G, O_L2B, True, tt)


def bc_row(ap, n):
    a = ap.ap
    return bass.AP(tensor=ap.tensor, offset=ap.offset, ap=[list(a[0]), [0, n]])


def _t5_bucket(dist):
    max_exact = 16
    d = np.maximum(dist, 1).astype(np.float32)
    scale = (32 - max_exact) / math.log(2048 / max_exact)
    large = max_exact + (np.log(d / max_exact) * scale).astype(np.int32)
    large = np.minimum(large, 31)
    return np.where(dist < max_exact, dist, large).astype(np.int32)


def host_consts():
    cf = np.zeros((128, 1664), np.float32)
    cf[:, 0:128] = np.eye(128, dtype=np.float32)
    cf[:, 128:256] = np.tril(np.ones((128, 128), np.float32))
    cf[:, 256:768] = np.arange(512, dtype=np.float32)[None, :]
    cf[64, 768:896] = 1.0
    cf[:, 1024:1152] = 1.0
    oh = np.zeros((3, 33, EW), np.float32)
    for g, dil in enumerate(DILS):
        for s in range(EW):
            st = s - 127
            if 0 <= st <= 128:
                b = int(_t5_bucket(np.array([st * dil]))[0])
                oh[g, b, s] = 1.0
            else:
                oh[g, 32, s] = NEG
    return cf, oh


def pack_small(inp):
    sp = np.zeros((DEPTH, 128, NSP), np.float32)

    def fm(v, n):
        return np.ascontiguousarray(v.reshape(DEPTH, n, 128).transpose(0, 2, 1))

    def st(v):
        sh = v.shape
        v = v.reshape((DEPTH, 24, 2, 64) + sh[3:])
        v = np.moveaxis(v, 1, 3)
        return np.ascontiguousarray(v.reshape((DEPTH, 128, 24) + sh[3:]))

    sp[:, :, O_BA:O_BA + 102] = fm(inp["b_in"], 102)
    sp[:, :, O_SG:O_SG + 6] = fm(inp["sgu_ln_g"], 6)
    sp[:, :, O_SB:O_SB + 6] = fm(inp["sgu_ln_b"], 6)
    sp[:, :, O_LR:O_LR + 24] = st(inp["lam_re"])
    sp[:, :, O_LI:O_LI + 24] = st(inp["lam_im"])
    sp[:, :, O_LD:O_LD + 24] = st(np.repeat(inp["log_dt"][:, :, None], 64, axis=2))
    sp[:, :, O_BRE:O_BRE + 384] = st(inp["b_re"]).reshape(DEPTH, 128, 384)
    sp[:, :, O_BIM:O_BIM + 384] = st(inp["b_im"]).reshape(DEPTH, 128, 384)
    sp[:, :, O_CRE:O_CRE + 384] = st(np.swapaxes(inp["c_re"], 2, 3)).reshape(DEPTH, 128, 384)
    sp[:, :, O_CIM:O_CIM + 384] = st(np.swapaxes(inp["c_im"], 2, 3)).reshape(DEPTH, 128, 384)
    sp[:, :, O_DSK:O_DSK + 6] = fm(inp["d_skip"], 6)
    sp[:, :, O_BGLU:O_BGLU + 6] = fm(inp["b_glu"], 6)
    sp[:, :, O_L1G:O_L1G + 16] = fm(inp["ln1_g"], 16)
    sp[:, :, O_L1B:O_L1B + 16] = fm(inp["ln1_b"], 16)
    sp[:, :, O_L2G:O_L2G + 16] = fm(inp["ln2_g"], 16)
    sp[:, :, O_L2B:O_L2B + 16] = fm(inp["ln2_b"], 16)
    return sp


_PROG = {}


def make_in_maps(inp, cores):
    cf, oh = host_consts()
    sp = pack_small(inp)
    shared = {
        "w_in": np.ascontiguousarray(inp["w_in"], dtype=np.float32),
        "w_pa": np.ascontiguousarray(inp["w_pa"], dtype=np.float32),
        "w_pb": np.ascontiguousarray(inp["w_pb"], dtype=np.float32),
        "w_pc": np.ascontiguousarray(inp["w_pc"], dtype=np.float32),
        "w_o": np.ascontiguousarray(inp["w_o"], dtype=np.float32),
        "w_glu": np.ascontiguousarray(inp["w_glu"], dtype=np.float32),
        "w_ffn_in": np.ascontiguousarray(inp["w_ffn_in"], dtype=np.float32),
        "w_ffn_out": np.ascontiguousarray(inp["w_ffn_out"], dtype=np.float32),
        "w_s": np.ascontiguousarray(inp["w_s"], dtype=np.float32),
        "b_s": np.ascontiguousarray(inp["b_s"].reshape(DEPTH, 768), dtype=np.float32),
        "smallp": sp,
        "rel_bias": np.ascontiguousarray(inp["rel_bias"], dtype=np.float32),
        "constf": cf,
        "onehot": oh,
    }
    maps = []
    for b in cores:
        m = dict(shared)
        m["xT"] = np.ascontiguousarray(inp["x"][b].T, dtype=np.float32)
        maps.append(m)
    return maps


def kernel(**inputs):
    inp = {k: np.asarray(v) for k, v in inputs.items()}
    if "full" not in _PROG:
        _PROG["full"] = Prog()
    prog = _PROG["full"]
    maps = make_in_maps(inp, list(range(8)))
    res = run_bass_kernel_spmd(prog.nc, maps, core_ids=list(range(8)))
    out = np.stack([np.ascontiguousarray(res.results[b]["outT"].T) for b in range(8)], axis=0)
    return out.astype(np.float32)
```

```python
import math
import os
from contextlib import ExitStack

import numpy as np
import ml_dtypes

import concourse.bass as bass
import concourse.mybir as mybir
from concourse.bass_utils import run_bass_kernel_spmd

F32 = mybir.dt.float32
F32R = mybir.dt.float32r
BF16 = mybir.dt.bfloat16
I32 = mybir.dt.int32
AF = mybir.ActivationFunctionType
ALU = mybir.AluOpType

S = 4096
D = 2048
TT = 512
NTT = S // TT
DEPTH = 4
NCC = 102
Q_OFF, K_OFF, V_OFF, U_OFF, VG_OFF, UC_OFF, GL_OFF = 0, 1536, 3072, 4608, 5376, 6144, 6912
DFF = 5632
ALPHA = (2 * DEPTH) ** 0.25
DILS = (1, 4, 16)
EW = 383
NEG = -30000.0
TWO_PI = 2.0 * math.pi

O_BA, O_SG, O_SB, O_LR, O_LI, O_LD = 0, 102, 108, 114, 138, 162
O_BRE, O_BIM, O_CRE, O_CIM = 186, 570, 954, 1338
O_DSK, O_BGLU, O_L1G, O_L1B, O_L2G, O_L2B = 1722, 1728, 1734, 1750, 1766, 1782
NSP = 1798

SAME_ENGINE_SYNC = bool(int(os.environ.get("K_SES", "1")))


class Res:
    __slots__ = ("name", "w", "r")

    def __init__(self, name):
        self.name = name
        self.w = {}
        self.r = {}


class KB:
    def __init__(self, nc, es):
        self.nc = nc
        self.es = es
        self.eng = {"pe": nc.tensor, "act": nc.scalar, "dve": nc.vector, "pool": nc.gpsimd, "sp": nc.sync}
        self.sem = {}
        self.cnt = {}
        self.waited = {}
        self.resd = {}
        for e in ("pe", "act", "dve", "pool"):
            self._mksem(e)

    def _mksem(self, key):
        if key not in self.sem:
            self.sem[key] = self.es.enter_context(self.nc.semaphore("s_" + key))
            self.cnt[key] = 0
        return self.sem[key]

    def R(self, *key):
        r = self.resd.get(key)
        if r is None:
            r = Res(str(key))
            self.resd[key] = r
        return r

    def _wait(self, e, evs):
        for key, val in evs.items():
            if key == e and not SAME_ENGINE_SYNC:
                continue
            if key == "pe" and e == "pe":
                continue
            if self.waited.get((e, key), 0) >= val:
                continue
            self.eng[e].wait_ge(self.sem[key], val)
            self.waited[(e, key)] = val

    @staticmethod
    def _merge(d, s):
        for k, v in s.items():
            if d.get(k, 0) < v:
                d[k] = v

    def _deps(self, reads, writes):
        evs = {}
        for r in reads:
            self._merge(evs, r.w)
        for w in writes:
            self._merge(evs, w.w)
            self._merge(evs, w.r)
        return evs

    def _commit(self, ev, reads, writes):
        k, v = ev
        for r in reads:
            if r.r.get(k, 0) < v:
                r.r[k] = v
        for w in writes:
            w.w = {k: v}
            w.r = {}

    def op(self, e, fn, reads=(), writes=()):
        self._wait(e, self._deps(reads, writes))
        ins = fn()
        self.cnt[e] += 1
        ins.then_inc(self.sem[e], 1)
        self._commit((e, self.cnt[e]), reads, writes)

    def mm(self, out, pairs, reads, writes, transpose=False):
        self._wait("pe", self._deps(reads, writes))
        n = len(pairs)
        ins = None
        for i, (a, b) in enumerate(pairs):
            ins = self.nc.tensor.matmul(out, lhsT=a, rhs=b, start=(i == 0), stop=(i == n - 1))
        self.cnt["pe"] += 1
        ins.then_inc(self.sem["pe"], 1)
        self._commit(("pe", self.cnt["pe"]), reads, writes)

    def tr(self, out, in_, ident, reads, writes):
        self._wait("pe", self._deps(reads, writes))
        ins = self.nc.tensor.transpose(out, in_, ident)
        self.cnt["pe"] += 1
        ins.then_inc(self.sem["pe"], 1)
        self._commit(("pe", self.cnt["pe"]), reads, writes)

    def dma(self, q, out, in_, reads, writes, semkey, **kw):
        self._mksem(semkey)
        self._wait(q, self._deps(reads, writes))
        ins = self.eng[q].dma_start(out=out, in_=in_, **kw)
        self.cnt[semkey] += 16
        ins.then_inc(self.sem[semkey], 16)
        self._commit((semkey, self.cnt[semkey]), reads, writes)

    def barrier(self):
        evs = {k: v for k, v in self.cnt.items() if v > 0}
        for e in ("pe", "act", "dve", "pool", "sp"):
            for key, val in evs.items():
                if self.waited.get((e, key), 0) >= val:
                    continue
                self.eng[e].wait_ge(self.sem[key], val)
                self.waited[(e, key)] = val


def bc_mid(ap, n):
    a = ap.ap
    return bass.AP(tensor=ap.tensor, offset=ap.offset, ap=[list(a[0]), [0, n], list(a[1])])


def bc_last(ap, n):
    a = ap.ap
    return bass.AP(tensor=ap.tensor, offset=ap.offset, ap=[list(a[0]), list(a[1]), [0, n]])


class WStream:
    def __init__(self, kb, es, nslots, slot_elems):
        self.kb = kb
        self.n = nslots
        self.tiles = [es.enter_context(kb.nc.sbuf_tensor("wsl%d" % i, [128, slot_elems], BF16)) for i in range(nslots)]
        self.res = [Res("wsl%d" % i) for i in range(nslots)]
        self.sched = []
        self.issued = 0
        self.consumed = 0

    def plan(self, tag, dram_ap, nelem, dres):
        self.sched.append((tag, dram_ap, nelem, dres))

    def get(self, tag):
        i = self.consumed
        assert self.sched[i][0] == tag, (self.sched[i][0], tag)
        while self.issued < min(len(self.sched), i + self.n - 1):
            k = self.issued
            _, dap, ne, dres = self.sched[k]
            sl = k % self.n
            self.kb.dma("sp", self.tiles[sl][:, 0:ne], dap, reads=dres, writes=[self.res[sl]], semkey="wsl%d" % sl)
            self.issued += 1
        self.consumed += 1
        sl = i % self.n
        return self.tiles[sl], self.res[sl]


class Prog:
    def __init__(self, nlayers=DEPTH, phases="ABCDEF", debug=False):
        self.nlayers = nlayers
        self.phases = phases
        self.debug = debug
        self.nc = bass.Bass("TRN2", target_bir_lowering=False)
        self.build()

    def dram(self, name, shape, dt, kind):
        return self.nc.dram_tensor(name, list(shape), dt, kind=kind)

    def build(self):
        nc = self.nc
        dk = "ExternalOutput" if self.debug else "Internal"
        self.d_xT = self.dram("xT", [D, S], F32, "ExternalInput")
        self.d_w_in = self.dram("w_in", [DEPTH, D, NCC * 128], F32, "ExternalInput")
        self.d_w_pa = self.dram("w_pa", [DEPTH, 512, D], F32, "ExternalInput")
        self.d_w_pb = self.dram("w_pb", [DEPTH, 768, D], F32, "ExternalInput")
        self.d_w_pc = self.dram("w_pc", [DEPTH, 768, D], F32, "ExternalInput")
        self.d_w_o = self.dram("w_o", [DEPTH, D, D], F32, "ExternalInput")
        self.d_w_glu = self.dram("w_glu", [DEPTH, 768, 768], F32, "ExternalInput")
        self.d_w_f1 = self.dram("w_ffn_in", [DEPTH, D, 2 * DFF], F32, "ExternalInput")
        self.d_w_f2 = self.dram("w_ffn_out", [DEPTH, DFF, D], F32, "ExternalInput")
        self.d_w_s = self.dram("w_s", [DEPTH, 6, 128, 128], F32, "ExternalInput")
        self.d_b_s = self.dram("b_s", [DEPTH, 768], F32, "ExternalInput")
        self.d_sp = self.dram("smallp", [DEPTH, 128, NSP], F32, "ExternalInput")
        self.d_relb = self.dram("rel_bias", [32, 24], F32, "ExternalInput")
        self.d_cf = self.dram("constf", [128, 1664], F32, "ExternalInput")
        self.d_oh = self.dram("onehot", [3, 33, EW], F32, "ExternalInput")
        self.d_out = self.dram("outT", [D, S], F32, "ExternalOutput")
        self.d_P = self.dram("P", [NCC * 128, S], BF16, dk)
        self.d_YA = self.dram("YA", [512, S], BF16, dk)
        self.d_YB = self.dram("YB", [768, S], BF16, dk)
        self.d_YC0 = self.dram("YC0", [768, S], BF16, dk)
        self.d_XT = [self.dram("XTa", [D, S], F32, dk), self.dram("XTb", [D, S], F32, dk)]
        self.d_E = self.dram("Ed", [24, EW], F32, "Internal")
        self.d_Z = self.dram("Zd", [24, 128 * EW], F32, "Internal")
        self.d_WBin = [self.dram("WBin%d" % i, [NCC, 128, 2048], BF16, "Internal") for i in range(2)]
        self.d_WBm = [self.dram("WBm%d" % i, [16, 128, 2048], BF16, "Internal") for i in range(2)]
        self.d_WBo = [self.dram("WBo%d" % i, [16, 128, 2048], BF16, "Internal") for i in range(2)]
        self.d_WBf1 = [self.dram("WBf1%d" % i, [88, 128, 2048], BF16, "Internal") for i in range(2)]
        self.d_WBf2 = [self.dram("WBf2%d" % i, [32, 128, 2816], BF16, "Internal") for i in range(2)]
        self.d_WBg = [self.dram("WBg%d" % i, [2, 128, 2304], BF16, "Internal") for i in range(2)]

        with ExitStack() as es:
            self.es = es
            kb = self.kb = KB(nc, es)
            blk = es.enter_context(nc.Block())

            @blk.sync
            def _(sync):
                self.emit()

    def sb(self, es, name, shape, dt):
        self._uid = getattr(self, "_uid", 0) + 1
        t = es.enter_context(self.nc.sbuf_tensor("%s_%d" % (name, self._uid), list(shape), dt))
        sz = int(np.prod(shape[1:])) * (2 if dt == BF16 else 4)
        self._cur = getattr(self, "_cur", 0) + sz
        self._peak = max(getattr(self, "_peak", 0), self._cur)
        if os.environ.get("K_MEM"):
            print("SB alloc", name, sz, "cur", self._cur)

        def _free():
            self._cur -= sz
        es.callback(_free)
        return t

    def psum(self):
        i = self.ps_i % 8
        self.ps_i += 1
        return self.ps_tiles[i], self.ps_res[i]

    def V(self, fn, reads=(), writes=()):
        self.kb.op("dve", fn, reads, writes)

    def A(self, fn, reads=(), writes=()):
        self.kb.op("act", fn, reads, writes)

    def emit(self):
        nc, kb, es = self.nc, self.kb, self.es
        self.ps_tiles = [es.enter_context(nc.psum_tensor("ps%d" % i, [128, 512], F32)) for i in range(8)]
        self.ps_res = [Res("ps%d" % i) for i in range(8)]
        self.ps_i = 0
        self.ws = WStream(kb, es, 6, 2816)
        self.cf = self.sb(es, "cf", [128, 1664], F32)
        self.r_cf = Res("cf")
        kb.dma("sp", self.cf[:], self.d_cf.ap(), [], [self.r_cf], "cst")
        self.identF = self.cf[:, 0:128]
        self.tril = self.cf[:, 128:256]
        self.iota = self.cf[:, 256:768]
        self.sel = self.cf[:, 768:896]
        self.onesF = self.cf[:, 1024:1152]
        self.onesR = self.cf[:, 1024:1152].bitcast(F32R)
        self.identB = self.sb(es, "identB", [128, 128], BF16)
        self.onesB = self.sb(es, "onesB", [128, 128], BF16)
        self.r_ib = Res("identB")
        self.V(lambda: nc.vector.tensor_copy(self.identB[:], self.identF), [self.r_cf], [self.r_ib])
        self.V(lambda: nc.vector.tensor_copy(self.onesB[:], self.onesF), [self.r_cf], [self.r_ib])
        self.sp = self.sb(es, "sp", [128, NSP], F32)
        self.r_sp = Res("sp")

        self.cast_q = []
        self.plan_weights()
        if "B" in self.phases:
            self.bias_setup()
        self.cast_weights(0)
        self.pump_casts(9)
        for l in range(self.nlayers):
            self.l = l
            self.par = l % 2
            self.x_in = self.d_xT if l == 0 else self.d_XT[(l - 1) % 2]
            self.x_out = self.d_out if l == self.nlayers - 1 else self.d_XT[l % 2]
            kb.dma("sp", self.sp[:], self.d_sp.ap()[l], [], [self.r_sp], "spl")
            if l + 1 < self.nlayers and l > 0:
                self.cast_weights(l + 1)
                if "E" not in self.phases:
                    self.pump_casts()
            if "A" in self.phases:
                ag = self.phase_A_gen()
                for _ in range(self.NQ * 6):
                    next(ag)
                    if l == 0:
                        self.pump_casts(4)
                self.a_emitted = self.NQ * 6
                self.a_done = False
                if "D" in self.phases:
                    self.phase_D(ag)
                while self.a_emitted < self.NQ * (6 + 48):
                    self.pumpA(ag)
                kb.barrier()
                if "B" in self.phases:
                    self.phase_B(ag)
                    kb.barrier()
                if "C" in self.phases:
                    self.phase_C(ag)
                for _ in ag:
                    pass
                kb.barrier()
            if l == 0:
                self.pump_casts()
                if l + 1 < self.nlayers:
                    self.cast_weights(l + 1)
            if "E" in self.phases:
                self.phase_EF()
                self.pump_casts()
                kb.barrier()
        kb.barrier()

    def cast_weights(self, l):
        kb = self.kb
        par = l % 2
        q = self.cast_q

        def cast(dst_t, tile, src_t, src_off, row_stride, kch, ncol_tile, resname, kofs=0):
            dst_elems = dst_t.ap().shape[2]
            src = bass.AP(tensor=src_t, offset=src_off, ap=[[row_stride, 128], [128 * row_stride, kch], [1, ncol_tile]])
            dst = bass.AP(tensor=dst_t, offset=tile * 128 * dst_elems + kofs * ncol_tile,
                          ap=[[dst_elems, 128], [ncol_tile, kch], [1, ncol_tile]])
            q.append((dst, src, (resname, par), "cast_%s_%d" % (resname, par)))

        seen = set()
        for (q_, cc) in self.a_order():
            if cc in seen:
                continue
            seen.add(cc)
            cast(self.d_WBin[par], cc, self.d_w_in, l * D * 13056 + cc * 128, 13056, 16, 128, "win%d" % self.win_group(cc))
        for h in range(2):
            cast(self.d_WBg[par], h, self.d_w_glu, l * 768 * 768 + h * 3 * 128 * 768, 768, 3, 768, "wglu")
        for dc in range(16):
            cast(self.d_WBm[par], dc, self.d_w_pa, l * 512 * D + dc * 128, D, 4, 128, "wm", kofs=0)
            cast(self.d_WBm[par], dc, self.d_w_pb, l * 768 * D + dc * 128, D, 6, 128, "wm", kofs=4)
            cast(self.d_WBm[par], dc, self.d_w_pc, l * 768 * D + dc * 128, D, 6, 128, "wm", kofs=10)
        for dc in range(16):
            cast(self.d_WBo[par], dc, self.d_w_o, l * D * D + dc * 128, D, 16, 128, "wo")
        for j in range(88):
            cast(self.d_WBf1[par], j, self.d_w_f1, l * D * 2 * DFF + j * 128, 2 * DFF, 16, 128, "wf1")
        for dc in range(16):
            for h in range(2):
                cast(self.d_WBf2[par], dc * 2 + h, self.d_w_f2, l * DFF * D + h * 22 * 128 * D + dc * 128, D, 22, 128, "wf2")

    def win_group(self, cc):
        if not hasattr(self, "_wing"):
            order = []
            for (q_, c) in self.a_order():
                if c not in order:
                    order.append(c)
            self._wing = {c: i // 9 for i, c in enumerate(order)}
        return self._wing[cc]

    def pump_casts(self, n=None):
        kb = self.kb
        while self.cast_q and (n is None or n > 0):
            dst, src, rkey, sem = self.cast_q.pop(0)
            kb.dma("pool", dst, src, [], [kb.R(*rkey)], sem)
            if n is not None:
                n -= 1

    def plan_weights(self):
        kb = self.kb
        ws = self.ws
        for l in range(self.nlayers):
            par = l % 2
            if "A" in self.phases:
                for (q, cc) in self.a_order():
                    ws.plan(("A", l, q, cc), self.d_WBin[par].ap()[cc], 2048, [kb.R("win%d" % self.win_group(cc), par)])
            if "E" in self.phases:
                for tt in range(NTT):
                    for h in range(2):
                        ws.plan(("G", l, tt, h), self.d_WBg[par].ap()[h], 2304, [kb.R("wglu", par)])
                    for dc in range(16):
                        ws.plan(("M", l, tt, dc), self.d_WBm[par].ap()[dc], 2048, [kb.R("wm", par)])
                    for dc in range(16):
                        ws.plan(("O", l, tt, dc), self.d_WBo[par].ap()[dc], 2048, [kb.R("wo", par)])
                    for j in range(44):
                        ws.plan(("F1g", l, tt, j), self.d_WBf1[par].ap()[j], 2048, [kb.R("wf1", par)])
                        ws.plan(("F1u", l, tt, j), self.d_WBf1[par].ap()[44 + j], 2048, [kb.R("wf1", par)])
                    for dc in range(16):
                        for h in range(2):
                            ws.plan(("F2", l, tt, dc, h), self.d_WBf2[par].ap()[dc * 2 + h], 2816, [kb.R("wf2", par)])

    NQ = 4

    def a_groups(self):
        ucs = list(range(UC_OFF // 128, UC_OFF // 128 + 6))
        mix = list(range(0, UC_OFF // 128))
        gates = list(range(GL_OFF // 128, NCC))
        return [ucs, mix, gates]

    def a_order(self):
        out = []
        for gi, grp in enumerate(self.a_groups()):
            out += [(q, cc) for q in range(self.NQ) for cc in grp]
        return out

    def pumpA(self, ag, n=1):
        for _ in range(n):
            if self.a_done:
                return
            if next(ag) == "hold":
                self.a_done = True
            else:
                self.a_emitted += 1

    def phase_A_gen(self):
        nc, kb, l = self.nc, self.kb, self.l
        QS = S // self.NQ
        with ExitStack() as es:
            xb = self.sb(es, "xb", [128, 16, QS], BF16)
            r_xb = Res("xb")
            oA = [self.sb(es, "oA%d" % i, [128, QS], BF16) for i in range(2)]
            r_oA = [Res("oA%d" % i) for i in range(2)]
            xin = self.x_in.ap().rearrange("(k p) s -> p k s", p=128)
            cur = None
            it = 0
            first_rest = True
            for (q, cc) in self.a_order():
                gid = 0 if UC_OFF // 128 <= cc < UC_OFF // 128 + 6 else (1 if cc < UC_OFF // 128 else 2)
                if cur != (q, gid):
                    cur = (q, gid)
                    for k4 in range(4):
                        kb.dma("pool", xb[:, k4 * 4:(k4 + 1) * 4, :], xin[:, k4 * 4:(k4 + 1) * 4, q * QS:(q + 1) * QS],
                               [kb.R("X", l, t) for t in range(NTT)], [r_xb], "xbld")
                wt, r_w = self.ws.get(("A", l, q, cc))
                w3 = wt[:, 0:2048].rearrange("p (k c) -> p k c", c=128)
                if cc < 36 or 48 <= cc < 54:
                    fn = AF.Identity
                elif cc < 48:
                    fn = AF.Gelu_apprx_tanh
                else:
                    fn = AF.Sigmoid
                sl = it % 2
                it += 1
                for t in range(QS // TT):
                    ps, r_ps = self.psum()
                    kb.mm(ps[:], [(w3[:, k, :], xb[:, k, t * TT:(t + 1) * TT]) for k in range(16)],
                          [r_w, r_xb], [r_ps])
                    self.A(lambda: nc.scalar.activation(out=oA[sl][:, t * TT:(t + 1) * TT], in_=ps[:], func=fn,
                                                        bias=self.sp[:, O_BA + cc:O_BA + cc + 1], scale=1.0),
                           [r_ps, self.r_sp], [r_oA[sl]])
                kb.dma("pool", self.d_P.ap()[cc * 128:(cc + 1) * 128, q * QS:(q + 1) * QS], oA[sl][:],
                       [r_oA[sl]], [kb.R("P", cc)], "oAst%d" % sl)
                yield
            yield "hold"

    def bias_setup(self):
        nc, kb = self.nc, self.kb
        with ExitStack() as es:
            relb = self.sb(es, "relb", [33, 24], F32)
            oh = self.sb(es, "oh", [33, 3, EW], F32)
            eo = self.sb(es, "eo", [8, 3, EW], F32)
            r1, r2, r3 = Res("relb"), Res("oh"), Res("eo")
            self.V(lambda: nc.vector.memset(relb[:], 1.0), [], [r1])
            kb.dma("sp", relb[0:32, :], self.d_relb.ap(), [], [r1], "bs1")
            kb.dma("sp", oh[:], self.d_oh.ap().rearrange("g b s -> b g s"), [], [r2], "bs2")
            for g in range(3):
                ps, r_ps = self.psum()
                kb.mm(ps[0:8, 0:EW], [(relb[:, g * 8:(g + 1) * 8], oh[:, g, :])], [r1, r2], [r_ps])
                self.V(lambda: nc.vector.tensor_copy(eo[:, g, :], ps[0:8, 0:EW]), [r_ps], [r3])
                kb.dma("sp", self.d_E.ap()[g * 8:(g + 1) * 8, :], eo[:, g, :], [r3], [kb.R("E")], "bs3")
            src = bass.AP(tensor=self.d_E, offset=0, ap=[[EW, 24], [0, 128], [1, EW]])
            dst = bass.AP(tensor=self.d_Z, offset=0, ap=[[128 * EW, 24], [EW, 128], [1, EW]])
            kb.dma("sp", dst, src, [kb.R("E")], [kb.R("Z")], "bs4")
            kb.barrier()

    def phase_B(self, ag):
        nc, kb, l = self.nc, self.kb, self.l
        with ExitStack() as es:
            qkv = [self.sb(es, "qkv%d" % i, [64, 3, S], BF16) for i in range(2)]
            r_qkv = [Res("qkv%d" % i) for i in range(2)]
            va = [self.sb(es, "va%d" % i, [128, 32, 128], BF16) for i in range(2)]
            r_va = [Res("va%d" % i) for i in range(2)]
            bm = [self.sb(es, "bm%d" % i, [128, 2, 128], F32) for i in range(2)]
            r_bm = [Res("bm%d" % i) for i in range(2)]
            acc = [self.sb(es, "acc%d" % i, [128, S], F32) for i in range(1)] * 2
            r_acc = [Res("acc%d" % i) for i in range(1)] * 2
            tq = [self.sb(es, "tq%d" % i, [128, 2, 128], F32) for i in range(4)]
            r_tq = [Res("tq%d" % i) for i in range(4)]
            NPM = 12
            LAG = 9
            pm = [self.sb(es, "pm%d" % i, [128, 2, 128], BF16) for i in range(NPM)]
            r_pm = [Res("pm%d" % i) for i in range(NPM)]
            yo = [self.sb(es, "yo%d" % i, [64, S], BF16) for i in range(1)] * 2
            r_yo = [Res("yo%d" % i) for i in range(1)] * 2
            rd = [self.sb(es, "rd%d" % i, [64, TT], F32) for i in range(2)]
            r_rd = [Res("rd%d" % i) for i in range(2)]
            for i in range(2):
                self.V(lambda: nc.vector.memset(va[i][:, :, 64:128], 1.0), [], [r_va[i]])

            def load(it):
                hl, g = divmod(it, 3)
                head = g * 8 + hl
                sl = it % 2
                for j, off in enumerate((Q_OFF, K_OFF, V_OFF)):
                    row = off + head * 64
                    kb.dma("sp", qkv[sl][:, j, :], self.d_P.ap()[row:row + 64, :], [kb.R("P", row // 128)], [r_qkv[sl]],
                           "qkvld%d" % sl)
                src = bass.AP(tensor=self.d_Z, offset=head * 128 * EW + 127, ap=[[EW - 1, 128], [128, 2], [1, 128]])
                kb.dma("sp", bm[sl][:], src, [kb.R("Z")], [r_bm[sl]], "bmld%d" % sl)

            load(0)
            blk_i = 0
            for it in range(24):
                hl, g = divmod(it, 3)
                r = DILS[g]
                nb = 32 // r
                sl = it % 2
                if it + 1 < 24:
                    load(it + 1)
                q = qkv[sl]
                a = acc[hl % 2]
                r_a = r_acc[hl % 2]

                def tok(c, n):
                    st = 128 * n * r + c
                    return slice(st, st + 127 * r + 1, r)

                for b8 in range(4):
                    ps, r_ps = self.psum()
                    psb = ps[:].bitcast(BF16)
                    for j in range(8):
                        b = b8 * 8 + j
                        c, n = divmod(b, nb)
                        kb.tr(psb[:, j * 64:(j + 1) * 64], q[:, 2, tok(c, n)], self.identB[0:64, 0:64],
                              [r_qkv[sl], self.r_ib], [r_ps])
                    self.V(lambda: nc.vector.tensor_copy(va[sl][:, b8 * 8:(b8 + 1) * 8, 0:64],
                                                         psb[:, 0:512].rearrange("p (j d) -> p j d", d=64)),
                           [r_ps], [r_va[sl]])
                pend = []

                def emit_pv(b, pi):
                    c, n = divmod(b, nb)
                    ps2, r_ps2 = self.psum()
                    pairs = [(va[sl][:, b, :], pm[pi][:, 0, :])]
                    if n > 0:
                        pairs.append((va[sl][:, b - 1, :], pm[pi][:, 1, :]))
                    kb.mm(ps2[:, 0:128], pairs, [r_va[sl], r_pm[pi]], [r_ps2])
                    if g == 0:
                        self.V(lambda: nc.vector.tensor_copy(a[:, tok(c, n)], ps2[:, 0:128]), [r_ps2], [r_a])
                    else:
                        self.V(lambda: nc.vector.tensor_tensor(out=a[:, tok(c, n)], in0=a[:, tok(c, n)], in1=ps2[:, 0:128],
                                                               op=ALU.add), [r_ps2, r_a], [r_a])

                for b in range(32):
                    if blk_i % 20 == 0:
                        self.pumpA(ag, 2)
                    c, n = divmod(b, nb)
                    np_ = 1 if n == 0 else 2
                    ps, r_ps = self.psum()
                    sc = ps[:, 0:256].rearrange("p (a q) -> p a q", q=128)
                    kb.mm(sc[:, 0, :], [(q[:, 1, tok(c, n)], q[:, 0, tok(c, n)])], [r_qkv[sl]], [r_ps])
                    if n > 0:
                        kb.mm(sc[:, 1, :], [(q[:, 1, tok(c, n - 1)], q[:, 0, tok(c, n)])], [r_qkv[sl]], [r_ps])
                    ti = blk_i % 4
                    pi = blk_i % NPM
                    blk_i += 1
                    self.V(lambda: nc.vector.scalar_tensor_tensor(out=tq[ti][:, 0:np_, :], in0=sc[:, 0:np_, :], scalar=0.125,
                                                                  in1=bm[sl][:, 0:np_, :], op0=ALU.mult, op1=ALU.add),
                           [r_ps, r_bm[sl]], [r_tq[ti]])
                    self.A(lambda: nc.scalar.activation(out=pm[pi][:, 0:np_, :], in_=tq[ti][:, 0:np_, :], func=AF.Exp),
                           [r_tq[ti]], [r_pm[pi]])
                    pend.append((b, pi))
                    if len(pend) > LAG:
                        emit_pv(*pend.pop(0))
                while pend:
                    emit_pv(*pend.pop(0))
                if g == 2:
                    ysl = hl % 2
                    for t in range(NTT):
                        ps, r_ps = self.psum()
                        kb.mm(ps[:, :], [(self.sel, a[:, t * TT:(t + 1) * TT])], [self.r_cf, r_a], [r_ps])
                        di = t % 2
                        self.A(lambda: nc.scalar.activation(out=rd[di][:], in_=ps[0:64, :], func=AF.Ln), [r_ps], [r_rd[di]])
                        self.A(lambda: nc.scalar.activation(out=rd[di][:], in_=rd[di][:], func=AF.Exp, scale=-1.0), [r_rd[di]], [r_rd[di]])
                        self.V(lambda: nc.vector.tensor_tensor(out=yo[ysl][:, t * TT:(t + 1) * TT], in0=a[0:64, t * TT:(t + 1) * TT],
                                                               in1=rd[di][:], op=ALU.mult), [r_a, r_rd[di]], [r_yo[ysl]])
                    kb.dma("pool", self.d_YA.ap()[hl * 64:(hl + 1) * 64, :], yo[ysl][:], [r_yo[ysl]], [kb.R("YA", hl // 2)],
                           "yast")

    def ln_stats(self, es_tiles, src_chunks, r_src, nfeat, bf_chunks=None, r_bf=None, sq_chunks=None, r_sqc=None):
        nc, kb = self.nc, self.kb
        sq, r_sq, st, r_st = es_tiles
        ps1, r_ps1 = self.psum()
        kb.mm(ps1[:], [(self.onesB[:], c) for c in bf_chunks], [self.r_ib] + r_bf, [r_ps1])
        ps2, r_ps2 = self.psum()
        n = len(src_chunks)
        if sq_chunks is not None:
            kb.mm(ps2[:], [(self.onesB[:], c) for c in sq_chunks], [self.r_ib] + r_sqc, [r_ps2])
            src_chunks = []
        else:
            kb._wait("pe", kb._deps([self.r_ib], [r_ps2]))
        for i, c in enumerate(src_chunks):
            k = i % len(sq)
            self.A(lambda: nc.scalar.activation(out=sq[k][:], in_=c, func=AF.Square), r_src, [r_sq[k]])
            kb._wait("pe", kb._deps([r_sq[k]], []))
            ins = nc.tensor.matmul(ps2[:], lhsT=self.onesB[:], rhs=sq[k][:], start=(i == 0), stop=(i == n - 1))
            kb.cnt["pe"] += 1
            ins.then_inc(kb.sem["pe"], 1)
            kb._commit(("pe", kb.cnt["pe"]), [r_sq[k]], [r_ps2] if i == n - 1 else [])
        mean, msq, rstd, mr = st
        inv = 1.0 / nfeat
        self.V(lambda: nc.vector.tensor_scalar(out=mean[:], in0=ps1[:], scalar1=inv, scalar2=None, op0=ALU.mult), [r_ps1], [r_st[0]])
        self.V(lambda: nc.vector.tensor_tensor(out=msq[:], in0=mean[:], in1=mean[:], op=ALU.mult), [r_st[0]], [r_st[1]])
        self.V(lambda: nc.vector.scalar_tensor_tensor(out=msq[:], in0=ps2[:], scalar=inv, in1=msq[:], op0=ALU.mult, op1=ALU.subtract),
               [r_ps2, r_st[1]], [r_st[1]])
        self.V(lambda: nc.vector.tensor_scalar(out=msq[:], in0=msq[:], scalar1=1e-5, scalar2=None, op0=ALU.add), [r_st[1]], [r_st[1]])
        self.A(lambda: nc.scalar.activation(out=msq[:], in_=msq[:], func=AF.Sqrt), [r_st[1]], [r_st[1]])
        self.V(lambda: nc.vector.reciprocal(out=rstd[:], in_=msq[:]), [r_st[1]], [r_st[2]])
        self.V(lambda: nc.vector.tensor_tensor(out=mr[:], in0=mean[:], in1=rstd[:], op=ALU.mult), [r_st[0], r_st[2]], [r_st[3]])
        return rstd, r_st[2], mr, r_st[3]

    def ln_tiles(self, es, pfx):
        sq = [self.sb(es, pfx + "sq%d" % i, [128, TT], BF16) for i in range(4)]
        r_sq = [Res(pfx + "sq%d" % i) for i in range(4)]
        st = [self.sb(es, pfx + "st%d" % i, [128, TT], F32) for i in range(4)]
        r_st = [Res(pfx + "st%d" % i) for i in range(4)]
        return sq, r_sq, st, r_st

    def phase_C(self, ag):
        nc, kb, l = self.nc, self.kb, self.l
        with ExitStack() as es:
            wsl = self.sb(es, "wsl", [128, 6, 128], F32)
            wsm = self.sb(es, "wsm", [128, 6, 128], BF16)
            wsT = self.sb(es, "wsT", [128, 6, 128], BF16)
            bsb = self.sb(es, "bsb", [128, 768], F32)
            r_wsl, r_wsm, r_wsT, r_bsb = Res("wsl"), Res("wsm"), Res("wsT"), Res("bsb")
            kb.dma("sp", wsl[:], self.d_w_s.ap()[l].rearrange("g t s -> t g s"), [], [r_wsl], "cws")
            kb.dma("sp", bsb[:], self.d_b_s.ap()[l].partition_broadcast(128), [], [r_bsb], "cbs")
            self.V(lambda: nc.vector.tensor_tensor(out=wsm[:], in0=wsl[:], in1=bc_mid(self.tril, 6), op=ALU.mult),
                   [r_wsl, self.r_cf], [r_wsm])
            ps, r_ps = self.psum()
            psb = ps[:].bitcast(BF16)
            for g in range(6):
                kb.tr(psb[:, g * 128:(g + 1) * 128], wsm[:, g, :], self.identB[:], [r_wsm, self.r_ib], [r_ps])
            self.V(lambda: nc.vector.tensor_copy(wsT[:], psb[:, 0:768].rearrange("p (g t) -> p g t", t=128)), [r_ps], [r_wsT])

            uv = [self.sb(es, "uv%d" % i, [128, 12, TT], BF16) for i in range(2)]
            r_uv = [Res("uv%d" % i) for i in range(2)]
            vf = self.sb(es, "vf", [128, 6, TT], F32)
            r_vf = Res("vf")
            vn = self.sb(es, "vn", [128, 6, TT], BF16)
            r_vn = Res("vn")
            vnT = self.sb(es, "vnT", [128, 6, 4, 128], BF16)
            r_vnT = Res("vnT")
            tmp = [self.sb(es, "ctmp%d" % i, [128, TT], F32) for i in range(2)]
            r_tmp = [Res("ctmp%d" % i) for i in range(2)]
            yb = [self.sb(es, "ybo%d" % i, [128, 6, TT], BF16) for i in range(2)]
            r_yb = [Res("ybo%d" % i) for i in range(2)]
            lnt = self.ln_tiles(es, "c")

            def load(tt):
                sl = tt % 2
                src = self.d_P.ap()[U_OFF:U_OFF + 1536, tt * TT:(tt + 1) * TT].rearrange("(c p) s -> p c s", p=128)
                kb.dma("sp", uv[sl][:], src, [kb.R("P", U_OFF // 128 + c) for c in range(12)], [r_uv[sl]], "uvld%d" % sl)

            load(0)
            for tt in range(NTT):
                sl = tt % 2
                if tt + 1 < NTT:
                    load(tt + 1)
                self.pumpA(ag, 3)
                self.V(lambda: nc.vector.tensor_copy(vf[:], uv[sl][:, 6:12, :]), [r_uv[sl]], [r_vf])
                rstd, r_rstd, mr, r_mr = self.ln_stats(lnt, [vf[:, c, :] for c in range(6)], [r_vf], 768.0,
                                                       [uv[sl][:, 6 + c, :] for c in range(6)], [r_uv[sl]])
                for c in range(6):
                    k = c % 2
                    self.V(lambda: nc.vector.tensor_tensor(out=tmp[k][:], in0=vf[:, c, :], in1=rstd[:], op=ALU.mult),
                           [r_vf, r_rstd], [r_tmp[k]])
                    self.V(lambda: nc.vector.tensor_tensor(out=tmp[k][:], in0=tmp[k][:], in1=mr[:], op=ALU.subtract),
                           [r_tmp[k], r_mr], [r_tmp[k]])
                    self.A(lambda: nc.scalar.activation(out=vn[:, c, :], in_=tmp[k][:], func=AF.Identity,
                                                        scale=self.sp[:, O_SG + c:O_SG + c + 1], bias=self.sp[:, O_SB + c:O_SB + c + 1]),
                           [r_tmp[k], self.r_sp], [r_vn])
                for c in range(6):
                    ps, r_ps = self.psum()
                    psb = ps[:].bitcast(BF16)
                    for j in range(4):
                        kb.tr(psb[:, j * 128:(j + 1) * 128], vn[:, c, j * 128:(j + 1) * 128], self.identB[:], [r_vn, self.r_ib], [r_ps])
                    self.V(lambda: nc.vector.tensor_copy(vnT[:, c, :, :], psb[:, 0:512].rearrange("p (j d) -> p j d", d=128)),
                           [r_ps], [r_vnT])
                for c in range(6):
                    ps, r_ps = self.psum()
                    for j in range(4):
                        kb.mm(ps[:, j * 128:(j + 1) * 128], [(vnT[:, c, j, :], wsT[:, c, :])], [r_vnT, r_wsT], [r_ps])
                    k = c % 2
                    self.V(lambda: nc.vector.tensor_tensor(out=tmp[k][:].rearrange("p (j t) -> p j t", t=128),
                                                           in0=ps[:].rearrange("p (j t) -> p j t", t=128),
                                                           in1=bc_mid(bsb[:, c * 128:(c + 1) * 128], 4), op=ALU.add),
                           [r_ps, r_bsb], [r_tmp[k]])
                    self.V(lambda: nc.vector.tensor_tensor(out=yb[sl][:, c, :], in0=tmp[k][:], in1=uv[sl][:, c, :], op=ALU.mult),
                           [r_tmp[k], r_uv[sl]], [r_yb[sl]])
                dst = self.d_YB.ap()[:, tt * TT:(tt + 1) * TT].rearrange("(c p) s -> p c s", p=128)
                kb.dma("pool", dst, yb[sl][:], [r_yb[sl]], [kb.R("YB", tt)], "ybst%d" % sl)

    def range_reduce(self, x, r_x, tmpf, tmpi, r_t, shape_ap=None):
        nc = self.nc
        C1 = 6.28125
        C2 = TWO_PI - C1
        self.V(lambda: nc.vector.tensor_scalar(out=tmpf, in0=x, scalar1=1.0 / TWO_PI, scalar2=None, op0=ALU.mult), [r_x], [r_t])
        self.V(lambda: nc.vector.tensor_copy(tmpi, tmpf), [r_t], [r_t])
        self.V(lambda: nc.vector.tensor_copy(tmpf, tmpi), [r_t], [r_t])
        self.V(lambda: nc.vector.scalar_tensor_tensor(out=x, in0=tmpf, scalar=-C1, in1=x, op0=ALU.mult, op1=ALU.add), [r_t, r_x], [r_x])
        self.V(lambda: nc.vector.scalar_tensor_tensor(out=x, in0=tmpf, scalar=-C2, in1=x, op0=ALU.mult, op1=ALU.add), [r_t, r_x], [r_x])
        self.V(lambda: nc.vector.tensor_scalar(out=tmpf, in0=x, scalar1=math.pi, scalar2=-TWO_PI, op0=ALU.is_gt, op1=ALU.mult), [r_x], [r_t])
        self.V(lambda: nc.vector.tensor_tensor(out=x, in0=x, in1=tmpf, op=ALU.add), [r_t, r_x], [r_x])
        self.V(lambda: nc.vector.tensor_scalar(out=tmpf, in0=x, scalar1=-math.pi, scalar2=TWO_PI, op0=ALU.is_lt, op1=ALU.mult), [r_x], [r_t])
        self.V(lambda: nc.vector.tensor_tensor(out=x, in0=x, in1=tmpf, op=ALU.add), [r_t, r_x], [r_x])
        self.V(lambda: nc.vector.tensor_scalar(out=x, in0=x, scalar1=math.pi, scalar2=-math.pi, op0=ALU.min, op1=ALU.max), [r_x], [r_x])

    def phase_D(self, ag):
        nc, kb, l = self.nc, self.kb, self.l
        sp = self.sp
        with ExitStack() as es:
            NS = 24
            pp = self.sb(es, "s5p", [128, 20, NS], F32)
            ppi = self.sb(es, "s5pi", [128, 2, NS], I32)
            r_pp = Res("s5p")
            lr, li, ld = sp[:, O_LR:O_LR + NS], sp[:, O_LI:O_LI + NS], sp[:, O_LD:O_LD + NS]
            (DT, MAG, TH, SN, CS, T0, T1, SH, EM1, ABI, AR1, INV, CR, CI, TH2, C5, S5, T2, T3, T4) = [pp[:, i, :] for i in range(20)]
            RS = [self.r_sp, r_pp]

            def v(fn):
                self.V(fn, RS, [r_pp])

            def a(fn):
                self.A(fn, RS, [r_pp])

            a(lambda: nc.scalar.activation(out=DT, in_=ld, func=AF.Exp))
            v(lambda: nc.vector.tensor_tensor(out=T0, in0=lr, in1=DT, op=ALU.mult))
            a(lambda: nc.scalar.activation(out=MAG, in_=T0, func=AF.Exp))
            v(lambda: nc.vector.tensor_scalar(out=EM1, in0=T0, scalar1=1.0 / 6.0, scalar2=1.0, op0=ALU.mult, op1=ALU.add))
            for dv in (5.0, 4.0, 3.0, 2.0):
                v(lambda: nc.vector.tensor_tensor(out=EM1, in0=EM1, in1=T0, op=ALU.mult))
                v(lambda: nc.vector.tensor_scalar(out=EM1, in0=EM1, scalar1=1.0 / dv, scalar2=1.0, op0=ALU.mult, op1=ALU.add))
            v(lambda: nc.vector.tensor_tensor(out=EM1, in0=EM1, in1=T0, op=ALU.mult))
            v(lambda: nc.vector.tensor_tensor(out=TH, in0=li, in1=DT, op=ALU.mult))
            v(lambda: nc.vector.tensor_copy(T1, TH))
            self.range_reduce(T1, r_pp, T2, ppi[:, 0, :], r_pp)
            a(lambda: nc.scalar.activation(out=SN, in_=T1, func=AF.Sin))
            v(lambda: nc.vector.tensor_scalar(out=T1, in0=TH, scalar1=math.pi / 2, scalar2=None, op0=ALU.add))
            self.range_reduce(T1, r_pp, T2, ppi[:, 0, :], r_pp)
            a(lambda: nc.scalar.activation(out=CS, in_=T1, func=AF.Sin))
            v(lambda: nc.vector.tensor_scalar(out=T1, in0=TH, scalar1=0.5, scalar2=None, op0=ALU.mult))
            self.range_reduce(T1, r_pp, T2, ppi[:, 0, :], r_pp)
            a(lambda: nc.scalar.activation(out=SH, in_=T1, func=AF.Sin))
            v(lambda: nc.vector.tensor_tensor(out=ABI, in0=MAG, in1=SN, op=ALU.mult))
            v(lambda: nc.vector.tensor_tensor(out=AR1, in0=EM1, in1=CS, op=ALU.mult))
            v(lambda: nc.vector.tensor_tensor(out=T1, in0=SH, in1=SH, op=ALU.mult))
            v(lambda: nc.vector.scalar_tensor_tensor(out=AR1, in0=T1, scalar=-2.0, in1=AR1, op0=ALU.mult, op1=ALU.add))
            v(lambda: nc.vector.tensor_tensor(out=T1, in0=lr, in1=lr, op=ALU.mult))
            v(lambda: nc.vector.tensor_tensor(out=T2, in0=li, in1=li, op=ALU.mult))
            v(lambda: nc.vector.tensor_tensor(out=T1, in0=T1, in1=T2, op=ALU.add))
            v(lambda: nc.vector.reciprocal(out=INV, in_=T1))
            v(lambda: nc.vector.tensor_tensor(out=T1, in0=AR1, in1=lr, op=ALU.mult))
            v(lambda: nc.vector.tensor_tensor(out=T2, in0=ABI, in1=li, op=ALU.mult))
            v(lambda: nc.vector.tensor_tensor(out=T1, in0=T1, in1=T2, op=ALU.add))
            v(lambda: nc.vector.tensor_tensor(out=CR, in0=T1, in1=INV, op=ALU.mult))
            v(lambda: nc.vector.tensor_tensor(out=T1, in0=ABI, in1=lr, op=ALU.mult))
            v(lambda: nc.vector.tensor_tensor(out=T2, in0=AR1, in1=li, op=ALU.mult))
            v(lambda: nc.vector.tensor_tensor(out=T1, in0=T1, in1=T2, op=ALU.subtract))
            v(lambda: nc.vector.tensor_tensor(out=CI, in0=T1, in1=INV, op=ALU.mult))
            v(lambda: nc.vector.tensor_scalar(out=TH2, in0=TH, scalar1=float(TT), scalar2=None, op0=ALU.mult))
            self.range_reduce(TH2, r_pp, T2, ppi[:, 0, :], r_pp)
            a(lambda: nc.scalar.activation(out=S5, in_=TH2, func=AF.Sin))
            v(lambda: nc.vector.tensor_scalar(out=T1, in0=TH2, scalar1=math.pi / 2, scalar2=None, op0=ALU.add))
            self.range_reduce(T1, r_pp, T2, ppi[:, 0, :], r_pp)
            a(lambda: nc.scalar.activation(out=C5, in_=T1, func=AF.Sin))

            bbr = self.sb(es, "bbr", [128, NS, 16], F32)
            bbi = self.sb(es, "bbi", [128, NS, 16], F32)
            bt = self.sb(es, "bbt", [128, NS, 16], F32)
            r_bb = Res("bb")
            bre = sp[:, O_BRE:O_BRE + 384].rearrange("p (s h) -> p s h", h=16)
            bim = sp[:, O_BIM:O_BIM + 384].rearrange("p (s h) -> p s h", h=16)
            cre = sp[:, O_CRE:O_CRE + 384].rearrange("p (s h) -> p s h", h=16)
            cim = sp[:, O_CIM:O_CIM + 384].rearrange("p (s h) -> p s h", h=16)
            crb, cib = bc_last(CR, 16), bc_last(CI, 16)
            RB = [self.r_sp, r_pp, r_bb]
            self.V(lambda: nc.vector.tensor_tensor(out=bbr[:], in0=bre, in1=crb, op=ALU.mult), RB, [r_bb])
            self.V(lambda: nc.vector.tensor_tensor(out=bt[:], in0=bim, in1=cib, op=ALU.mult), RB, [r_bb])
            self.V(lambda: nc.vector.tensor_tensor(out=bbr[:], in0=bbr[:], in1=bt[:], op=ALU.subtract), RB, [r_bb])
            self.V(lambda: nc.vector.tensor_tensor(out=bbi[:], in0=bim, in1=crb, op=ALU.mult), RB, [r_bb])
            self.V(lambda: nc.vector.tensor_tensor(out=bt[:], in0=bre, in1=cib, op=ALU.mult), RB, [r_bb])
            self.V(lambda: nc.vector.tensor_tensor(out=bbi[:], in0=bbi[:], in1=bt[:], op=ALU.add), RB, [r_bb])

            bwr = self.sb(es, "bwr", [128, NS, 128], BF16)
            bwi = self.sb(es, "bwi", [128, NS, 128], BF16)
            cwr = self.sb(es, "cwr", [128, NS, 128], BF16)
            cwi = self.sb(es, "cwi", [128, NS, 128], BF16)
            r_bw, r_cw = Res("bw"), Res("cw")
            stg = [self.sb(es, "stg%d" % i, [128, 128], F32) for i in range(2)]
            r_stg = [Res("stg%d" % i) for i in range(2)]
            self.V(lambda: nc.vector.memset(cwr[:], 0.0), [], [r_cw])
            self.V(lambda: nc.vector.memset(cwi[:], 0.0), [], [r_cw])
            for sc in range(NS):
                c0 = (sc % 4) * 32
                for hf in range(2):
                    ps_ = slice(hf * 64, hf * 64 + 64)
                    cs_ = slice(c0 + hf * 16, c0 + hf * 16 + 16)
                    self.V(lambda: nc.vector.tensor_copy(cwr[ps_, sc, cs_], cre[ps_, sc, :]), [self.r_sp, r_cw], [r_cw])
                    self.V(lambda: nc.vector.tensor_scalar(out=cwi[ps_, sc, cs_], in0=cim[ps_, sc, :], scalar1=-1.0, scalar2=None,
                                                           op0=ALU.mult), [self.r_sp, r_cw], [r_cw])
            ti = 0
            for sc in range(NS):
                c0 = (sc % 4) * 32
                for src, dstw in ((bbr, bwr), (bbi, bwi)):
                    k = ti % 2
                    ti += 1
                    self.V(lambda: nc.vector.memset(stg[k][:], 0.0), [], [r_stg[k]])
                    for hf in range(2):
                        ps_ = slice(hf * 64, hf * 64 + 64)
                        cs_ = slice(c0 + hf * 16, c0 + hf * 16 + 16)
                        self.V(lambda: nc.vector.tensor_copy(stg[k][ps_, cs_], src[ps_, sc, :]), [r_bb, r_stg[k]], [r_stg[k]])
                    ps, r_ps = self.psum()
                    kb.tr(ps[:, 0:128], stg[k][:], self.identF, [r_stg[k], self.r_cf], [r_ps])
                    self.V(lambda: nc.vector.tensor_copy(dstw[:, sc, :], ps[:, 0:128]), [r_ps], [r_bw])

            cosT = self.sb(es, "cosT", [128, 4, TT], F32)
            sinT = self.sb(es, "sinT", [128, 4, TT], F32)
            rho = self.sb(es, "rho", [128, 4, TT], F32)
            r_tab = Res("tab")
            phs = self.sb(es, "phs", [128, TT], F32)
            phf = self.sb(es, "phf", [128, TT], F32)
            phi = self.sb(es, "phi", [128, TT], I32)
            r_ph, r_pht = Res("phs"), Res("pht")
            ut = [self.sb(es, "ut%d" % i, [128, TT], BF16) for i in range(4)]
            r_ut = [Res("ut%d" % i) for i in range(4)]
            ut2, r_ut2 = ut, r_ut
            pend_y = []
            NT = 4
            tmp = [self.sb(es, "dtmp%d" % i, [128, TT], F32) for i in range(NT)]
            r_tmp = [Res("dtmp%d" % i) for i in range(NT)]
            dre = [self.sb(es, "dre%d" % i, [128, TT], F32) for i in range(2)]
            dim = [self.sb(es, "dim%d" % i, [128, TT], F32) for i in range(2)]
            wre = [self.sb(es, "wre%d" % i, [128, TT], F32) for i in range(3)]
            wim = [self.sb(es, "wim%d" % i, [128, TT], F32) for i in range(3)]
            r_d = [Res("dd%d" % i) for i in range(2)]
            r_w = [Res("ww%d" % i) for i in range(3)]
            xre = [self.sb(es, "xre%d" % i, [128, 4, TT], BF16) for i in range(2)]
            xim = [self.sb(es, "xim%d" % i, [128, 4, TT], BF16) for i in range(2)]
            r_x = [Res("xx%d" % i) for i in range(2)]
            car = self.sb(es, "car", [128, 4, 4], F32)
            r_car4 = [Res("car%d" % i) for i in range(4)]
            so = [self.sb(es, "so%d" % i, [128, TT], F32) for i in range(2)]
            r_so = [Res("so%d" % i) for i in range(2)]
            yo = [self.sb(es, "dyo%d" % i, [128, TT], BF16) for i in range(2)]
            r_yo = [Res("dyo%d" % i) for i in range(2)]
            tix = 0
            wi = 0
            ptix = 0
            ptmp = [self.sb(es, "ptmp%d" % i, [128, TT], F32) for i in range(4)]
            r_ptmp = [Res("ptmp%d" % i) for i in range(4)]

            def load(i):
                uc_, tt_ = divmod(i, NTT)
                sl = i % 4
                kb.dma("sp", ut[sl][:], self.d_P.ap()[UC_OFF + uc_ * 128:UC_OFF + (uc_ + 1) * 128, tt_ * TT:(tt_ + 1) * TT],
                       [kb.R("P", UC_OFF // 128 + uc_)], [r_ut[sl]], "utld%d" % (sl % 2))

            load(0)
            for uc in range(6):
                for s4 in range(4):
                    sc = uc * 4 + s4
                    th_ap = TH[:, sc:sc + 1]
                    self.V(lambda: nc.vector.tensor_scalar(out=phs[:], in0=self.iota, scalar1=th_ap, scalar2=None, op0=ALU.mult),
                           [self.r_cf, r_pp], [r_ph])
                    self.V(lambda: nc.vector.tensor_copy(phf[:], phs[:]), [r_ph], [r_pht])
                    self.range_reduce(phf[:], r_pht, phs[:], phi[:], r_ph)
                    self.A(lambda: nc.scalar.activation(out=sinT[:, s4, :], in_=phf[:], func=AF.Sin), [r_pht, r_tab], [r_tab])
                    self.V(lambda: nc.vector.tensor_scalar(out=phs[:], in0=self.iota, scalar1=th_ap, scalar2=None, op0=ALU.mult),
                           [self.r_cf, r_pp, r_ph], [r_ph])
                    self.V(lambda: nc.vector.tensor_scalar(out=phf[:], in0=phs[:], scalar1=math.pi / 2, scalar2=None, op0=ALU.add),
                           [r_ph, r_pht], [r_pht])
                    self.range_reduce(phf[:], r_pht, phs[:], phi[:], r_ph)
                    self.A(lambda: nc.scalar.activation(out=cosT[:, s4, :], in_=phf[:], func=AF.Sin), [r_pht, r_tab], [r_tab])
                    self.A(lambda: nc.scalar.activation(out=rho[:, s4, :], in_=self.iota, func=AF.Identity, scale=0.0,
                                                        bias=MAG[:, sc:sc + 1]), [self.r_cf, r_pp, r_tab], [r_tab])
                self.V(lambda: nc.vector.memset(car[:], 0.0), [], r_car4)
                for tt in range(NTT):
                    xs = (uc * NTT + tt) % 2
                    usl = (uc * NTT + tt) % 4
                    if uc * NTT + tt + 1 < 6 * NTT:
                        load(uc * NTT + tt + 1)
                    tsl = slice(tt * TT, (tt + 1) * TT)
                    for s4 in range(4):
                        sc = uc * 4 + s4
                        self.pumpA(ag, 1 + (s4 % 2))
                        if l == 0:
                            self.pump_casts(1)
                        pr, r_pr = self.psum()
                        kb.mm(pr[:], [(bwr[:, sc, :], ut[usl][:])], [r_bw, r_ut[usl]], [r_pr])
                        pi_, r_pi = self.psum()
                        kb.mm(pi_[:], [(bwi[:, sc, :], ut[usl][:])], [r_bw, r_ut[usl]], [r_pi])
                        if s4 == 1 and pend_y:
                            pend_y.pop(0)()
                        c_, s_ = cosT[:, s4, :], sinT[:, s4, :]
                        k = wi % 2
                        kw = wi % 3
                        wi += 1
                        t = [tmp[(tix + i) % NT] for i in range(2)]
                        rt = [r_tmp[(tix + i) % NT] for i in range(2)]
                        tix += 2
                        self.V(lambda: nc.vector.tensor_tensor(out=t[0][:], in0=pr[:], in1=c_, op=ALU.mult), [r_pr, r_tab], [rt[0]])
                        self.V(lambda: nc.vector.tensor_tensor(out=t[1][:], in0=pi_[:], in1=s_, op=ALU.mult), [r_pi, r_tab], [rt[1]])
                        self.V(lambda: nc.vector.tensor_tensor(out=dre[k][:], in0=t[0][:], in1=t[1][:], op=ALU.add), [rt[0], rt[1]], [r_d[k]])
                        self.V(lambda: nc.vector.tensor_tensor(out=t[0][:], in0=pi_[:], in1=c_, op=ALU.mult), [r_pi, r_tab, rt[0]], [rt[0]])
                        self.V(lambda: nc.vector.tensor_tensor(out=t[1][:], in0=pr[:], in1=s_, op=ALU.mult), [r_pr, r_tab, rt[1]], [rt[1]])
                        self.V(lambda: nc.vector.tensor_tensor(out=dim[k][:], in0=t[0][:], in1=t[1][:], op=ALU.subtract), [rt[0], rt[1]], [r_d[k]])
                        self.V(lambda: nc.vector.tensor_tensor_scan(out=wre[kw][:], data0=rho[:, s4, :], data1=dre[k][:],
                                                                    initial=car[:, s4, 0:1], op0=ALU.mult, op1=ALU.add),
                               [r_tab, r_d[k], r_car4[s4]], [r_w[kw]])
                        self.V(lambda: nc.vector.tensor_tensor_scan(out=wim[kw][:], data0=rho[:, s4, :], data1=dim[k][:],
                                                                    initial=car[:, s4, 1:2], op0=ALU.mult, op1=ALU.add),
                               [r_tab, r_d[k], r_car4[s4]], [r_w[kw]])
                        if tt + 1 < NTT:
                            c5, s5 = C5[:, sc:sc + 1], S5[:, sc:sc + 1]
                            wl_r, wl_i = wre[kw][:, TT - 1:TT], wim[kw][:, TT - 1:TT]
                            self.V(lambda: nc.vector.tensor_scalar(out=car[:, s4, 2:3], in0=wl_i, scalar1=s5, scalar2=None, op0=ALU.mult),
                                   [r_w[kw], r_pp, r_car4[s4]], [r_car4[s4]])
                            self.V(lambda: nc.vector.tensor_scalar(out=car[:, s4, 3:4], in0=wl_r, scalar1=s5, scalar2=None, op0=ALU.mult),
                                   [r_w[kw], r_pp, r_car4[s4]], [r_car4[s4]])
                            self.V(lambda: nc.vector.scalar_tensor_tensor(out=car[:, s4, 0:1], in0=wl_r, scalar=c5, in1=car[:, s4, 2:3],
                                                                          op0=ALU.mult, op1=ALU.subtract), [r_w[kw], r_pp, r_car4[s4]], [r_car4[s4]])
                            self.V(lambda: nc.vector.scalar_tensor_tensor(out=car[:, s4, 1:2], in0=wl_i, scalar=c5, in1=car[:, s4, 3:4],
                                                                          op0=ALU.mult, op1=ALU.add), [r_w[kw], r_pp, r_car4[s4]], [r_car4[s4]])
                        t = [ptmp[(ptix + i) % 4] for i in range(2)]
                        rt = [r_ptmp[(ptix + i) % 4] for i in range(2)]
                        ptix += 2
                        P_ = lambda fn, rd, wr: kb.op("pool", fn, rd, wr)
                        P_(lambda: nc.gpsimd.tensor_tensor(out=t[0][:], in0=wre[kw][:], in1=c_, op=ALU.mult), [r_w[kw], r_tab], [rt[0]])
                        P_(lambda: nc.gpsimd.tensor_tensor(out=t[1][:], in0=wim[kw][:], in1=s_, op=ALU.mult), [r_w[kw], r_tab], [rt[1]])
                        P_(lambda: nc.gpsimd.tensor_tensor(out=xre[xs][:, s4, :], in0=t[0][:], in1=t[1][:], op=ALU.subtract),
                           [rt[0], rt[1]], [r_x[xs]])
                        P_(lambda: nc.gpsimd.tensor_tensor(out=t[0][:], in0=wre[kw][:], in1=s_, op=ALU.mult), [r_w[kw], r_tab, rt[0]], [rt[0]])
                        P_(lambda: nc.gpsimd.tensor_tensor(out=t[1][:], in0=wim[kw][:], in1=c_, op=ALU.mult), [r_w[kw], r_tab, rt[1]], [rt[1]])
                        P_(lambda: nc.gpsimd.tensor_tensor(out=xim[xs][:, s4, :], in0=t[0][:], in1=t[1][:], op=ALU.add),
                           [rt[0], rt[1]], [r_x[xs]])
                    def emit_y(uc=uc, tt=tt, xs=xs, usl=usl, tsl=tsl):
                        py, r_py = self.psum()
                        pairs = []
                        for s4 in range(4):
                            sc = uc * 4 + s4
                            pairs.append((cwr[:, sc, :], xre[xs][:, s4, :]))
                            pairs.append((cwi[:, sc, :], xim[xs][:, s4, :]))
                        kb.mm(py[:], pairs, [r_cw, r_x[xs]], [r_py])
                        os_ = tt % 2
                        self.V(lambda: nc.vector.scalar_tensor_tensor(out=so[os_][:], in0=ut2[usl][:], scalar=sp[:, O_DSK + uc:O_DSK + uc + 1],
                                                                      in1=py[:], op0=ALU.mult, op1=ALU.add),
                               [r_ut2[usl], self.r_sp, r_py], [r_so[os_]])
                        self.A(lambda: nc.scalar.activation(out=yo[os_][:], in_=so[os_][:], func=AF.Gelu_apprx_tanh), [r_so[os_]], [r_yo[os_]])
                        kb.dma("pool", self.d_YC0.ap()[uc * 128:(uc + 1) * 128, tsl], yo[os_][:], [r_yo[os_]], [kb.R("YC0", tt)],
                               "ycst%d" % os_)
                    pend_y.append(emit_y)
            while pend_y:
                pend_y.pop(0)()

    def phase_EF(self):
        nc, kb, l = self.nc, self.kb, self.l
        sp = self.sp
        with ExitStack() as es:
            arena = self.sb(es, "arena", [128, 44 * TT], BF16)
            hT = arena[:, :].rearrange("p (k t) -> p k t", t=TT)
            yaT = hT[:, 0:4, :]
            ybT = hT[:, 4:10, :]
            y0T = hT[:, 10:16, :]
            ycT = hT[:, 16:22, :]
            mT = hT[:, 22:38, :]
            r_in = Res("ef_in")
            r_yc, r_mT, r_hT = Res("ycT"), Res("mT"), Res("hT")
            rr = self.sb(es, "rr", [128, 16, TT], F32)
            r_rrc = [Res("rr%d" % i) for i in range(16)]
            x1b = self.sb(es, "x1b", [128, 16, TT], BF16)
            r_x1b = Res("x1b")
            sqb = self.sb(es, "sqb", [128, 16, TT], BF16)
            r_sqb = Res("sqb")
            gt = [self.sb(es, "gt%d" % i, [128, 3, TT], BF16) for i in range(2)]
            r_gt = [Res("gt%d" % i) for i in range(2)]
            xr = [self.sb(es, "xr%d" % i, [128, TT], F32) for i in range(2)]
            r_xr = [Res("xr%d" % i) for i in range(2)]
            tmp = [self.sb(es, "etmp%d" % i, [128, TT], F32) for i in range(6)]
            r_tmp = [Res("etmp%d" % i) for i in range(6)]
            lnt = self.ln_tiles(es, "e")
            tix = 0
            X_in = self.x_in.ap()
            X_out = self.x_out.ap()

            def layer_norm(goff, boff, final_store, tt):
                nonlocal tix
                rstd, r_rstd, mr, r_mr = self.ln_stats(lnt, [rr[:, c, :] for c in range(16)], r_rrc, float(D),
                                                       [x1b[:, c, :] for c in range(16)], [r_x1b],
                                                       [sqb[:, c, :] for c in range(16)], [r_sqb])
                for c in range(16):
                    k = tix % 6
                    tix += 1
                    self.V(lambda: nc.vector.tensor_tensor(out=tmp[k][:], in0=rr[:, c, :], in1=rstd[:], op=ALU.mult),
                           [r_rrc[c], r_rstd], [r_tmp[k]])
                    self.V(lambda: nc.vector.tensor_tensor(out=tmp[k][:], in0=tmp[k][:], in1=mr[:], op=ALU.subtract),
                           [r_tmp[k], r_mr], [r_tmp[k]])
                    self.A(lambda: nc.scalar.activation(out=rr[:, c, :], in_=tmp[k][:], func=AF.Identity,
                                                        scale=sp[:, goff + c:goff + c + 1], bias=sp[:, boff + c:boff + c + 1]),
                           [r_tmp[k], self.r_sp], [r_rrc[c]])
                    if not final_store:
                        self.A(lambda: nc.scalar.activation(out=x1b[:, c, :], in_=tmp[k][:], func=AF.Identity,
                                                            scale=sp[:, goff + c:goff + c + 1], bias=sp[:, boff + c:boff + c + 1]),
                               [r_tmp[k], self.r_sp], [r_x1b])
                if final_store:
                    dst = X_out[:, tt * TT:(tt + 1) * TT].rearrange("(c p) s -> p c s", p=128)
                    kb.dma("pool", dst, rr[:], r_rrc, [kb.R("X", l + 1, tt)], "xst")

            for tt in range(NTT):
                tsl = slice(tt * TT, (tt + 1) * TT)
                kb.dma("sp", yaT, self.d_YA.ap()[:, tsl].rearrange("(c p) s -> p c s", p=128), [kb.R("YA", i) for i in range(4)],
                       [r_in, r_hT], "efld")
                kb.dma("sp", ybT, self.d_YB.ap()[:, tsl].rearrange("(c p) s -> p c s", p=128), [kb.R("YB", tt)], [r_in, r_hT], "efld")
                kb.dma("sp", y0T, self.d_YC0.ap()[:, tsl].rearrange("(c p) s -> p c s", p=128), [kb.R("YC0", tt)], [r_in, r_hT], "efld")
                wg = []
                for h in range(2):
                    wt, r_w = self.ws.get(("G", l, tt, h))
                    wg.append((wt[:, 0:2304].rearrange("p (k c) -> p k c", c=768), r_w))
                for oc in range(6):
                    ps, r_ps = self.psum()
                    pairs = [(wg[k // 3][0][:, k % 3, oc * 128:(oc + 1) * 128], y0T[:, k, :]) for k in range(6)]
                    kb.mm(ps[:], pairs, [wg[0][1], wg[1][1], r_in], [r_ps])
                    k = tix % 6
                    tix += 1
                    self.A(lambda: nc.scalar.activation(out=tmp[k][:], in_=ps[:], func=AF.Sigmoid, bias=sp[:, O_BGLU + oc:O_BGLU + oc + 1],
                                                        scale=1.0), [r_ps, self.r_sp], [r_tmp[k]])
                    self.V(lambda: nc.vector.tensor_tensor(out=ycT[:, oc, :], in0=y0T[:, oc, :], in1=tmp[k][:], op=ALU.mult),
                           [r_in, r_tmp[k]], [r_yc])
                for dc in range(16):
                    gs = dc % 2
                    src = bass.AP(tensor=self.d_P, offset=(GL_OFF + dc * 128) * S + tt * TT, ap=[[S, 128], [D * S, 3], [1, TT]])
                    kb.dma("sp", gt[gs][:], src, [kb.R("P", GL_OFF // 128 + br * 16 + dc) for br in range(3)], [r_gt[gs]], "gtld%d" % gs)
                    wt, r_w = self.ws.get(("M", l, tt, dc))
                    w3 = wt[:, 0:2048].rearrange("p (k c) -> p k c", c=128)
                    pa, r_pa = self.psum()
                    kb.mm(pa[:], [(w3[:, k, :], yaT[:, k, :]) for k in range(4)], [r_w, r_in], [r_pa])
                    pb, r_pb = self.psum()
                    kb.mm(pb[:], [(w3[:, 4 + k, :], ybT[:, k, :]) for k in range(6)], [r_w, r_in], [r_pb])
                    pc, r_pc = self.psum()
                    kb.mm(pc[:], [(w3[:, 10 + k, :], ycT[:, k, :]) for k in range(6)], [r_w, r_yc], [r_pc])
                    k0, k1 = tix % 6, (tix + 1) % 6
                    tix += 2
                    self.V(lambda: nc.vector.tensor_tensor(out=tmp[k0][:], in0=pa[:], in1=gt[gs][:, 0, :], op=ALU.mult), [r_pa, r_gt[gs]], [r_tmp[k0]])
                    self.V(lambda: nc.vector.tensor_tensor(out=tmp[k1][:], in0=pb[:], in1=gt[gs][:, 1, :], op=ALU.mult), [r_pb, r_gt[gs]], [r_tmp[k1]])
                    k2 = tix % 6
                    tix += 1
                    self.V(lambda: nc.vector.tensor_tensor(out=tmp[k2][:], in0=pc[:], in1=gt[gs][:, 2, :], op=ALU.mult), [r_pc, r_gt[gs]], [r_tmp[k2]])
                    kb.op("pool", lambda: nc.gpsimd.tensor_tensor(out=tmp[k0][:], in0=tmp[k0][:], in1=tmp[k1][:], op=ALU.add), [r_tmp[k0], r_tmp[k1]], [r_tmp[k0]])
                    kb.op("pool", lambda: nc.gpsimd.tensor_tensor(out=mT[:, dc, :], in0=tmp[k0][:], in1=tmp[k2][:], op=ALU.add), [r_tmp[k0], r_tmp[k2]], [r_mT])
                for dc in range(16):
                    xs = dc % 2
                    kb.dma("sp", xr[xs][:], X_in[dc * 128:(dc + 1) * 128, tsl], [kb.R("X", l, tt)], [r_xr[xs]], "xrld%d" % xs)
                    wt, r_w = self.ws.get(("O", l, tt, dc))
                    w3 = wt[:, 0:2048].rearrange("p (k c) -> p k c", c=128)
                    ps, r_ps = self.psum()
                    kb.mm(ps[:], [(w3[:, k, :], mT[:, k, :]) for k in range(16)], [r_w, r_mT], [r_ps])
                    self.V(lambda: nc.vector.scalar_tensor_tensor(out=rr[:, dc, :], in0=xr[xs][:], scalar=float(ALPHA), in1=ps[:],
                                                                  op0=ALU.mult, op1=ALU.add), [r_xr[xs], r_ps], [r_rrc[dc]])
                    kb.op("pool", lambda: nc.gpsimd.tensor_copy(x1b[:, dc, :], rr[:, dc, :]), [r_rrc[dc]], [r_x1b])
                    self.A(lambda: nc.scalar.activation(out=sqb[:, dc, :], in_=rr[:, dc, :], func=AF.Square), [r_rrc[dc]], [r_sqb])
                layer_norm(O_L1G, O_L1B, False, tt)
                for j in range(44):
                    self.pump_casts(1)
                    wtg, r_wg = self.ws.get(("F1g", l, tt, j))
                    wtu, r_wu = self.ws.get(("F1u", l, tt, j))
                    g3 = wtg[:, 0:2048].rearrange("p (k c) -> p k c", c=128)
                    u3 = wtu[:, 0:2048].rearrange("p (k c) -> p k c", c=128)
                    pg, r_pg = self.psum()
                    kb.mm(pg[:], [(g3[:, k, :], x1b[:, k, :]) for k in range(16)], [r_wg, r_x1b], [r_pg])
                    pu, r_pu = self.psum()
                    kb.mm(pu[:], [(u3[:, k, :], x1b[:, k, :]) for k in range(16)], [r_wu, r_x1b], [r_pu])
                    k = tix % 6
                    tix += 1
                    self.A(lambda: nc.scalar.activation(out=tmp[k][:], in_=pg[:], func=AF.Silu), [r_pg], [r_tmp[k]])
                    self.V(lambda: nc.vector.tensor_tensor(out=hT[:, j, :], in0=pu[:], in1=tmp[k][:], op=ALU.mult),
                           [r_pu, r_tmp[k]], [r_hT, r_in, r_yc, r_mT])
                for dc in range(16):
                    ps, r_ps = self.psum()
                    wa, r_wa = self.ws.get(("F2", l, tt, dc, 0))
                    wb, r_wb = self.ws.get(("F2", l, tt, dc, 1))
                    a3 = wa[:, 0:2816].rearrange("p (k c) -> p k c", c=128)
                    b3 = wb[:, 0:2816].rearrange("p (k c) -> p k c", c=128)
                    pairs = [(a3[:, k, :], hT[:, k, :]) for k in range(22)] + [(b3[:, k, :], hT[:, 22 + k, :]) for k in range(22)]
                    kb.mm(ps[:], pairs, [r_wa, r_wb, r_hT], [r_ps])
                    self.V(lambda: nc.vector.scalar_tensor_tensor(out=rr[:, dc, :], in0=rr[:, dc, :], scalar=float(ALPHA), in1=ps[:],
                                                                  op0=ALU.mult, op1=ALU.add), [r_ps], [r_rrc[dc]])
                    kb.op("pool", lambda: nc.gpsimd.tensor_copy(x1b[:, dc, :], rr[:, dc, :]), [r_rrc[dc]], [r_x1b])
                    self.A(lambda: nc.scalar.activation(out=sqb[:, dc, :], in_=rr[:, dc, :], func=AF.Square), [r_rrc[dc]], [r_sqb])
                layer_norm(O_L2G, O_L2B, True, tt)


def bc_row(ap, n):
    a = ap.ap
    return bass.AP(tensor=ap.tensor, offset=ap.offset, ap=[list(a[0]), [0, n]])


def _t5_bucket(dist):
    max_exact = 16
    d = np.maximum(dist, 1).astype(np.float32)
    scale = (32 - max_exact) / math.log(2048 / max_exact)
    large = max_exact + (np.log(d / max_exact) * scale).astype(np.int32)
    large = np.minimum(large, 31)
    return np.where(dist < max_exact, dist, large).astype(np.int32)


def host_consts():
    cf = np.zeros((128, 1664), np.float32)
    cf[:, 0:128] = np.eye(128, dtype=np.float32)
    cf[:, 128:256] = np.tril(np.ones((128, 128), np.float32))
    cf[:, 256:768] = np.arange(512, dtype=np.float32)[None, :]
    cf[64, 768:896] = 1.0
    cf[:, 1024:1152] = 1.0
    oh = np.zeros((3, 33, EW), np.float32)
    for g, dil in enumerate(DILS):
        for s in range(EW):
            st = s - 127
            if 0 <= st <= 128:
                b = int(_t5_bucket(np.array([st * dil]))[0])
                oh[g, b, s] = 1.0
            else:
                oh[g, 32, s] = NEG
    return cf, oh


def pack_small(inp):
    sp = np.zeros((DEPTH, 128, NSP), np.float32)

    def fm(v, n):
        return np.ascontiguousarray(v.reshape(DEPTH, n, 128).transpose(0, 2, 1))

    def st(v):
        sh = v.shape
        v = v.reshape((DEPTH, 24, 2, 64) + sh[3:])
        v = np.moveaxis(v, 1, 3)
        return np.ascontiguousarray(v.reshape((DEPTH, 128, 24) + sh[3:]))

    sp[:, :, O_BA:O_BA + 102] = fm(inp["b_in"], 102)
    sp[:, :, O_SG:O_SG + 6] = fm(inp["sgu_ln_g"], 6)
    sp[:, :, O_SB:O_SB + 6] = fm(inp["sgu_ln_b"], 6)
    sp[:, :, O_LR:O_LR + 24] = st(inp["lam_re"])
    sp[:, :, O_LI:O_LI + 24] = st(inp["lam_im"])
    sp[:, :, O_LD:O_LD + 24] = st(np.repeat(inp["log_dt"][:, :, None], 64, axis=2))
    sp[:, :, O_BRE:O_BRE + 384] = st(inp["b_re"]).reshape(DEPTH, 128, 384)
    sp[:, :, O_BIM:O_BIM + 384] = st(inp["b_im"]).reshape(DEPTH, 128, 384)
    sp[:, :, O_CRE:O_CRE + 384] = st(np.swapaxes(inp["c_re"], 2, 3)).reshape(DEPTH, 128, 384)
    sp[:, :, O_CIM:O_CIM + 384] = st(np.swapaxes(inp["c_im"], 2, 3)).reshape(DEPTH, 128, 384)
    sp[:, :, O_DSK:O_DSK + 6] = fm(inp["d_skip"], 6)
    sp[:, :, O_BGLU:O_BGLU + 6] = fm(inp["b_glu"], 6)
    sp[:, :, O_L1G:O_L1G + 16] = fm(inp["ln1_g"], 16)
    sp[:, :, O_L1B:O_L1B + 16] = fm(inp["ln1_b"], 16)
    sp[:, :, O_L2G:O_L2G + 16] = fm(inp["ln2_g"], 16)
    sp[:, :, O_L2B:O_L2B + 16] = fm(inp["ln2_b"], 16)
    return sp


_PROG = {}


def make_in_maps(inp, cores):
    cf, oh = host_consts()
    sp = pack_small(inp)
    shared = {
        "w_in": np.ascontiguousarray(inp["w_in"], dtype=np.float32),
        "w_pa": np.ascontiguousarray(inp["w_pa"], dtype=np.float32),
        "w_pb": np.ascontiguousarray(inp["w_pb"], dtype=np.float32),
        "w_pc": np.ascontiguousarray(inp["w_pc"], dtype=np.float32),
        "w_o": np.ascontiguousarray(inp["w_o"], dtype=np.float32),
        "w_glu": np.ascontiguousarray(inp["w_glu"], dtype=np.float32),
        "w_ffn_in": np.ascontiguousarray(inp["w_ffn_in"], dtype=np.float32),
        "w_ffn_out": np.ascontiguousarray(inp["w_ffn_out"], dtype=np.float32),
        "w_s": np.ascontiguousarray(inp["w_s"], dtype=np.float32),
        "b_s": np.ascontiguousarray(inp["b_s"].reshape(DEPTH, 768), dtype=np.float32),
        "smallp": sp,
        "rel_bias": np.ascontiguousarray(inp["rel_bias"], dtype=np.float32),
        "constf": cf,
        "onehot": oh,
    }
    maps = []
    for b in cores:
        m = dict(shared)
        m["xT"] = np.ascontiguousarray(inp["x"][b].T, dtype=np.float32)
        maps.append(m)
    return maps


def kernel(**inputs):
    inp = {k: np.asarray(v) for k, v in inputs.items()}
    if "full" not in _PROG:
        _PROG["full"] = Prog()
    prog = _PROG["full"]
    maps = make_in_maps(inp, list(range(8)))
    res = run_bass_kernel_spmd(prog.nc, maps, core_ids=list(range(8)))
    out = np.stack([np.ascontiguousarray(res.results[b]["outT"].T) for b in range(8)], axis=0)
    return out.astype(np.float32)
```

```python
import math
import os
from contextlib import ExitStack

import numpy as np
import ml_dtypes

import concourse.bass as bass
import concourse.mybir as mybir
from concourse.bass_utils import run_bass_kernel_spmd

F32 = mybir.dt.float32
F32R = mybir.dt.float32r
BF16 = mybir.dt.bfloat16
I32 = mybir.dt.int32
AF = mybir.ActivationFunctionType
ALU = mybir.AluOpType

S = 4096
D = 2048
TT = 512
NTT = S // TT
DEPTH = 4
NCC = 102
Q_OFF, K_OFF, V_OFF, U_OFF, VG_OFF, UC_OFF, GL_OFF = 0, 1536, 3072, 4608, 5376, 6144, 6912
DFF = 5632
ALPHA = (2 * DEPTH) ** 0.25
DILS = (1, 4, 16)
EW = 383
NEG = -30000.0
TWO_PI = 2.0 * math.pi

O_BA, O_SG, O_SB, O_LR, O_LI, O_LD = 0, 102, 108, 114, 138, 162
O_BRE, O_BIM, O_CRE, O_CIM = 186, 570, 954, 1338
O_DSK, O_BGLU, O_L1G, O_L1B, O_L2G, O_L2B = 1722, 1728, 1734, 1750, 1766, 1782
NSP = 1798

SAME_ENGINE_SYNC = bool(int(os.environ.get("K_SES", "1")))


class Res:
    __slots__ = ("name", "w", "r")

    def __init__(self, name):
        self.name = name
        self.w = {}
        self.r = {}


class KB:
    def __init__(self, nc, es):
        self.nc = nc
        self.es = es
        self.eng = {"pe": nc.tensor, "act": nc.scalar, "dve": nc.vector, "pool": nc.gpsimd, "sp": nc.sync}
        self.sem = {}
        self.cnt = {}
        self.waited = {}
        self.resd = {}
        for e in ("pe", "act", "dve", "pool"):
            self._mksem(e)

    def _mksem(self, key):
        if key not in self.sem:
            self.sem[key] = self.es.enter_context(self.nc.semaphore("s_" + key))
            self.cnt[key] = 0
        return self.sem[key]

    def R(self, *key):
        r = self.resd.get(key)
        if r is None:
            r = Res(str(key))
            self.resd[key] = r
        return r

    def _wait(self, e, evs):
        for key, val in evs.items():
            if key == e and not SAME_ENGINE_SYNC:
                continue
            if key == "pe" and e == "pe":
                continue
            if self.waited.get((e, key), 0) >= val:
                continue
            self.eng[e].wait_ge(self.sem[key], val)
            self.waited[(e, key)] = val

    @staticmethod
    def _merge(d, s):
        for k, v in s.items():
            if d.get(k, 0) < v:
                d[k] = v

    def _deps(self, reads, writes):
        evs = {}
        for r in reads:
            self._merge(evs, r.w)
        for w in writes:
            self._merge(evs, w.w)
            self._merge(evs, w.r)
        return evs

    def _commit(self, ev, reads, writes):
        k, v = ev
        for r in reads:
            if r.r.get(k, 0) < v:
                r.r[k] = v
        for w in writes:
            w.w = {k: v}
            w.r = {}

    def op(self, e, fn, reads=(), writes=()):
        self._wait(e, self._deps(reads, writes))
        ins = fn()
        self.cnt[e] += 1
        ins.then_inc(self.sem[e], 1)
        self._commit((e, self.cnt[e]), reads, writes)

    def mm(self, out, pairs, reads, writes, transpose=False):
        self._wait("pe", self._deps(reads, writes))
        n = len(pairs)
        ins = None
        for i, (a, b) in enumerate(pairs):
            ins = self.nc.tensor.matmul(out, lhsT=a, rhs=b, start=(i == 0), stop=(i == n - 1))
        self.cnt["pe"] += 1
        ins.then_inc(self.sem["pe"], 1)
        self._commit(("pe", self.cnt["pe"]), reads, writes)

    def tr(self, out, in_, ident, reads, writes):
        self._wait("pe", self._deps(reads, writes))
        ins = self.nc.tensor.transpose(out, in_, ident)
        self.cnt["pe"] += 1
        ins.then_inc(self.sem["pe"], 1)
        self._commit(("pe", self.cnt["pe"]), reads, writes)

    def dma(self, q, out, in_, reads, writes, semkey, **kw):
        self._mksem(semkey)
        self._wait(q, self._deps(reads, writes))
        ins = self.eng[q].dma_start(out=out, in_=in_, **kw)
        self.cnt[semkey] += 16
        ins.then_inc(self.sem[semkey], 16)
        self._commit((semkey, self.cnt[semkey]), reads, writes)

    def barrier(self):
        evs = {k: v for k, v in self.cnt.items() if v > 0}
        for e in ("pe", "act", "dve", "pool", "sp"):
            for key, val in evs.items():
                if self.waited.get((e, key), 0) >= val:
                    continue
                self.eng[e].wait_ge(self.sem[key], val)
                self.waited[(e, key)] = val


def bc_mid(ap, n):
    a = ap.ap
    return bass.AP(tensor=ap.tensor, offset=ap.offset, ap=[list(a[0]), [0, n], list(a[1])])


def bc_last(ap, n):
    a = ap.ap
    return bass.AP(tensor=ap.tensor, offset=ap.offset, ap=[list(a[0]), list(a[1]), [0, n]])


class WStream:
    def __init__(self, kb, es, nslots, slot_elems):
        self.kb = kb
        self.n = nslots
        self.tiles = [es.enter_context(kb.nc.sbuf_tensor("wsl%d" % i, [128, slot_elems], BF16)) for i in range(nslots)]
        self.res = [Res("wsl%d" % i) for i in range(nslots)]
        self.sched = []
        self.issued = 0
        self.consumed = 0

    def plan(self, tag, dram_ap, nelem, dres):
        self.sched.append((tag, dram_ap, nelem, dres))

    def get(self, tag):
        i = self.consumed
        assert self.sched[i][0] == tag, (self.sched[i][0], tag)
        while self.issued < min(len(self.sched), i + self.n - 1):
            k = self.issued
            _, dap, ne, dres = self.sched[k]
            sl = k % self.n
            self.kb.dma("sp", self.tiles[sl][:, 0:ne], dap, reads=dres, writes=[self.res[sl]], semkey="wsl%d" % sl)
            self.issued += 1
        self.consumed += 1
        sl = i % self.n
        return self.tiles[sl], self.res[sl]


class Prog:
    def __init__(self, nlayers=DEPTH, phases="ABCDEF", debug=False):
        self.nlayers = nlayers
        self.phases = phases
        self.debug = debug
        self.nc = bass.Bass("TRN2", target_bir_lowering=False)
        self.build()

    def dram(self, name, shape, dt, kind):
        return self.nc.dram_tensor(name, list(shape), dt, kind=kind)

    def build(self):
        nc = self.nc
        dk = "ExternalOutput" if self.debug else "Internal"
        self.d_xT = self.dram("xT", [D, S], F32, "ExternalInput")
        self.d_w_in = self.dram("w_in", [DEPTH, D, NCC * 128], F32, "ExternalInput")
        self.d_w_pa = self.dram("w_pa", [DEPTH, 512, D], F32, "ExternalInput")
        self.d_w_pb = self.dram("w_pb", [DEPTH, 768, D], F32, "ExternalInput")
        self.d_w_pc = self.dram("w_pc", [DEPTH, 768, D], F32, "ExternalInput")
        self.d_w_o = self.dram("w_o", [DEPTH, D, D], F32, "ExternalInput")
        self.d_w_glu = self.dram("w_glu", [DEPTH, 768, 768], F32, "ExternalInput")
        self.d_w_f1 = self.dram("w_ffn_in", [DEPTH, D, 2 * DFF], F32, "ExternalInput")
        self.d_w_f2 = self.dram("w_ffn_out", [DEPTH, DFF, D], F32, "ExternalInput")
        self.d_w_s = self.dram("w_s", [DEPTH, 6, 128, 128], F32, "ExternalInput")
        self.d_b_s = self.dram("b_s", [DEPTH, 768], F32, "ExternalInput")
        self.d_sp = self.dram("smallp", [DEPTH, 128, NSP], F32, "ExternalInput")
        self.d_relb = self.dram("rel_bias", [32, 24], F32, "ExternalInput")
        self.d_cf = self.dram("constf", [128, 1664], F32, "ExternalInput")
        self.d_oh = self.dram("onehot", [3, 33, EW], F32, "ExternalInput")
        self.d_out = self.dram("outT", [D, S], F32, "ExternalOutput")
        self.d_P = self.dram("P", [NCC * 128, S], BF16, dk)
        self.d_YA = self.dram("YA", [512, S], BF16, dk)
        self.d_YB = self.dram("YB", [768, S], BF16, dk)
        self.d_YC0 = self.dram("YC0", [768, S], BF16, dk)
        self.d_XT = [self.dram("XTa", [D, S], F32, dk), self.dram("XTb", [D, S], F32, dk)]
        self.d_E = self.dram("Ed", [24, EW], F32, "Internal")
        self.d_Z = self.dram("Zd", [24, 128 * EW], F32, "Internal")
        self.d_WBin = [self.dram("WBin%d" % i, [NCC, 128, 2048], BF16, "Internal") for i in range(2)]
        self.d_WBm = [self.dram("WBm%d" % i, [16, 128, 2048], BF16, "Internal") for i in range(2)]
        self.d_WBo = [self.dram("WBo%d" % i, [16, 128, 2048], BF16, "Internal") for i in range(2)]
        self.d_WBf1 = [self.dram("WBf1%d" % i, [88, 128, 2048], BF16, "Internal") for i in range(2)]
        self.d_WBf2 = [self.dram("WBf2%d" % i, [32, 128, 2816], BF16, "Internal") for i in range(2)]
        self.d_WBg = [self.dram("WBg%d" % i, [2, 128, 2304], BF16, "Internal") for i in range(2)]

        with ExitStack() as es:
            self.es = es
            kb = self.kb = KB(nc, es)
            blk = es.enter_context(nc.Block())

            @blk.sync
            def _(sync):
                self.emit()

    def sb(self, es, name, shape, dt):
        self._uid = getattr(self, "_uid", 0) + 1
        t = es.enter_context(self.nc.sbuf_tensor("%s_%d" % (name, self._uid), list(shape), dt))
        sz = int(np.prod(shape[1:])) * (2 if dt == BF16 else 4)
        self._cur = getattr(self, "_cur", 0) + sz
        self._peak = max(getattr(self, "_peak", 0), self._cur)
        if os.environ.get("K_MEM"):
            print("SB alloc", name, sz, "cur", self._cur)

        def _free():
            self._cur -= sz
        es.callback(_free)
        return t

    def psum(self):
        i = self.ps_i % 8
        self.ps_i += 1
        return self.ps_tiles[i], self.ps_res[i]

    def V(self, fn, reads=(), writes=()):
        self.kb.op("dve", fn, reads, writes)

    def A(self, fn, reads=(), writes=()):
        self.kb.op("act", fn, reads, writes)

    def emit(self):
        nc, kb, es = self.nc, self.kb, self.es
        self.ps_tiles = [es.enter_context(nc.psum_tensor("ps%d" % i, [128, 512], F32)) for i in range(8)]
        self.ps_res = [Res("ps%d" % i) for i in range(8)]
        self.ps_i = 0
        self.ws = WStream(kb, es, 6, 2816)
        self.cf = self.sb(es, "cf", [128, 1664], F32)
        self.r_cf = Res("cf")
        kb.dma("sp", self.cf[:], self.d_cf.ap(), [], [self.r_cf], "cst")
        self.identF = self.cf[:, 0:128]
        self.tril = self.cf[:, 128:256]
        self.iota = self.cf[:, 256:768]
        self.sel = self.cf[:, 768:896]
        self.onesF = self.cf[:, 1024:1152]
        self.onesR = self.cf[:, 1024:1152].bitcast(F32R)
        self.identB = self.sb(es, "identB", [128, 128], BF16)
        self.onesB = self.sb(es, "onesB", [128, 128], BF16)
        self.r_ib = Res("identB")
        self.V(lambda: nc.vector.tensor_copy(self.identB[:], self.identF), [self.r_cf], [self.r_ib])
        self.V(lambda: nc.vector.tensor_copy(self.onesB[:], self.onesF), [self.r_cf], [self.r_ib])
        self.sp = self.sb(es, "sp", [128, NSP], F32)
        self.r_sp = Res("sp")

        self.cast_q = []
        self.plan_weights()
        if "B" in self.phases:
            self.bias_setup()
        self.cast_weights(0)
        self.pump_casts(9)
        for l in range(self.nlayers):
            self.l = l
            self.par = l % 2
            self.x_in = self.d_xT if l == 0 else self.d_XT[(l - 1) % 2]
            self.x_out = self.d_out if l == self.nlayers - 1 else self.d_XT[l % 2]
            kb.dma("sp", self.sp[:], self.d_sp.ap()[l], [], [self.r_sp], "spl")
            if l + 1 < self.nlayers and l > 0:
                self.cast_weights(l + 1)
                if "E" not in self.phases:
                    self.pump_casts()
            if "A" in self.phases:
                ag = self.phase_A_gen()
                for _ in range(self.NQ * 6):
                    next(ag)
                    if l == 0:
                        self.pump_casts(4)
                self.a_emitted = self.NQ * 6
                self.a_done = False
                if "D" in self.phases:
                    self.phase_D(ag)
                while self.a_emitted < self.NQ * (6 + 48):
                    self.pumpA(ag)
                kb.barrier()
                if "B" in self.phases:
                    self.phase_B(ag)
                    kb.barrier()
                if "C" in self.phases:
                    self.phase_C(ag)
                for _ in ag:
                    pass
                kb.barrier()
            if l == 0:
                self.pump_casts()
                if l + 1 < self.nlayers:
                    self.cast_weights(l + 1)
            if "E" in self.phases:
                self.phase_EF()
                self.pump_casts()
                kb.barrier()
        kb.barrier()

    def cast_weights(self, l):
        kb = self.kb
        par = l % 2
        q = self.cast_q

        def cast(dst_t, tile, src_t, src_off, row_stride, kch, ncol_tile, resname, kofs=0):
            dst_elems = dst_t.ap().shape[2]
            src = bass.AP(tensor=src_t, offset=src_off, ap=[[row_stride, 128], [128 * row_stride, kch], [1, ncol_tile]])
            dst = bass.AP(tensor=dst_t, offset=tile * 128 * dst_elems + kofs * ncol_tile,
                          ap=[[dst_elems, 128], [ncol_tile, kch], [1, ncol_tile]])
            q.append((dst, src, (resname, par), "cast_%s_%d" % (resname, par)))

        seen = set()
        for (q_, cc) in self.a_order():
            if cc in seen:
                continue
            seen.add(cc)
            cast(self.d_WBin[par], cc, self.d_w_in, l * D * 13056 + cc * 128, 13056, 16, 128, "win%d" % self.win_group(cc))
        for h in range(2):
            cast(self.d_WBg[par], h, self.d_w_glu, l * 768 * 768 + h * 3 * 128 * 768, 768, 3, 768, "wglu")
        for dc in range(16):
            cast(self.d_WBm[par], dc, self.d_w_pa, l * 512 * D + dc * 128, D, 4, 128, "wm", kofs=0)
            cast(self.d_WBm[par], dc, self.d_w_pb, l * 768 * D + dc * 128, D, 6, 128, "wm", kofs=4)
            cast(self.d_WBm[par], dc, self.d_w_pc, l * 768 * D + dc * 128, D, 6, 128, "wm", kofs=10)
        for dc in range(16):
            cast(self.d_WBo[par], dc, self.d_w_o, l * D * D + dc * 128, D, 16, 128, "wo")
        for j in range(88):
            cast(self.d_WBf1[par], j, self.d_w_f1, l * D * 2 * DFF + j * 128, 2 * DFF, 16, 128, "wf1")
        for dc in range(16):
            for h in range(2):
                cast(self.d_WBf2[par], dc * 2 + h, self.d_w_f2, l * DFF * D + h * 22 * 128 * D + dc * 128, D, 22, 128, "wf2")

    def win_group(self, cc):
        if not hasattr(self, "_wing"):
            order = []
            for (q_, c) in self.a_order():
                if c not in order:
                    order.append(c)
            self._wing = {c: i // 9 for i, c in enumerate(order)}
        return self._wing[cc]

    def pump_casts(self, n=None):
        kb = self.kb
        while self.cast_q and (n is None or n > 0):
            dst, src, rkey, sem = self.cast_q.pop(0)
            kb.dma("pool", dst, src, [], [kb.R(*rkey)], sem)
            if n is not None:
                n -= 1

    def plan_weights(self):
        kb = self.kb
        ws = self.ws
        for l in range(self.nlayers):
            par = l % 2
            if "A" in self.phases:
                for (q, cc) in self.a_order():
                    ws.plan(("A", l, q, cc), self.d_WBin[par].ap()[cc], 2048, [kb.R("win%d" % self.win_group(cc), par)])
            if "E" in self.phases:
                for tt in range(NTT):
                    for h in range(2):
                        ws.plan(("G", l, tt, h), self.d_WBg[par].ap()[h], 2304, [kb.R("wglu", par)])
                    for dc in range(16):
                        ws.plan(("M", l, tt, dc), self.d_WBm[par].ap()[dc], 2048, [kb.R("wm", par)])
                    for dc in range(16):
                        ws.plan(("O", l, tt, dc), self.d_WBo[par].ap()[dc], 2048, [kb.R("wo", par)])
                    for j in range(44):
                        ws.plan(("F1g", l, tt, j), self.d_WBf1[par].ap()[j], 2048, [kb.R("wf1", par)])
                        ws.plan(("F1u", l, tt, j), self.d_WBf1[par].ap()[44 + j], 2048, [kb.R("wf1", par)])
                    for dc in range(16):
                        for h in range(2):
                            ws.plan(("F2", l, tt, dc, h), self.d_WBf2[par].ap()[dc * 2 + h], 2816, [kb.R("wf2", par)])

    NQ = 4

    def a_groups(self):
        ucs = list(range(UC_OFF // 128, UC_OFF // 128 + 6))
        mix = list(range(0, UC_OFF // 128))
        gates = list(range(GL_OFF // 128, NCC))
        return [ucs, mix, gates]

    def a_order(self):
        out = []
        for gi, grp in enumerate(self.a_groups()):
            out += [(q, cc) for q in range(self.NQ) for cc in grp]
        return out

    def pumpA(self, ag, n=1):
        for _ in range(n):
            if self.a_done:
                return
            if next(ag) == "hold":
                self.a_done = True
            else:
                self.a_emitted += 1

    def phase_A_gen(self):
        nc, kb, l = self.nc, self.kb, self.l
        QS = S // self.NQ
        with ExitStack() as es:
            xb = self.sb(es, "xb", [128, 16, QS], BF16)
            r_xb = Res("xb")
            oA = [self.sb(es, "oA%d" % i, [128, QS], BF16) for i in range(2)]
            r_oA = [Res("oA%d" % i) for i in range(2)]
            xin = self.x_in.ap().rearrange("(k p) s -> p k s", p=128)
            cur = None
            it = 0
            first_rest = True
            for (q, cc) in self.a_order():
                gid = 0 if UC_OFF // 128 <= cc < UC_OFF // 128 + 6 else (1 if cc < UC_OFF // 128 else 2)
                if cur != (q, gid):
                    cur = (q, gid)
                    for k4 in range(4):
                        kb.dma("pool", xb[:, k4 * 4:(k4 + 1) * 4, :], xin[:, k4 * 4:(k4 + 1) * 4, q * QS:(q + 1) * QS],
                               [kb.R("X", l, t) for t in range(NTT)], [r_xb], "xbld")
                wt, r_w = self.ws.get(("A", l, q, cc))
                w3 = wt[:, 0:2048].rearrange("p (k c) -> p k c", c=128)
                if cc < 36 or 48 <= cc < 54:
                    fn = AF.Identity
                elif cc < 48:
                    fn = AF.Gelu_apprx_tanh
                else:
                    fn = AF.Sigmoid
                sl = it % 2
                it += 1
                for t in range(QS // TT):
                    ps, r_ps = self.psum()
                    kb.mm(ps[:], [(w3[:, k, :], xb[:, k, t * TT:(t + 1) * TT]) for k in range(16)],
                          [r_w, r_xb], [r_ps])
                    self.A(lambda: nc.scalar.activation(out=oA[sl][:, t * TT:(t + 1) * TT], in_=ps[:], func=fn,
                                                        bias=self.sp[:, O_BA + cc:O_BA + cc + 1], scale=1.0),
                           [r_ps, self.r_sp], [r_oA[sl]])
                kb.dma("pool", self.d_P.ap()[cc * 128:(cc + 1) * 128, q * QS:(q + 1) * QS], oA[sl][:],
                       [r_oA[sl]], [kb.R("P", cc)], "oAst%d" % sl)
                yield
            yield "hold"

    def bias_setup(self):
        nc, kb = self.nc, self.kb
        with ExitStack() as es:
            relb = self.sb(es, "relb", [33, 24], F32)
            oh = self.sb(es, "oh", [33, 3, EW], F32)
            eo = self.sb(es, "eo", [8, 3, EW], F32)
            r1, r2, r3 = Res("relb"), Res("oh"), Res("eo")
            self.V(lambda: nc.vector.memset(relb[:], 1.0), [], [r1])
            kb.dma("sp", relb[0:32, :], self.d_relb.ap(), [], [r1], "bs1")
            kb.dma("sp", oh[:], self.d_oh.ap().rearrange("g b s -> b g s"), [], [r2], "bs2")
            for g in range(3):
                ps, r_ps = self.psum()
                kb.mm(ps[0:8, 0:EW], [(relb[:, g * 8:(g + 1) * 8], oh[:, g, :])], [r1, r2], [r_ps])
                self.V(lambda: nc.vector.tensor_copy(eo[:, g, :], ps[0:8, 0:EW]), [r_ps], [r3])
                kb.dma("sp", self.d_E.ap()[g * 8:(g + 1) * 8, :], eo[:, g, :], [r3], [kb.R("E")], "bs3")
            src = bass.AP(tensor=self.d_E, offset=0, ap=[[EW, 24], [0, 128], [1, EW]])
            dst = bass.AP(tensor=self.d_Z, offset=0, ap=[[128 * EW, 24], [EW, 128], [1, EW]])
            kb.dma("sp", dst, src, [kb.R("E")], [kb.R("Z")], "bs4")
            kb.barrier()

    def phase_B(self, ag):
        nc, kb, l = self.nc, self.kb, self.l
        with ExitStack() as es:
            qkv = [self.sb(es, "qkv%d" % i, [64, 3, S], BF16) for i in range(2)]
            r_qkv = [Res("qkv%d" % i) for i in range(2)]
            va = [self.sb(es, "va%d" % i, [128, 32, 128], BF16) for i in range(2)]
            r_va = [Res("va%d" % i) for i in range(2)]
            bm = [self.sb(es, "bm%d" % i, [128, 2, 128], F32) for i in range(2)]
            r_bm = [Res("bm%d" % i) for i in range(2)]
            acc = [self.sb(es, "acc%d" % i, [128, S], F32) for i in range(1)] * 2
            r_acc = [Res("acc%d" % i) for i in range(1)] * 2
            tq = [self.sb(es, "tq%d" % i, [128, 2, 128], F32) for i in range(4)]
            r_tq = [Res("tq%d" % i) for i in range(4)]
            NPM = 12
            LAG = 9
            pm = [self.sb(es, "pm%d" % i, [128, 2, 128], BF16) for i in range(NPM)]
            r_pm = [Res("pm%d" % i) for i in range(NPM)]
            yo = [self.sb(es, "yo%d" % i, [64, S], BF16) for i in range(1)] * 2
            r_yo = [Res("yo%d" % i) for i in range(1)] * 2
            rd = [self.sb(es, "rd%d" % i, [64, TT], F32) for i in range(2)]
            r_rd = [Res("rd%d" % i) for i in range(2)]
            for i in range(2):
                self.V(lambda: nc.vector.memset(va[i][:, :, 64:128], 1.0), [], [r_va[i]])

            def load(it):
                hl, g = divmod(it, 3)
                head = g * 8 + hl
                sl = it % 2
                for j, off in enumerate((Q_OFF, K_OFF, V_OFF)):
                    row = off + head * 64
                    kb.dma("sp", qkv[sl][:, j, :], self.d_P.ap()[row:row + 64, :], [kb.R("P", row // 128)], [r_qkv[sl]],
                           "qkvld%d" % sl)
                src = bass.AP(tensor=self.d_Z, offset=head * 128 * EW + 127, ap=[[EW - 1, 128], [128, 2], [1, 128]])
                kb.dma("sp", bm[sl][:], src, [kb.R("Z")], [r_bm[sl]], "bmld%d" % sl)

            load(0)
            blk_i = 0
            for it in range(24):
                hl, g = divmod(it, 3)
                r = DILS[g]
                nb = 32 // r
                sl = it % 2
                if it + 1 < 24:
                    load(it + 1)
                q = qkv[sl]
                a = acc[hl % 2]
                r_a = r_acc[hl % 2]

                def tok(c, n):
                    st = 128 * n * r + c
                    return slice(st, st + 127 * r + 1, r)

                for b8 in range(4):
                    ps, r_ps = self.psum()
                    psb = ps[:].bitcast(BF16)
                    for j in range(8):
                        b = b8 * 8 + j
                        c, n = divmod(b, nb)
                        kb.tr(psb[:, j * 64:(j + 1) * 64], q[:, 2, tok(c, n)], self.identB[0:64, 0:64],
                              [r_qkv[sl], self.r_ib], [r_ps])
                    self.V(lambda: nc.vector.tensor_copy(va[sl][:, b8 * 8:(b8 + 1) * 8, 0:64],
                                                         psb[:, 0:512].rearrange("p (j d) -> p j d", d=64)),
                           [r_ps], [r_va[sl]])
                pend = []

                def emit_pv(b, pi):
                    c, n = divmod(b, nb)
                    ps2, r_ps2 = self.psum()
                    pairs = [(va[sl][:, b, :], pm[pi][:, 0, :])]
                    if n > 0:
                        pairs.append((va[sl][:, b - 1, :], pm[pi][:, 1, :]))
                    kb.mm(ps2[:, 0:128], pairs, [r_va[sl], r_pm[pi]], [r_ps2])
                    if g == 0:
                        self.V(lambda: nc.vector.tensor_copy(a[:, tok(c, n)], ps2[:, 0:128]), [r_ps2], [r_a])
                    else:
                        self.V(lambda: nc.vector.tensor_tensor(out=a[:, tok(c, n)], in0=a[:, tok(c, n)], in1=ps2[:, 0:128],
                                                               op=ALU.add), [r_ps2, r_a], [r_a])

                for b in range(32):
                    if blk_i % 20 == 0:
                        self.pumpA(ag, 2)
                    c, n = divmod(b, nb)
                    np_ = 1 if n == 0 else 2
                    ps, r_ps = self.psum()
                    sc = ps[:, 0:256].rearrange("p (a q) -> p a q", q=128)
                    kb.mm(sc[:, 0, :], [(q[:, 1, tok(c, n)], q[:, 0, tok(c, n)])], [r_qkv[sl]], [r_ps])
                    if n > 0:
                        kb.mm(sc[:, 1, :], [(q[:, 1, tok(c, n - 1)], q[:, 0, tok(c, n)])], [r_qkv[sl]], [r_ps])
                    ti = blk_i % 4
                    pi = blk_i % NPM
                    blk_i += 1
                    self.V(lambda: nc.vector.scalar_tensor_tensor(out=tq[ti][:, 0:np_, :], in0=sc[:, 0:np_, :], scalar=0.125,
                                                                  in1=bm[sl][:, 0:np_, :], op0=ALU.mult, op1=ALU.add),
                           [r_ps, r_bm[sl]], [r_tq[ti]])
                    self.A(lambda: nc.scalar.activation(out=pm[pi][:, 0:np_, :], in_=tq[ti][:, 0:np_, :], func=AF.Exp),
                           [r_tq[ti]], [r_pm[pi]])
                    pend.append((b, pi))
                    if len(pend) > LAG:
                        emit_pv(*pend.pop(0))
                while pend:
                    emit_pv(*pend.pop(0))
                if g == 2:
                    ysl = hl % 2
                    for t in range(NTT):
                        ps, r_ps = self.psum()
                        kb.mm(ps[:, :], [(self.sel, a[:, t * TT:(t + 1) * TT])], [self.r_cf, r_a], [r_ps])
                        di = t % 2
                        self.A(lambda: nc.scalar.activation(out=rd[di][:], in_=ps[0:64, :], func=AF.Ln), [r_ps], [r_rd[di]])
                        self.A(lambda: nc.scalar.activation(out=rd[di][:], in_=rd[di][:], func=AF.Exp, scale=-1.0), [r_rd[di]], [r_rd[di]])
                        self.V(lambda: nc.vector.tensor_tensor(out=yo[ysl][:, t * TT:(t + 1) * TT], in0=a[0:64, t * TT:(t + 1) * TT],
                                                               in1=rd[di][:], op=ALU.mult), [r_a, r_rd[di]], [r_yo[ysl]])
                    kb.dma("pool", self.d_YA.ap()[hl * 64:(hl + 1) * 64, :], yo[ysl][:], [r_yo[ysl]], [kb.R("YA", hl // 2)],
                           "yast")

    def ln_stats(self, es_tiles, src_chunks, r_src, nfeat, bf_chunks=None, r_bf=None, sq_chunks=None, r_sqc=None):
        nc, kb = self.nc, self.kb
        sq, r_sq, st, r_st = es_tiles
        ps1, r_ps1 = self.psum()
        kb.mm(ps1[:], [(self.onesB[:], c) for c in bf_chunks], [self.r_ib] + r_bf, [r_ps1])
        ps2, r_ps2 = self.psum()
        n = len(src_chunks)
        if sq_chunks is not None:
            kb.mm(ps2[:], [(self.onesB[:], c) for c in sq_chunks], [self.r_ib] + r_sqc, [r_ps2])
            src_chunks = []
        else:
            kb._wait("pe", kb._deps([self.r_ib], [r_ps2]))
        for i, c in enumerate(src_chunks):
            k = i % len(sq)
            self.A(lambda: nc.scalar.activation(out=sq[k][:], in_=c, func=AF.Square), r_src, [r_sq[k]])
            kb._wait("pe", kb._deps([r_sq[k]], []))
            ins = nc.tensor.matmul(ps2[:], lhsT=self.onesB[:], rhs=sq[k][:], start=(i == 0), stop=(i == n - 1))
            kb.cnt["pe"] += 1
            ins.then_inc(kb.sem["pe"], 1)
            kb._commit(("pe", kb.cnt["pe"]), [r_sq[k]], [r_ps2] if i == n - 1 else [])
        mean, msq, rstd, mr = st
        inv = 1.0 / nfeat
        self.V(lambda: nc.vector.tensor_scalar(out=mean[:], in0=ps1[:], scalar1=inv, scalar2=None, op0=ALU.mult), [r_ps1], [r_st[0]])
        self.V(lambda: nc.vector.tensor_tensor(out=msq[:], in0=mean[:], in1=mean[:], op=ALU.mult), [r_st[0]], [r_st[1]])
        self.V(lambda: nc.vector.scalar_tensor_tensor(out=msq[:], in0=ps2[:], scalar=inv, in1=msq[:], op0=ALU.mult, op1=ALU.subtract),
               [r_ps2, r_st[1]], [r_st[1]])
        self.V(lambda: nc.vector.tensor_scalar(out=msq[:], in0=msq[:], scalar1=1e-5, scalar2=None, op0=ALU.add), [r_st[1]], [r_st[1]])
        self.A(lambda: nc.scalar.activation(out=msq[:], in_=msq[:], func=AF.Sqrt), [r_st[1]], [r_st[1]])
        self.V(lambda: nc.vector.reciprocal(out=rstd[:], in_=msq[:]), [r_st[1]], [r_st[2]])
        self.V(lambda: nc.vector.tensor_tensor(out=mr[:], in0=mean[:], in1=rstd[:], op=ALU.mult), [r_st[0], r_st[2]], [r_st[3]])
        return rstd, r_st[2], mr, r_st[3]

    def ln_tiles(self, es, pfx):
        sq = [self.sb(es, pfx + "sq%d" % i, [128, TT], BF16) for i in range(4)]
        r_sq = [Res(pfx + "sq%d" % i) for i in range(4)]
        st = [self.sb(es, pfx + "st%d" % i, [128, TT], F32) for i in range(4)]
        r_st = [Res(pfx + "st%d" % i) for i in range(4)]
        return sq, r_sq, st, r_st

    def phase_C(self, ag):
        nc, kb, l = self.nc, self.kb, self.l
        with ExitStack() as es:
            wsl = self.sb(es, "wsl", [128, 6, 128], F32)
            wsm = self.sb(es, "wsm", [128, 6, 128], BF16)
            wsT = self.sb(es, "wsT", [128, 6, 128], BF16)
            bsb = self.sb(es, "bsb", [128, 768], F32)
            r_wsl, r_wsm, r_wsT, r_bsb = Res("wsl"), Res("wsm"), Res("wsT"), Res("bsb")
            kb.dma("sp", wsl[:], self.d_w_s.ap()[l].rearrange("g t s -> t g s"), [], [r_wsl], "cws")
            kb.dma("sp", bsb[:], self.d_b_s.ap()[l].partition_broadcast(128), [], [r_bsb], "cbs")
            self.V(lambda: nc.vector.tensor_tensor(out=wsm[:], in0=wsl[:], in1=bc_mid(self.tril, 6), op=ALU.mult),
                   [r_wsl, self.r_cf], [r_wsm])
            ps, r_ps = self.psum()
            psb = ps[:].bitcast(BF16)
            for g in range(6):
                kb.tr(psb[:, g * 128:(g + 1) * 128], wsm[:, g, :], self.identB[:], [r_wsm, self.r_ib], [r_ps])
            self.V(lambda: nc.vector.tensor_copy(wsT[:], psb[:, 0:768].rearrange("p (g t) -> p g t", t=128)), [r_ps], [r_wsT])

            uv = [self.sb(es, "uv%d" % i, [128, 12, TT], BF16) for i in range(2)]
            r_uv = [Res("uv%d" % i) for i in range(2)]
            vf = self.sb(es, "vf", [128, 6, TT], F32)
            r_vf = Res("vf")
            vn = self.sb(es, "vn", [128, 6, TT], BF16)
            r_vn = Res("vn")
            vnT = self.sb(es, "vnT", [128, 6, 4, 128], BF16)
            r_vnT = Res("vnT")
            tmp = [self.sb(es, "ctmp%d" % i, [128, TT], F32) for i in range(2)]
            r_tmp = [Res("ctmp%d" % i) for i in range(2)]
            yb = [self.sb(es, "ybo%d" % i, [128, 6, TT], BF16) for i in range(2)]
            r_yb = [Res("ybo%d" % i) for i in range(2)]
            lnt = self.ln_tiles(es, "c")

            def load(tt):
                sl = tt % 2
                src = self.d_P.ap()[U_OFF:U_OFF + 1536, tt * TT:(tt + 1) * TT].rearrange("(c p) s -> p c s", p=128)
                kb.dma("sp", uv[sl][:], src, [kb.R("P", U_OFF // 128 + c) for c in range(12)], [r_uv[sl]], "uvld%d" % sl)

            load(0)
            for tt in range(NTT):
                sl = tt % 2
                if tt + 1 < NTT:
                    load(tt + 1)
                self.pumpA(ag, 3)
                self.V(lambda: nc.vector.tensor_copy(vf[:], uv[sl][:, 6:12, :]), [r_uv[sl]], [r_vf])
                rstd, r_rstd, mr, r_mr = self.ln_stats(lnt, [vf[:, c, :] for c in range(6)], [r_vf], 768.0,
                                                       [uv[sl][:, 6 + c, :] for c in range(6)], [r_uv[sl]])
                for c in range(6):
                    k = c % 2
                    self.V(lambda: nc.vector.tensor_tensor(out=tmp[k][:], in0=vf[:, c, :], in1=rstd[:], op=ALU.mult),
                           [r_vf, r_rstd], [r_tmp[k]])
                    self.V(lambda: nc.vector.tensor_tensor(out=tmp[k][:], in0=tmp[k][:], in1=mr[:], op=ALU.subtract),
                           [r_tmp[k], r_mr], [r_tmp[k]])
                    self.A(lambda: nc.scalar.activation(out=vn[:, c, :], in_=tmp[k][:], func=AF.Identity,
                                                        scale=self.sp[:, O_SG + c:O_SG + c + 1], bias=self.sp[:, O_SB + c:O_SB + c + 1]),
                           [r_tmp[k], self.r_sp], [r_vn])
                for c in range(6):
                    ps, r_ps = self.psum()
                    psb = ps[:].bitcast(BF16)
                    for j in range(4):
                        kb.tr(psb[:, j * 128:(j + 1) * 128], vn[:, c, j * 128:(j + 1) * 128], self.identB[:], [r_vn, self.r_ib], [r_ps])
                    self.V(lambda: nc.vector.tensor_copy(vnT[:, c, :, :], psb[:, 0:512].rearrange("p (j d) -> p j d", d=128)),
                           [r_ps], [r_vnT])
                for c in range(6):
                    ps, r_ps = self.psum()
                    for j in range(4):
                        kb.mm(ps[:, j * 128:(j + 1) * 128], [(vnT[:, c, j, :], wsT[:, c, :])], [r_vnT, r_wsT], [r_ps])
                    k = c % 2
                    self.V(lambda: nc.vector.tensor_tensor(out=tmp[k][:].rearrange("p (j t) -> p j t", t=128),
                                                           in0=ps[:].rearrange("p (j t) -> p j t", t=128),
                                                           in1=bc_mid(bsb[:, c * 128:(c + 1) * 128], 4), op=ALU.add),
                           [r_ps, r_bsb], [r_tmp[k]])
                    self.V(lambda: nc.vector.tensor_tensor(out=yb[sl][:, c, :], in0=tmp[k][:], in1=uv[sl][:, c, :], op=ALU.mult),
                           [r_tmp[k], r_uv[sl]], [r_yb[sl]])
                dst = self.d_YB.ap()[:, tt * TT:(tt + 1) * TT].rearrange("(c p) s -> p c s", p=128)
                kb.dma("pool", dst, yb[sl][:], [r_yb[sl]], [kb.R("YB", tt)], "ybst%d" % sl)

    def range_reduce(self, x, r_x, tmpf, tmpi, r_t, shape_ap=None):
        nc = self.nc
        C1 = 6.28125
        C2 = TWO_PI - C1
        self.V(lambda: nc.vector.tensor_scalar(out=tmpf, in0=x, scalar1=1.0 / TWO_PI, scalar2=None, op0=ALU.mult), [r_x], [r_t])
        self.V(lambda: nc.vector.tensor_copy(tmpi, tmpf), [r_t], [r_t])
        self.V(lambda: nc.vector.tensor_copy(tmpf, tmpi), [r_t], [r_t])
        self.V(lambda: nc.vector.scalar_tensor_tensor(out=x, in0=tmpf, scalar=-C1, in1=x, op0=ALU.mult, op1=ALU.add), [r_t, r_x], [r_x])
        self.V(lambda: nc.vector.scalar_tensor_tensor(out=x, in0=tmpf, scalar=-C2, in1=x, op0=ALU.mult, op1=ALU.add), [r_t, r_x], [r_x])
        self.V(lambda: nc.vector.tensor_scalar(out=tmpf, in0=x, scalar1=math.pi, scalar2=-TWO_PI, op0=ALU.is_gt, op1=ALU.mult), [r_x], [r_t])
        self.V(lambda: nc.vector.tensor_tensor(out=x, in0=x, in1=tmpf, op=ALU.add), [r_t, r_x], [r_x])
        self.V(lambda: nc.vector.tensor_scalar(out=tmpf, in0=x, scalar1=-math.pi, scalar2=TWO_PI, op0=ALU.is_lt, op1=ALU.mult), [r_x], [r_t])
        self.V(lambda: nc.vector.tensor_tensor(out=x, in0=x, in1=tmpf, op=ALU.add), [r_t, r_x], [r_x])
        self.V(lambda: nc.vector.tensor_scalar(out=x, in0=x, scalar1=math.pi, scalar2=-math.pi, op0=ALU.min, op1=ALU.max), [r_x], [r_x])

    def phase_D(self, ag):
        nc, kb, l = self.nc, self.kb, self.l
        sp = self.sp
        with ExitStack() as es:
            NS = 24
            pp = self.sb(es, "s5p", [128, 20, NS], F32)
            ppi = self.sb(es, "s5pi", [128, 2, NS], I32)
            r_pp = Res("s5p")
            lr, li, ld = sp[:, O_LR:O_LR + NS], sp[:, O_LI:O_LI + NS], sp[:, O_LD:O_LD + NS]
            (DT, MAG, TH, SN, CS, T0, T1, SH, EM1, ABI, AR1, INV, CR, CI, TH2, C5, S5, T2, T3, T4) = [pp[:, i, :] for i in range(20)]
            RS = [self.r_sp, r_pp]

            def v(fn):
                self.V(fn, RS, [r_pp])

            def a(fn):
                self.A(fn, RS, [r_pp])

            a(lambda: nc.scalar.activation(out=DT, in_=ld, func=AF.Exp))
            v(lambda: nc.vector.tensor_tensor(out=T0, in0=lr, in1=DT, op=ALU.mult))
            a(lambda: nc.scalar.activation(out=MAG, in_=T0, func=AF.Exp))
            v(lambda: nc.vector.tensor_scalar(out=EM1, in0=T0, scalar1=1.0 / 6.0, scalar2=1.0, op0=ALU.mult, op1=ALU.add))
            for dv in (5.0, 4.0, 3.0, 2.0):
                v(lambda: nc.vector.tensor_tensor(out=EM1, in0=EM1, in1=T0, op=ALU.mult))
                v(lambda: nc.vector.tensor_scalar(out=EM1, in0=EM1, scalar1=1.0 / dv, scalar2=1.0, op0=ALU.mult, op1=ALU.add))
            v(lambda: nc.vector.tensor_tensor(out=EM1, in0=EM1, in1=T0, op=ALU.mult))
            v(lambda: nc.vector.tensor_tensor(out=TH, in0=li, in1=DT, op=ALU.mult))
            v(lambda: nc.vector.tensor_copy(T1, TH))
            self.range_reduce(T1, r_pp, T2, ppi[:, 0, :], r_pp)
            a(lambda: nc.scalar.activation(out=SN, in_=T1, func=AF.Sin))
            v(lambda: nc.vector.tensor_scalar(out=T1, in0=TH, scalar1=math.pi / 2, scalar2=None, op0=ALU.add))
            self.range_reduce(T1, r_pp, T2, ppi[:, 0, :], r_pp)
            a(lambda: nc.scalar.activation(out=CS, in_=T1, func=AF.Sin))
            v(lambda: nc.vector.tensor_scalar(out=T1, in0=TH, scalar1=0.5, scalar2=None, op0=ALU.mult))
            self.range_reduce(T1, r_pp, T2, ppi[:, 0, :], r_pp)
            a(lambda: nc.scalar.activation(out=SH, in_=T1, func=AF.Sin))
            v(lambda: nc.vector.tensor_tensor(out=ABI, in0=MAG, in1=SN, op=ALU.mult))
            v(lambda: nc.vector.tensor_tensor(out=AR1, in0=EM1, in1=CS, op=ALU.mult))
            v(lambda: nc.vector.tensor_tensor(out=T1, in0=SH, in1=SH, op=ALU.mult))
            v(lambda: nc.vector.scalar_tensor_tensor(out=AR1, in0=T1, scalar=-2.0, in1=AR1, op0=ALU.mult, op1=ALU.add))
            v(lambda: nc.vector.tensor_tensor(out=T1, in0=lr, in1=lr, op=ALU.mult))
            v(lambda: nc.vector.tensor_tensor(out=T2, in0=li, in1=li, op=ALU.mult))
            v(lambda: nc.vector.tensor_tensor(out=T1, in0=T1, in1=T2, op=ALU.add))
            v(lambda: nc.vector.reciprocal(out=INV, in_=T1))
            v(lambda: nc.vector.tensor_tensor(out=T1, in0=AR1, in1=lr, op=ALU.mult))
            v(lambda: nc.vector.tensor_tensor(out=T2, in0=ABI, in1=li, op=ALU.mult))
            v(lambda: nc.vector.tensor_tensor(out=T1, in0=T1, in1=T2, op=ALU.add))
            v(lambda: nc.vector.tensor_tensor(out=CR, in0=T1, in1=INV, op=ALU.mult))
            v(lambda: nc.vector.tensor_tensor(out=T1, in0=ABI, in1=lr, op=ALU.mult))
            v(lambda: nc.vector.tensor_tensor(out=T2, in0=AR1, in1=li, op=ALU.mult))
            v(lambda: nc.vector.tensor_tensor(out=T1, in0=T1, in1=T2, op=ALU.subtract))
            v(lambda: nc.vector.tensor_tensor(out=CI, in0=T1, in1=INV, op=ALU.mult))
            v(lambda: nc.vector.tensor_scalar(out=TH2, in0=TH, scalar1=float(TT), scalar2=None, op0=ALU.mult))
            self.range_reduce(TH2, r_pp, T2, ppi[:, 0, :], r_pp)
            a(lambda: nc.scalar.activation(out=S5, in_=TH2, func=AF.Sin))
            v(lambda: nc.vector.tensor_scalar(out=T1, in0=TH2, scalar1=math.pi / 2, scalar2=None, op0=ALU.add))
            self.range_reduce(T1, r_pp, T2, ppi[:, 0, :], r_pp)
            a(lambda: nc.scalar.activation(out=C5, in_=T1, func=AF.Sin))

            bbr = self.sb(es, "bbr", [128, NS, 16], F32)
            bbi = self.sb(es, "bbi", [128, NS, 16], F32)
            bt = self.sb(es, "bbt", [128, NS, 16], F32)
            r_bb = Res("bb")
            bre = sp[:, O_BRE:O_BRE + 384].rearrange("p (s h) -> p s h", h=16)
            bim = sp[:, O_BIM:O_BIM + 384].rearrange("p (s h) -> p s h", h=16)
            cre = sp[:, O_CRE:O_CRE + 384].rearrange("p (s h) -> p s h", h=16)
            cim = sp[:, O_CIM:O_CIM + 384].rearrange("p (s h) -> p s h", h=16)
            crb, cib = bc_last(CR, 16), bc_last(CI, 16)
            RB = [self.r_sp, r_pp, r_bb]
            self.V(lambda: nc.vector.tensor_tensor(out=bbr[:], in0=bre, in1=crb, op=ALU.mult), RB, [r_bb])
            self.V(lambda: nc.vector.tensor_tensor(out=bt[:], in0=bim, in1=cib, op=ALU.mult), RB, [r_bb])
            self.V(lambda: nc.vector.tensor_tensor(out=bbr[:], in0=bbr[:], in1=bt[:], op=ALU.subtract), RB, [r_bb])
            self.V(lambda: nc.vector.tensor_tensor(out=bbi[:], in0=bim, in1=crb, op=ALU.mult), RB, [r_bb])
            self.V(lambda: nc.vector.tensor_tensor(out=bt[:], in0=bre, in1=cib, op=ALU.mult), RB, [r_bb])
            self.V(lambda: nc.vector.tensor_tensor(out=bbi[:], in0=bbi[:], in1=bt[:], op=ALU.add), RB, [r_bb])

            bwr = self.sb(es, "bwr", [128, NS, 128], BF16)
            bwi = self.sb(es, "bwi", [128, NS, 128], BF16)
            cwr = self.sb(es, "cwr", [128, NS, 128], BF16)
            cwi = self.sb(es, "cwi", [128, NS, 128], BF16)
            r_bw, r_cw = Res("bw"), Res("cw")
            stg = [self.sb(es, "stg%d" % i, [128, 128], F32) for i in range(2)]
            r_stg = [Res("stg%d" % i) for i in range(2)]
            self.V(lambda: nc.vector.memset(cwr[:], 0.0), [], [r_cw])
            self.V(lambda: nc.vector.memset(cwi[:], 0.0), [], [r_cw])
            for sc in range(NS):
                c0 = (sc % 4) * 32
                for hf in range(2):
                    ps_ = slice(hf * 64, hf * 64 + 64)
                    cs_ = slice(c0 + hf * 16, c0 + hf * 16 + 16)
                    self.V(lambda: nc.vector.tensor_copy(cwr[ps_, sc, cs_], cre[ps_, sc, :]), [self.r_sp, r_cw], [r_cw])
                    self.V(lambda: nc.vector.tensor_scalar(out=cwi[ps_, sc, cs_], in0=cim[ps_, sc, :], scalar1=-1.0, scalar2=None,
                                                           op0=ALU.mult), [self.r_sp, r_cw], [r_cw])
            ti = 0
            for sc in range(NS):
                c0 = (sc % 4) * 32
                for src, dstw in ((bbr, bwr), (bbi, bwi)):
                    k = ti % 2
                    ti += 1
                    self.V(lambda: nc.vector.memset(stg[k][:], 0.0), [], [r_stg[k]])
                    for hf in range(2):
                        ps_ = slice(hf * 64, hf * 64 + 64)
                        cs_ = slice(c0 + hf * 16, c0 + hf * 16 + 16)
                        self.V(lambda: nc.vector.tensor_copy(stg[k][ps_, cs_], src[ps_, sc, :]), [r_bb, r_stg[k]], [r_stg[k]])
                    ps, r_ps = self.psum()
                    kb.tr(ps[:, 0:128], stg[k][:], self.identF, [r_stg[k], self.r_cf], [r_ps])
                    self.V(lambda: nc.vector.tensor_copy(dstw[:, sc, :], ps[:, 0:128]), [r_ps], [r_bw])

            cosT = self.sb(es, "cosT", [128, 4, TT], F32)
            sinT = self.sb(es, "sinT", [128, 4, TT], F32)
            rho = self.sb(es, "rho", [128, 4, TT], F32)
            r_tab = Res("tab")
            phs = self.sb(es, "phs", [128, TT], F32)
            phf = self.sb(es, "phf", [128, TT], F32)
            phi = self.sb(es, "phi", [128, TT], I32)
            r_ph, r_pht = Res("phs"), Res("pht")
            ut = [self.sb(es, "ut%d" % i, [128, TT], BF16) for i in range(4)]
            r_ut = [Res("ut%d" % i) for i in range(4)]
            ut2, r_ut2 = ut, r_ut
            pend_y = []
            NT = 4
            tmp = [self.sb(es, "dtmp%d" % i, [128, TT], F32) for i in range(NT)]
            r_tmp = [Res("dtmp%d" % i) for i in range(NT)]
            dre = [self.sb(es, "dre%d" % i, [128, TT], F32) for i in range(2)]
            dim = [self.sb(es, "dim%d" % i, [128, TT], F32) for i in range(2)]
            wre = [self.sb(es, "wre%d" % i, [128, TT], F32) for i in range(3)]
            wim = [self.sb(es, "wim%d" % i, [128, TT], F32) for i in range(3)]
            r_d = [Res("dd%d" % i) for i in range(2)]
            r_w = [Res("ww%d" % i) for i in range(3)]
            xre = [self.sb(es, "xre%d" % i, [128, 4, TT], BF16) for i in range(2)]
            xim = [self.sb(es, "xim%d" % i, [128, 4, TT], BF16) for i in range(2)]
            r_x = [Res("xx%d" % i) for i in range(2)]
            car = self.sb(es, "car", [128, 4, 4], F32)
            r_car4 = [Res("car%d" % i) for i in range(4)]
            so = [self.sb(es, "so%d" % i, [128, TT], F32) for i in range(2)]
            r_so = [Res("so%d" % i) for i in range(2)]
            yo = [self.sb(es, "dyo%d" % i, [128, TT], BF16) for i in range(2)]
            r_yo = [Res("dyo%d" % i) for i in range(2)]
            tix = 0
            wi = 0
            ptix = 0
            ptmp = [self.sb(es, "ptmp%d" % i, [128, TT], F32) for i in range(4)]
            r_ptmp = [Res("ptmp%d" % i) for i in range(4)]

            def load(i):
                uc_, tt_ = divmod(i, NTT)
                sl = i % 4
                kb.dma("sp", ut[sl][:], self.d_P.ap()[UC_OFF + uc_ * 128:UC_OFF + (uc_ + 1) * 128, tt_ * TT:(tt_ + 1) * TT],
                       [kb.R("P", UC_OFF // 128 + uc_)], [r_ut[sl]], "utld%d" % (sl % 2))

            load(0)
            for uc in range(6):
                for s4 in range(4):
                    sc = uc * 4 + s4
                    self.pumpA(ag, 2)
                    th_ap = TH[:, sc:sc + 1]
                    self.V(lambda: nc.vector.tensor_scalar(out=phs[:], in0=self.iota, scalar1=th_ap, scalar2=None, op0=ALU.mult),
                           [self.r_cf, r_pp], [r_ph])
                    self.V(lambda: nc.vector.tensor_copy(phf[:], phs[:]), [r_ph], [r_pht])
                    self.range_reduce(phf[:], r_pht, phs[:], phi[:], r_ph)
                    self.A(lambda: nc.scalar.activation(out=sinT[:, s4, :], in_=phf[:], func=AF.Sin), [r_pht, r_tab], [r_tab])
                    self.V(lambda: nc.vector.tensor_scalar(out=phs[:], in0=self.iota, scalar1=th_ap, scalar2=None, op0=ALU.mult),
                           [self.r_cf, r_pp, r_ph], [r_ph])
                    self.V(lambda: nc.vector.tensor_scalar(out=phf[:], in0=phs[:], scalar1=math.pi / 2, scalar2=None, op0=ALU.add),
                           [r_ph, r_pht], [r_pht])
                    self.range_reduce(phf[:], r_pht, phs[:], phi[:], r_ph)
                    self.A(lambda: nc.scalar.activation(out=cosT[:, s4, :], in_=phf[:], func=AF.Sin), [r_pht, r_tab], [r_tab])
                    self.A(lambda: nc.scalar.activation(out=rho[:, s4, :], in_=self.iota, func=AF.Identity, scale=0.0,
                                                        bias=MAG[:, sc:sc + 1]), [self.r_cf, r_pp, r_tab], [r_tab])
                self.V(lambda: nc.vector.memset(car[:], 0.0), [], r_car4)
                for tt in range(NTT):
                    xs = (uc * NTT + tt) % 2
                    usl = (uc * NTT + tt) % 4
                    if uc * NTT + tt + 1 < 6 * NTT:
                        load(uc * NTT + tt + 1)
                    tsl = slice(tt * TT, (tt + 1) * TT)
                    for s4 in range(4):
                        sc = uc * 4 + s4
                        self.pumpA(ag, 1 + (1 if s4 == 0 else 0))
                        if l == 0:
                            self.pump_casts(1)
                        pr, r_pr = self.psum()
                        kb.mm(pr[:], [(bwr[:, sc, :], ut[usl][:])], [r_bw, r_ut[usl]], [r_pr])
                        pi_, r_pi = self.psum()
                        kb.mm(pi_[:], [(bwi[:, sc, :], ut[usl][:])], [r_bw, r_ut[usl]], [r_pi])
                        if s4 == 1 and pend_y:
                            pend_y.pop(0)()
                        c_, s_ = cosT[:, s4, :], sinT[:, s4, :]
                        k = wi % 2
                        kw = wi % 3
                        wi += 1
                        t = [tmp[(tix + i) % NT] for i in range(2)]
                        rt = [r_tmp[(tix + i) % NT] for i in range(2)]
                        tix += 2
                        self.V(lambda: nc.vector.tensor_tensor(out=t[0][:], in0=pr[:], in1=c_, op=ALU.mult), [r_pr, r_tab], [rt[0]])
                        self.V(lambda: nc.vector.tensor_tensor(out=t[1][:], in0=pi_[:], in1=s_, op=ALU.mult), [r_pi, r_tab], [rt[1]])
                        self.V(lambda: nc.vector.tensor_tensor(out=dre[k][:], in0=t[0][:], in1=t[1][:], op=ALU.add), [rt[0], rt[1]], [r_d[k]])
                        self.V(lambda: nc.vector.tensor_tensor(out=t[0][:], in0=pi_[:], in1=c_, op=ALU.mult), [r_pi, r_tab, rt[0]], [rt[0]])
                        self.V(lambda: nc.vector.tensor_tensor(out=t[1][:], in0=pr[:], in1=s_, op=ALU.mult), [r_pr, r_tab, rt[1]], [rt[1]])
                        self.V(lambda: nc.vector.tensor_tensor(out=dim[k][:], in0=t[0][:], in1=t[1][:], op=ALU.subtract), [rt[0], rt[1]], [r_d[k]])
                        self.V(lambda: nc.vector.tensor_tensor_scan(out=wre[kw][:], data0=rho[:, s4, :], data1=dre[k][:],
                                                                    initial=car[:, s4, 0:1], op0=ALU.mult, op1=ALU.add),
                               [r_tab, r_d[k], r_car4[s4]], [r_w[kw]])
                        self.V(lambda: nc.vector.tensor_tensor_scan(out=wim[kw][:], data0=rho[:, s4, :], data1=dim[k][:],
                                                                    initial=car[:, s4, 1:2], op0=ALU.mult, op1=ALU.add),
                               [r_tab, r_d[k], r_car4[s4]], [r_w[kw]])
                        if tt + 1 < NTT:
                            c5, s5 = C5[:, sc:sc + 1], S5[:, sc:sc + 1]
                            wl_r, wl_i = wre[kw][:, TT - 1:TT], wim[kw][:, TT - 1:TT]
                            self.V(lambda: nc.vector.tensor_scalar(out=car[:, s4, 2:3], in0=wl_i, scalar1=s5, scalar2=None, op0=ALU.mult),
                                   [r_w[kw], r_pp, r_car4[s4]], [r_car4[s4]])
                            self.V(lambda: nc.vector.tensor_scalar(out=car[:, s4, 3:4], in0=wl_r, scalar1=s5, scalar2=None, op0=ALU.mult),
                                   [r_w[kw], r_pp, r_car4[s4]], [r_car4[s4]])
                            self.V(lambda: nc.vector.scalar_tensor_tensor(out=car[:, s4, 0:1], in0=wl_r, scalar=c5, in1=car[:, s4, 2:3],
                                                                          op0=ALU.mult, op1=ALU.subtract), [r_w[kw], r_pp, r_car4[s4]], [r_car4[s4]])
                            self.V(lambda: nc.vector.scalar_tensor_tensor(out=car[:, s4, 1:2], in0=wl_i, scalar=c5, in1=car[:, s4, 3:4],
                                                                          op0=ALU.mult, op1=ALU.add), [r_w[kw], r_pp, r_car4[s4]], [r_car4[s4]])
                        t = [ptmp[(ptix + i) % 4] for i in range(2)]
                        rt = [r_ptmp[(ptix + i) % 4] for i in range(2)]
                        ptix += 2
                        P_ = lambda fn, rd, wr: kb.op("pool", fn, rd, wr)
                        P_(lambda: nc.gpsimd.tensor_tensor(out=t[0][:], in0=wre[kw][:], in1=c_, op=ALU.mult), [r_w[kw], r_tab], [rt[0]])
                        P_(lambda: nc.gpsimd.tensor_tensor(out=t[1][:], in0=wim[kw][:], in1=s_, op=ALU.mult), [r_w[kw], r_tab], [rt[1]])
                        P_(lambda: nc.gpsimd.tensor_tensor(out=xre[xs][:, s4, :], in0=t[0][:], in1=t[1][:], op=ALU.subtract),
                           [rt[0], rt[1]], [r_x[xs]])
                        P_(lambda: nc.gpsimd.tensor_tensor(out=t[0][:], in0=wre[kw][:], in1=s_, op=ALU.mult), [r_w[kw], r_tab, rt[0]], [rt[0]])
                        P_(lambda: nc.gpsimd.tensor_tensor(out=t[1][:], in0=wim[kw][:], in1=c_, op=ALU.mult), [r_w[kw], r_tab, rt[1]], [rt[1]])
                        P_(lambda: nc.gpsimd.tensor_tensor(out=xim[xs][:, s4, :], in0=t[0][:], in1=t[1][:], op=ALU.add),
                           [rt[0], rt[1]], [r_x[xs]])
                    def emit_y(uc=uc, tt=tt, xs=xs, usl=usl, tsl=tsl):
                        py, r_py = self.psum()
                        pairs = []
                        for s4 in range(4):
                            sc = uc * 4 + s4
                            pairs.append((cwr[:, sc, :], xre[xs][:, s4, :]))
                            pairs.append((cwi[:, sc, :], xim[xs][:, s4, :]))
                        kb.mm(py[:], pairs, [r_cw, r_x[xs]], [r_py])
                        os_ = tt % 2
                        self.V(lambda: nc.vector.scalar_tensor_tensor(out=so[os_][:], in0=ut2[usl][:], scalar=sp[:, O_DSK + uc:O_DSK + uc + 1],
                                                                      in1=py[:], op0=ALU.mult, op1=ALU.add),
                               [r_ut2[usl], self.r_sp, r_py], [r_so[os_]])
                        self.A(lambda: nc.scalar.activation(out=yo[os_][:], in_=so[os_][:], func=AF.Gelu_apprx_tanh), [r_so[os_]], [r_yo[os_]])
                        kb.dma("pool", self.d_YC0.ap()[uc * 128:(uc + 1) * 128, tsl], yo[os_][:], [r_yo[os_]], [kb.R("YC0", tt)],
                               "ycst%d" % os_)
                    pend_y.append(emit_y)
            while pend_y:
                pend_y.pop(0)()

    def phase_EF(self):
        nc, kb, l = self.nc, self.kb, self.l
        sp = self.sp
        with ExitStack() as es:
            arena = self.sb(es, "arena", [128, 44 * TT], BF16)
            hT = arena[:, :].rearrange("p (k t) -> p k t", t=TT)
            yaT = hT[:, 0:4, :]
            ybT = hT[:, 4:10, :]
            y0T = hT[:, 10:16, :]
            ycT = hT[:, 16:22, :]
            mT = hT[:, 22:38, :]
            r_in = Res("ef_in")
            r_yc, r_mT, r_hT = Res("ycT"), Res("mT"), Res("hT")
            rr = self.sb(es, "rr", [128, 16, TT], F32)
            r_rrc = [Res("rr%d" % i) for i in range(16)]
            x1b = self.sb(es, "x1b", [128, 16, TT], BF16)
            r_x1b = Res("x1b")
            sqb = self.sb(es, "sqb", [128, 16, TT], BF16)
            r_sqb = Res("sqb")
            gt = [self.sb(es, "gt%d" % i, [128, 3, TT], BF16) for i in range(2)]
            r_gt = [Res("gt%d" % i) for i in range(2)]
            xr = [self.sb(es, "xr%d" % i, [128, TT], F32) for i in range(2)]
            r_xr = [Res("xr%d" % i) for i in range(2)]
            tmp = [self.sb(es, "etmp%d" % i, [128, TT], F32) for i in range(6)]
            r_tmp = [Res("etmp%d" % i) for i in range(6)]
            lnt = self.ln_tiles(es, "e")
            tix = 0
            X_in = self.x_in.ap()
            X_out = self.x_out.ap()

            def layer_norm(goff, boff, final_store, tt):
                nonlocal tix
                rstd, r_rstd, mr, r_mr = self.ln_stats(lnt, [rr[:, c, :] for c in range(16)], r_rrc, float(D),
                                                       [x1b[:, c, :] for c in range(16)], [r_x1b],
                                                       [sqb[:, c, :] for c in range(16)], [r_sqb])
                for c in range(16):
                    k = tix % 6
                    tix += 1
                    self.V(lambda: nc.vector.tensor_tensor(out=tmp[k][:], in0=rr[:, c, :], in1=rstd[:], op=ALU.mult),
                           [r_rrc[c], r_rstd], [r_tmp[k]])
                    self.V(lambda: nc.vector.tensor_tensor(out=tmp[k][:], in0=tmp[k][:], in1=mr[:], op=ALU.subtract),
                           [r_tmp[k], r_mr], [r_tmp[k]])
                    self.A(lambda: nc.scalar.activation(out=rr[:, c, :], in_=tmp[k][:], func=AF.Identity,
                                                        scale=sp[:, goff + c:goff + c + 1], bias=sp[:, boff + c:boff + c + 1]),
                           [r_tmp[k], self.r_sp], [r_rrc[c]])
                    if not final_store:
                        self.A(lambda: nc.scalar.activation(out=x1b[:, c, :], in_=tmp[k][:], func=AF.Identity,
                                                            scale=sp[:, goff + c:goff + c + 1], bias=sp[:, boff + c:boff + c + 1]),
                               [r_tmp[k], self.r_sp], [r_x1b])
                if final_store:
                    dst = X_out[:, tt * TT:(tt + 1) * TT].rearrange("(c p) s -> p c s", p=128)
                    kb.dma("pool", dst, rr[:], r_rrc, [kb.R("X", l + 1, tt)], "xst")

            for tt in range(NTT):
                tsl = slice(tt * TT, (tt + 1) * TT)
                kb.dma("sp", yaT, self.d_YA.ap()[:, tsl].rearrange("(c p) s -> p c s", p=128), [kb.R("YA", i) for i in range(4)],
                       [r_in, r_hT], "efld")
                kb.dma("sp", ybT, self.d_YB.ap()[:, tsl].rearrange("(c p) s -> p c s", p=128), [kb.R("YB", tt)], [r_in, r_hT], "efld")
                kb.dma("sp", y0T, self.d_YC0.ap()[:, tsl].rearrange("(c p) s -> p c s", p=128), [kb.R("YC0", tt)], [r_in, r_hT], "efld")
                wg = []
                for h in range(2):
                    wt, r_w = self.ws.get(("G", l, tt, h))
                    wg.append((wt[:, 0:2304].rearrange("p (k c) -> p k c", c=768), r_w))
                for oc in range(6):
                    ps, r_ps = self.psum()
                    pairs = [(wg[k // 3][0][:, k % 3, oc * 128:(oc + 1) * 128], y0T[:, k, :]) for k in range(6)]
                    kb.mm(ps[:], pairs, [wg[0][1], wg[1][1], r_in], [r_ps])
                    k = tix % 6
                    tix += 1
                    self.A(lambda: nc.scalar.activation(out=tmp[k][:], in_=ps[:], func=AF.Sigmoid, bias=sp[:, O_BGLU + oc:O_BGLU + oc + 1],
                                                        scale=1.0), [r_ps, self.r_sp], [r_tmp[k]])
                    self.V(lambda: nc.vector.tensor_tensor(out=ycT[:, oc, :], in0=y0T[:, oc, :], in1=tmp[k][:], op=ALU.mult),
                           [r_in, r_tmp[k]], [r_yc])
                for dc in range(16):
                    gs = dc % 2
                    src = bass.AP(tensor=self.d_P, offset=(GL_OFF + dc * 128) * S + tt * TT, ap=[[S, 128], [D * S, 3], [1, TT]])
                    kb.dma("sp", gt[gs][:], src, [kb.R("P", GL_OFF // 128 + br * 16 + dc) for br in range(3)], [r_gt[gs]], "gtld%d" % gs)
                    wt, r_w = self.ws.get(("M", l, tt, dc))
                    w3 = wt[:, 0:2048].rearrange("p (k c) -> p k c", c=128)
                    pa, r_pa = self.psum()
                    kb.mm(pa[:], [(w3[:, k, :], yaT[:, k, :]) for k in range(4)], [r_w, r_in], [r_pa])
                    pb, r_pb = self.psum()
                    kb.mm(pb[:], [(w3[:, 4 + k, :], ybT[:, k, :]) for k in range(6)], [r_w, r_in], [r_pb])
                    pc, r_pc = self.psum()
                    kb.mm(pc[:], [(w3[:, 10 + k, :], ycT[:, k, :]) for k in range(6)], [r_w, r_yc], [r_pc])
                    k0, k1 = tix % 6, (tix + 1) % 6
                    tix += 2
                    self.V(lambda: nc.vector.tensor_tensor(out=tmp[k0][:], in0=pa[:], in1=gt[gs][:, 0, :], op=ALU.mult), [r_pa, r_gt[gs]], [r_tmp[k0]])
                    self.V(lambda: nc.vector.tensor_tensor(out=tmp[k1][:], in0=pb[:], in1=gt[gs][:, 1, :], op=ALU.mult), [r_pb, r_gt[gs]], [r_tmp[k1]])
                    k2 = tix % 6
                    tix += 1
                    self.V(lambda: nc.vector.tensor_tensor(out=tmp[k2][:], in0=pc[:], in1=gt[gs][:, 2, :], op=ALU.mult), [r_pc, r_gt[gs]], [r_tmp[k2]])
                    kb.op("pool", lambda: nc.gpsimd.tensor_tensor(out=tmp[k0][:], in0=tmp[k0][:], in1=tmp[k1][:], op=ALU.add), [r_tmp[k0], r_tmp[k1]], [r_tmp[k0]])
                    kb.op("pool", lambda: nc.gpsimd.tensor_tensor(out=mT[:, dc, :], in0=tmp[k0][:], in1=tmp[k2][:], op=ALU.add), [r_tmp[k0], r_tmp[k2]], [r_mT])
                for dc in range(16):
                    xs = dc % 2
                    kb.dma("sp", xr[xs][:], X_in[dc * 128:(dc + 1) * 128, tsl], [kb.R("X", l, tt)], [r_xr[xs]], "xrld%d" % xs)
                    wt, r_w = self.ws.get(("O", l, tt, dc))
                    w3 = wt[:, 0:2048].rearrange("p (k c) -> p k c", c=128)
                    ps, r_ps = self.psum()
                    kb.mm(ps[:], [(w3[:, k, :], mT[:, k, :]) for k in range(16)], [r_w, r_mT], [r_ps])
                    self.V(lambda: nc.vector.scalar_tensor_tensor(out=rr[:, dc, :], in0=xr[xs][:], scalar=float(ALPHA), in1=ps[:],
                                                                  op0=ALU.mult, op1=ALU.add), [r_xr[xs], r_ps], [r_rrc[dc]])
                    kb.op("pool", lambda: nc.gpsimd.tensor_copy(x1b[:, dc, :], rr[:, dc, :]), [r_rrc[dc]], [r_x1b])
                    self.A(lambda: nc.scalar.activation(out=sqb[:, dc, :], in_=rr[:, dc, :], func=AF.Square), [r_rrc[dc]], [r_sqb])
                layer_norm(O_L1G, O_L1B, False, tt)
                for j in range(44):
                    self.pump_casts(1)
                    wtg, r_wg = self.ws.get(("F1g", l, tt, j))
                    wtu, r_wu = self.ws.get(("F1u", l, tt, j))
                    g3 = wtg[:, 0:2048].rearrange("p (k c) -> p k c", c=128)
                    u3 = wtu[:, 0:2048].rearrange("p (k c) -> p k c", c=128)
                    pg, r_pg = self.psum()
                    kb.mm(pg[:], [(g3[:, k, :], x1b[:, k, :]) for k in range(16)], [r_wg, r_x1b], [r_pg])
                    pu, r_pu = self.psum()
                    kb.mm(pu[:], [(u3[:, k, :], x1b[:, k, :]) for k in range(16)], [r_wu, r_x1b], [r_pu])
                    k = tix % 6
                    tix += 1
                    self.A(lambda: nc.scalar.activation(out=tmp[k][:], in_=pg[:], func=AF.Silu), [r_pg], [r_tmp[k]])
                    self.V(lambda: nc.vector.tensor_tensor(out=hT[:, j, :], in0=pu[:], in1=tmp[k][:], op=ALU.mult),
                           [r_pu, r_tmp[k]], [r_hT, r_in, r_yc, r_mT])
                for dc in range(16):
                    ps, r_ps = self.psum()
                    wa, r_wa = self.ws.get(("F2", l, tt, dc, 0))
                    wb, r_wb = self.ws.get(("F2", l, tt, dc, 1))
                    a3 = wa[:, 0:2816].rearrange("p (k c) -> p k c", c=128)
                    b3 = wb[:, 0:2816].rearrange("p (k c) -> p k c", c=128)
                    pairs = [(a3[:, k, :], hT[:, k, :]) for k in range(22)] + [(b3[:, k, :], hT[:, 22 + k, :]) for k in range(22)]
                    kb.mm(ps[:], pairs, [r_wa, r_wb, r_hT], [r_ps])
                    self.V(lambda: nc.vector.scalar_tensor_tensor(out=rr[:, dc, :], in0=rr[:, dc, :], scalar=float(ALPHA), in1=ps[:],
                                                                  op0=ALU.mult, op1=ALU.add), [r_ps], [r_rrc[dc]])
                    kb.op("pool", lambda: nc.gpsimd.tensor_copy(x1b[:, dc, :], rr[:, dc, :]), [r_rrc[dc]], [r_x1b])
                    self.A(lambda: nc.scalar.activation(out=sqb[:, dc, :], in_=rr[:, dc, :], func=AF.Square), [r_rrc[dc]], [r_sqb])
                layer_norm(O_L2G, O_L2B, True, tt)


def bc_row(ap, n):
    a = ap.ap
    return bass.AP(tensor=ap.tensor, offset=ap.offset, ap=[list(a[0]), [0, n]])


def _t5_bucket(dist):
    max_exact = 16
    d = np.maximum(dist, 1).astype(np.float32)
    scale = (32 - max_exact) / math.log(2048 / max_exact)
    large = max_exact + (np.log(d / max_exact) * scale).astype(np.int32)
    large = np.minimum(large, 31)
    return np.where(dist < max_exact, dist, large).astype(np.int32)


def host_consts():
    cf = np.zeros((128, 1664), np.float32)
    cf[:, 0:128] = np.eye(128, dtype=np.float32)
    cf[:, 128:256] = np.tril(np.ones((128, 128), np.float32))
    cf[:, 256:768] = np.arange(512, dtype=np.float32)[None, :]
    cf[64, 768:896] = 1.0
    cf[:, 1024:1152] = 1.0
    oh = np.zeros((3, 33, EW), np.float32)
    for g, dil in enumerate(DILS):
        for s in range(EW):
            st = s - 127
            if 0 <= st <= 128:
                b = int(_t5_bucket(np.array([st * dil]))[0])
                oh[g, b, s] = 1.0
            else:
                oh[g, 32, s] = NEG
    return cf, oh


def pack_small(inp):
    sp = np.zeros((DEPTH, 128, NSP), np.float32)

    def fm(v, n):
        return np.ascontiguousarray(v.reshape(DEPTH, n, 128).transpose(0, 2, 1))

    def st(v):
        sh = v.shape
        v = v.reshape((DEPTH, 24, 2, 64) + sh[3:])
        v = np.moveaxis(v, 1, 3)
        return np.ascontiguousarray(v.reshape((DEPTH, 128, 24) + sh[3:]))

    sp[:, :, O_BA:O_BA + 102] = fm(inp["b_in"], 102)
    sp[:, :, O_SG:O_SG + 6] = fm(inp["sgu_ln_g"], 6)
    sp[:, :, O_SB:O_SB + 6] = fm(inp["sgu_ln_b"], 6)
    sp[:, :, O_LR:O_LR + 24] = st(inp["lam_re"])
    sp[:, :, O_LI:O_LI + 24] = st(inp["lam_im"])
    sp[:, :, O_LD:O_LD + 24] = st(np.repeat(inp["log_dt"][:, :, None], 64, axis=2))
    sp[:, :, O_BRE:O_BRE + 384] = st(inp["b_re"]).reshape(DEPTH, 128, 384)
    sp[:, :, O_BIM:O_BIM + 384] = st(inp["b_im"]).reshape(DEPTH, 128, 384)
    sp[:, :, O_CRE:O_CRE + 384] = st(np.swapaxes(inp["c_re"], 2, 3)).reshape(DEPTH, 128, 384)
    sp[:, :, O_CIM:O_CIM + 384] = st(np.swapaxes(inp["c_im"], 2, 3)).reshape(DEPTH, 128, 384)
    sp[:, :, O_DSK:O_DSK + 6] = fm(inp["d_skip"], 6)
    sp[:, :, O_BGLU:O_BGLU + 6] = fm(inp["b_glu"], 6)
    sp[:, :, O_L1G:O_L1G + 16] = fm(inp["ln1_g"], 16)
    sp[:, :, O_L1B:O_L1B + 16] = fm(inp["ln1_b"], 16)
    sp[:, :, O_L2G:O_L2G + 16] = fm(inp["ln2_g"], 16)
    sp[:, :, O_L2B:O_L2B + 16] = fm(inp["ln2_b"], 16)
    return sp


_PROG = {}


def make_in_maps(inp, cores):
    cf, oh = host_consts()
    sp = pack_small(inp)
    shared = {
        "w_in": np.ascontiguousarray(inp["w_in"], dtype=np.float32),
        "w_pa": np.ascontiguousarray(inp["w_pa"], dtype=np.float32),
        "w_pb": np.ascontiguousarray(inp["w_pb"], dtype=np.float32),
        "w_pc": np.ascontiguousarray(inp["w_pc"], dtype=np.float32),
        "w_o": np.ascontiguousarray(inp["w_o"], dtype=np.float32),
        "w_glu": np.ascontiguousarray(inp["w_glu"], dtype=np.float32),
        "w_ffn_in": np.ascontiguousarray(inp["w_ffn_in"], dtype=np.float32),
        "w_ffn_out": np.ascontiguousarray(inp["w_ffn_out"], dtype=np.float32),
        "w_s": np.ascontiguousarray(inp["w_s"], dtype=np.float32),
        "b_s": np.ascontiguousarray(inp["b_s"].reshape(DEPTH, 768), dtype=np.float32),
        "smallp": sp,
        "rel_bias": np.ascontiguousarray(inp["rel_bias"], dtype=np.float32),
        "constf": cf,
        "onehot": oh,
    }
    maps = []
    for b in cores:
        m = dict(shared)
        m["xT"] = np.ascontiguousarray(inp["x"][b].T, dtype=np.float32)
        maps.append(m)
    return maps


def kernel(**inputs):
    inp = {k: np.asarray(v) for k, v in inputs.items()}
    if "full" not in _PROG:
        _PROG["full"] = Prog()
    prog = _PROG["full"]
    maps = make_in_maps(inp, list(range(8)))
    res = run_bass_kernel_spmd(prog.nc, maps, core_ids=list(range(8)))
    out = np.stack([np.ascontiguousarray(res.results[b]["outT"].T) for b in range(8)], axis=0)
    return out.astype(np.float32)
```

```python
import math
import os
from contextlib import ExitStack

import numpy as np
import ml_dtypes

import concourse.bass as bass
import concourse.mybir as mybir
from concourse.bass_utils import run_bass_kernel_spmd

F32 = mybir.dt.float32
F32R = mybir.dt.float32r
BF16 = mybir.dt.bfloat16
I32 = mybir.dt.int32
AF = mybir.ActivationFunctionType
ALU = mybir.AluOpType

S = 4096
D = 2048
TT = 512
NTT = S // TT
DEPTH = 4
NCC = 102
Q_OFF, K_OFF, V_OFF, U_OFF, VG_OFF, UC_OFF, GL_OFF = 0, 1536, 3072, 4608, 5376, 6144, 6912
DFF = 5632
ALPHA = (2 * DEPTH) ** 0.25
DILS = (1, 4, 16)
EW = 383
NEG = -30000.0
TWO_PI = 2.0 * math.pi

O_BA, O_SG, O_SB, O_LR, O_LI, O_LD = 0, 102, 108, 114, 138, 162
O_BRE, O_BIM, O_CRE, O_CIM = 186, 570, 954, 1338
O_DSK, O_BGLU, O_L1G, O_L1B, O_L2G, O_L2B = 1722, 1728, 1734, 1750, 1766, 1782
NSP = 1798

SAME_ENGINE_SYNC = bool(int(os.environ.get("K_SES", "1")))


class Res:
    __slots__ = ("name", "w", "r")

    def __init__(self, name):
        self.name = name
        self.w = {}
        self.r = {}


class KB:
    def __init__(self, nc, es):
        self.nc = nc
        self.es = es
        self.eng = {"pe": nc.tensor, "act": nc.scalar, "dve": nc.vector, "pool": nc.gpsimd, "sp": nc.sync}
        self.sem = {}
        self.cnt = {}
        self.waited = {}
        self.resd = {}
        for e in ("pe", "act", "dve", "pool"):
            self._mksem(e)

    def _mksem(self, key):
        if key not in self.sem:
            self.sem[key] = self.es.enter_context(self.nc.semaphore("s_" + key))
            self.cnt[key] = 0
        return self.sem[key]

    def R(self, *key):
        r = self.resd.get(key)
        if r is None:
            r = Res(str(key))
            self.resd[key] = r
        return r

    def _wait(self, e, evs):
        for key, val in evs.items():
            if key == e and not SAME_ENGINE_SYNC:
                continue
            if key == "pe" and e == "pe":
                continue
            if self.waited.get((e, key), 0) >= val:
                continue
            self.eng[e].wait_ge(self.sem[key], val)
            self.waited[(e, key)] = val

    @staticmethod
    def _merge(d, s):
        for k, v in s.items():
            if d.get(k, 0) < v:
                d[k] = v

    def _deps(self, reads, writes):
        evs = {}
        for r in reads:
            self._merge(evs, r.w)
        for w in writes:
            self._merge(evs, w.w)
            self._merge(evs, w.r)
        return evs

    def _commit(self, ev, reads, writes):
        k, v = ev
        for r in reads:
            if r.r.get(k, 0) < v:
                r.r[k] = v
        for w in writes:
            w.w = {k: v}
            w.r = {}

    def op(self, e, fn, reads=(), writes=()):
        self._wait(e, self._deps(reads, writes))
        ins = fn()
        self.cnt[e] += 1
        ins.then_inc(self.sem[e], 1)
        self._commit((e, self.cnt[e]), reads, writes)

    def mm(self, out, pairs, reads, writes, transpose=False):
        self._wait("pe", self._deps(reads, writes))
        n = len(pairs)
        ins = None
        for i, (a, b) in enumerate(pairs):
            ins = self.nc.tensor.matmul(out, lhsT=a, rhs=b, start=(i == 0), stop=(i == n - 1))
        self.cnt["pe"] += 1
        ins.then_inc(self.sem["pe"], 1)
        self._commit(("pe", self.cnt["pe"]), reads, writes)

    def tr(self, out, in_, ident, reads, writes):
        self._wait("pe", self._deps(reads, writes))
        ins = self.nc.tensor.transpose(out, in_, ident)
        self.cnt["pe"] += 1
        ins.then_inc(self.sem["pe"], 1)
        self._commit(("pe", self.cnt["pe"]), reads, writes)

    def dma(self, q, out, in_, reads, writes, semkey, **kw):
        self._mksem(semkey)
        self._wait(q, self._deps(reads, writes))
        ins = self.eng[q].dma_start(out=out, in_=in_, **kw)
        self.cnt[semkey] += 16
        ins.then_inc(self.sem[semkey], 16)
        self._commit((semkey, self.cnt[semkey]), reads, writes)

    def barrier(self):
        evs = {k: v for k, v in self.cnt.items() if v > 0}
        for e in ("pe", "act", "dve", "pool", "sp"):
            for key, val in evs.items():
                if self.waited.get((e, key), 0) >= val:
                    continue
                self.eng[e].wait_ge(self.sem[key], val)
                self.waited[(e, key)] = val


def bc_mid(ap, n):
    a = ap.ap
    return bass.AP(tensor=ap.tensor, offset=ap.offset, ap=[list(a[0]), [0, n], list(a[1])])


def bc_last(ap, n):
    a = ap.ap
    return bass.AP(tensor=ap.tensor, offset=ap.offset, ap=[list(a[0]), list(a[1]), [0, n]])


class WStream:
    def __init__(self, kb, es, nslots, slot_elems):
        self.kb = kb
        self.n = nslots
        self.tiles = [es.enter_context(kb.nc.sbuf_tensor("wsl%d" % i, [128, slot_elems], BF16)) for i in range(nslots)]
        self.res = [Res("wsl%d" % i) for i in range(nslots)]
        self.sched = []
        self.issued = 0
        self.consumed = 0

    def plan(self, tag, dram_ap, nelem, dres):
        self.sched.append((tag, dram_ap, nelem, dres))

    def get(self, tag):
        i = self.consumed
        assert self.sched[i][0] == tag, (self.sched[i][0], tag)
        while self.issued < min(len(self.sched), i + self.n - 1):
            k = self.issued
            _, dap, ne, dres = self.sched[k]
            sl = k % self.n
            self.kb.dma("sp", self.tiles[sl][:, 0:ne], dap, reads=dres, writes=[self.res[sl]], semkey="wsl%d" % sl)
            self.issued += 1
        self.consumed += 1
        sl = i % self.n
        return self.tiles[sl], self.res[sl]


class Prog:
    def __init__(self, nlayers=DEPTH, phases="ABCDEF", debug=False):
        self.nlayers = nlayers
        self.phases = phases
        self.debug = debug
        self.nc = bass.Bass("TRN2", target_bir_lowering=False)
        self.build()

    def dram(self, name, shape, dt, kind):
        return self.nc.dram_tensor(name, list(shape), dt, kind=kind)

    def build(self):
        nc = self.nc
        dk = "ExternalOutput" if self.debug else "Internal"
        self.d_xT = self.dram("xT", [D, S], F32, "ExternalInput")
        self.d_w_in = self.dram("w_in", [DEPTH, D, NCC * 128], F32, "ExternalInput")
        self.d_w_pa = self.dram("w_pa", [DEPTH, 512, D], F32, "ExternalInput")
        self.d_w_pb = self.dram("w_pb", [DEPTH, 768, D], F32, "ExternalInput")
        self.d_w_pc = self.dram("w_pc", [DEPTH, 768, D], F32, "ExternalInput")
        self.d_w_o = self.dram("w_o", [DEPTH, D, D], F32, "ExternalInput")
        self.d_w_glu = self.dram("w_glu", [DEPTH, 768, 768], F32, "ExternalInput")
        self.d_w_f1 = self.dram("w_ffn_in", [DEPTH, D, 2 * DFF], F32, "ExternalInput")
        self.d_w_f2 = self.dram("w_ffn_out", [DEPTH, DFF, D], F32, "ExternalInput")
        self.d_w_s = self.dram("w_s", [DEPTH, 6, 128, 128], F32, "ExternalInput")
        self.d_b_s = self.dram("b_s", [DEPTH, 768], F32, "ExternalInput")
        self.d_sp = self.dram("smallp", [DEPTH, 128, NSP], F32, "ExternalInput")
        self.d_relb = self.dram("rel_bias", [32, 24], F32, "ExternalInput")
        self.d_cf = self.dram("constf", [128, 1664], F32, "ExternalInput")
        self.d_oh = self.dram("onehot", [3, 33, EW], F32, "ExternalInput")
        self.d_out = self.dram("outT", [D, S], F32, "ExternalOutput")
        self.d_P = self.dram("P", [NCC * 128, S], BF16, dk)
        self.d_YA = self.dram("YA", [512, S], BF16, dk)
        self.d_YB = self.dram("YB", [768, S], BF16, dk)
        self.d_YC0 = self.dram("YC0", [768, S], BF16, dk)
        self.d_XT = [self.dram("XTa", [D, S], F32, dk), self.dram("XTb", [D, S], F32, dk)]
        self.d_E = self.dram("Ed", [24, EW], F32, "Internal")
        self.d_Z = self.dram("Zd", [24, 128 * EW], F32, "Internal")
        self.d_WBin = [self.dram("WBin%d" % i, [NCC, 128, 2048], BF16, "Internal") for i in range(2)]
        self.d_WBm = [self.dram("WBm%d" % i, [16, 128, 2048], BF16, "Internal") for i in range(2)]
        self.d_WBo = [self.dram("WBo%d" % i, [16, 128, 2048], BF16, "Internal") for i in range(2)]
        self.d_WBf1 = [self.dram("WBf1%d" % i, [88, 128, 2048], BF16, "Internal") for i in range(2)]
        self.d_WBf2 = [self.dram("WBf2%d" % i, [32, 128, 2816], BF16, "Internal") for i in range(2)]
        self.d_WBg = [self.dram("WBg%d" % i, [2, 128, 2304], BF16, "Internal") for i in range(2)]

        with ExitStack() as es:
            self.es = es
            kb = self.kb = KB(nc, es)
            blk = es.enter_context(nc.Block())

            @blk.sync
            def _(sync):
                self.emit()

    def sb(self, es, name, shape, dt):
        self._uid = getattr(self, "_uid", 0) + 1
        t = es.enter_context(self.nc.sbuf_tensor("%s_%d" % (name, self._uid), list(shape), dt))
        sz = int(np.prod(shape[1:])) * (2 if dt == BF16 else 4)
        self._cur = getattr(self, "_cur", 0) + sz
        self._peak = max(getattr(self, "_peak", 0), self._cur)
        if os.environ.get("K_MEM"):
            print("SB alloc", name, sz, "cur", self._cur)

        def _free():
            self._cur -= sz
        es.callback(_free)
        return t

    def psum(self):
        i = self.ps_i % 8
        self.ps_i += 1
        return self.ps_tiles[i], self.ps_res[i]

    def V(self, fn, reads=(), writes=()):
        self.kb.op("dve", fn, reads, writes)

    def A(self, fn, reads=(), writes=()):
        self.kb.op("act", fn, reads, writes)

    def emit(self):
        nc, kb, es = self.nc, self.kb, self.es
        self.ps_tiles = [es.enter_context(nc.psum_tensor("ps%d" % i, [128, 512], F32)) for i in range(8)]
        self.ps_res = [Res("ps%d" % i) for i in range(8)]
        self.ps_i = 0
        self.ws = WStream(kb, es, 6, 2816)
        self.cf = self.sb(es, "cf", [128, 1664], F32)
        self.r_cf = Res("cf")
        kb.dma("sp", self.cf[:], self.d_cf.ap(), [], [self.r_cf], "cst")
        self.identF = self.cf[:, 0:128]
        self.tril = self.cf[:, 128:256]
        self.iota = self.cf[:, 256:768]
        self.sel = self.cf[:, 768:896]
        self.onesF = self.cf[:, 1024:1152]
        self.onesR = self.cf[:, 1024:1152].bitcast(F32R)
        self.identB = self.sb(es, "identB", [128, 128], BF16)
        self.onesB = self.sb(es, "onesB", [128, 128], BF16)
        self.r_ib = Res("identB")
        self.V(lambda: nc.vector.tensor_copy(self.identB[:], self.identF), [self.r_cf], [self.r_ib])
        self.V(lambda: nc.vector.tensor_copy(self.onesB[:], self.onesF), [self.r_cf], [self.r_ib])
        self.sp = self.sb(es, "sp", [128, NSP], F32)
        self.r_sp = Res("sp")

        self.cast_q = []
        self.plan_weights()
        if "B" in self.phases:
            self.bias_setup()
        self.cast_weights(0)
        self.pump_casts(9)
        for l in range(self.nlayers):
            self.l = l
            self.par = l % 2
            self.x_in = self.d_xT if l == 0 else self.d_XT[(l - 1) % 2]
            self.x_out = self.d_out if l == self.nlayers - 1 else self.d_XT[l % 2]
            kb.dma("sp", self.sp[:], self.d_sp.ap()[l], [], [self.r_sp], "spl")
            if l + 1 < self.nlayers and l > 0:
                self.cast_weights(l + 1)
                if "E" not in self.phases:
                    self.pump_casts()
            if "A" in self.phases:
                ag = self.phase_A_gen()
                for _ in range(self.NQ * 6):
                    next(ag)
                    if l == 0:
                        self.pump_casts(4)
                self.a_emitted = self.NQ * 6
                self.a_done = False
                if "D" in self.phases:
                    self.phase_D(ag)
                while self.a_emitted < self.NQ * (6 + 48):
                    self.pumpA(ag)
                kb.barrier()
                if "B" in self.phases:
                    self.phase_B(ag)
                    kb.barrier()
                if "C" in self.phases:
                    self.phase_C(ag)
                for _ in ag:
                    pass
                kb.barrier()
            if l == 0:
                self.pump_casts()
                if l + 1 < self.nlayers:
                    self.cast_weights(l + 1)
            if "E" in self.phases:
                self.phase_EF()
                self.pump_casts()
                kb.barrier()
        kb.barrier()

    def cast_weights(self, l):
        kb = self.kb
        par = l % 2
        q = self.cast_q

        def cast(dst_t, tile, src_t, src_off, row_stride, kch, ncol_tile, resname, kofs=0):
            dst_elems = dst_t.ap().shape[2]
            src = bass.AP(tensor=src_t, offset=src_off, ap=[[row_stride, 128], [128 * row_stride, kch], [1, ncol_tile]])
            dst = bass.AP(tensor=dst_t, offset=tile * 128 * dst_elems + kofs * ncol_tile,
                          ap=[[dst_elems, 128], [ncol_tile, kch], [1, ncol_tile]])
            q.append((dst, src, (resname, par), "cast_%s_%d" % (resname, par)))

        seen = set()
        for (q_, cc) in self.a_order():
            if cc in seen:
                continue
            seen.add(cc)
            cast(self.d_WBin[par], cc, self.d_w_in, l * D * 13056 + cc * 128, 13056, 16, 128, "win%d" % self.win_group(cc))
        for h in range(2):
            cast(self.d_WBg[par], h, self.d_w_glu, l * 768 * 768 + h * 3 * 128 * 768, 768, 3, 768, "wglu")
        for dc in range(16):
            cast(self.d_WBm[par], dc, self.d_w_pa, l * 512 * D + dc * 128, D, 4, 128, "wm", kofs=0)
            cast(self.d_WBm[par], dc, self.d_w_pb, l * 768 * D + dc * 128, D, 6, 128, "wm", kofs=4)
            cast(self.d_WBm[par], dc, self.d_w_pc, l * 768 * D + dc * 128, D, 6, 128, "wm", kofs=10)
        for dc in range(16):
            cast(self.d_WBo[par], dc, self.d_w_o, l * D * D + dc * 128, D, 16, 128, "wo")
        for j in range(88):
            cast(self.d_WBf1[par], j, self.d_w_f1, l * D * 2 * DFF + j * 128, 2 * DFF, 16, 128, "wf1")
        for dc in range(16):
            for h in range(2):
                cast(self.d_WBf2[par], dc * 2 + h, self.d_w_f2, l * DFF * D + h * 22 * 128 * D + dc * 128, D, 22, 128, "wf2")

    def win_group(self, cc):
        if not hasattr(self, "_wing"):
            order = []
            for (q_, c) in self.a_order():
                if c not in order:
                    order.append(c)
            self._wing = {c: i // 9 for i, c in enumerate(order)}
        return self._wing[cc]

    def pump_casts(self, n=None):
        kb = self.kb
        while self.cast_q and (n is None or n > 0):
            dst, src, rkey, sem = self.cast_q.pop(0)
            kb.dma("pool", dst, src, [], [kb.R(*rkey)], sem)
            if n is not None:
                n -= 1

    def plan_weights(self):
        kb = self.kb
        ws = self.ws
        for l in range(self.nlayers):
            par = l % 2
            if "A" in self.phases:
                for (q, cc) in self.a_order():
                    ws.plan(("A", l, q, cc), self.d_WBin[par].ap()[cc], 2048, [kb.R("win%d" % self.win_group(cc), par)])
            if "E" in self.phases:
                for tt in range(NTT):
                    for h in range(2):
                        ws.plan(("G", l, tt, h), self.d_WBg[par].ap()[h], 2304, [kb.R("wglu", par)])
                    for dc in range(16):
                        ws.plan(("M", l, tt, dc), self.d_WBm[par].ap()[dc], 2048, [kb.R("wm", par)])
                    for dc in range(16):
                        ws.plan(("O", l, tt, dc), self.d_WBo[par].ap()[dc], 2048, [kb.R("wo", par)])
                    for j in range(44):
                        ws.plan(("F1g", l, tt, j), self.d_WBf1[par].ap()[j], 2048, [kb.R("wf1", par)])
                        ws.plan(("F1u", l, tt, j), self.d_WBf1[par].ap()[44 + j], 2048, [kb.R("wf1", par)])
                    for dc in range(16):
                        for h in range(2):
                            ws.plan(("F2", l, tt, dc, h), self.d_WBf2[par].ap()[dc * 2 + h], 2816, [kb.R("wf2", par)])

    NQ = 4

    def a_groups(self):
        ucs = list(range(UC_OFF // 128, UC_OFF // 128 + 6))
        mix = list(range(0, UC_OFF // 128))
        gates = list(range(GL_OFF // 128, NCC))
        return [ucs, mix, gates]

    def a_order(self):
        out = []
        for gi, grp in enumerate(self.a_groups()):
            out += [(q, cc) for q in range(self.NQ) for cc in grp]
        return out

    def pumpA(self, ag, n=1):
        for _ in range(n):
            if self.a_done:
                return
            if next(ag) == "hold":
                self.a_done = True
            else:
                self.a_emitted += 1

    def phase_A_gen(self):
        nc, kb, l = self.nc, self.kb, self.l
        QS = S // self.NQ
        with ExitStack() as es:
            xb = self.sb(es, "xb", [128, 16, QS], BF16)
            r_xb = Res("xb")
            oA = [self.sb(es, "oA%d" % i, [128, QS], BF16) for i in range(2)]
            r_oA = [Res("oA%d" % i) for i in range(2)]
            xin = self.x_in.ap().rearrange("(k p) s -> p k s", p=128)
            cur = None
            it = 0
            first_rest = True
            for (q, cc) in self.a_order():
                gid = 0 if UC_OFF // 128 <= cc < UC_OFF // 128 + 6 else (1 if cc < UC_OFF // 128 else 2)
                if cur != (q, gid):
                    cur = (q, gid)
                    for k4 in range(4):
                        kb.dma("pool", xb[:, k4 * 4:(k4 + 1) * 4, :], xin[:, k4 * 4:(k4 + 1) * 4, q * QS:(q + 1) * QS],
                               [kb.R("X", l, t) for t in range(NTT)], [r_xb], "xbld")
                wt, r_w = self.ws.get(("A", l, q, cc))
                w3 = wt[:, 0:2048].rearrange("p (k c) -> p k c", c=128)
                if cc < 36 or 48 <= cc < 54:
                    fn = AF.Identity
                elif cc < 48:
                    fn = AF.Gelu_apprx_tanh
                else:
                    fn = AF.Sigmoid
                sl = it % 2
                it += 1
                for t in range(QS // TT):
                    ps, r_ps = self.psum()
                    kb.mm(ps[:], [(w3[:, k, :], xb[:, k, t * TT:(t + 1) * TT]) for k in range(16)],
                          [r_w, r_xb], [r_ps])
                    self.A(lambda: nc.scalar.activation(out=oA[sl][:, t * TT:(t + 1) * TT], in_=ps[:], func=fn,
                                                        bias=self.sp[:, O_BA + cc:O_BA + cc + 1], scale=1.0),
                           [r_ps, self.r_sp], [r_oA[sl]])
                kb.dma("pool", self.d_P.ap()[cc * 128:(cc + 1) * 128, q * QS:(q + 1) * QS], oA[sl][:],
                       [r_oA[sl]], [kb.R("P", cc)], "oAst%d" % sl)
                yield
            yield "hold"

    def bias_setup(self):
        nc, kb = self.nc, self.kb
        with ExitStack() as es:
            relb = self.sb(es, "relb", [33, 24], F32)
            oh = self.sb(es, "oh", [33, 3, EW], F32)
            eo = self.sb(es, "eo", [8, 3, EW], F32)
            r1, r2, r3 = Res("relb"), Res("oh"), Res("eo")
            self.V(lambda: nc.vector.memset(relb[:], 1.0), [], [r1])
            kb.dma("sp", relb[0:32, :], self.d_relb.ap(), [], [r1], "bs1")
            kb.dma("sp", oh[:], self.d_oh.ap().rearrange("g b s -> b g s"), [], [r2], "bs2")
            for g in range(3):
                ps, r_ps = self.psum()
                kb.mm(ps[0:8, 0:EW], [(relb[:, g * 8:(g + 1) * 8], oh[:, g, :])], [r1, r2], [r_ps])
                self.V(lambda: nc.vector.tensor_copy(eo[:, g, :], ps[0:8, 0:EW]), [r_ps], [r3])
                kb.dma("sp", self.d_E.ap()[g * 8:(g + 1) * 8, :], eo[:, g, :], [r3], [kb.R("E")], "bs3")
            src = bass.AP(tensor=self.d_E, offset=0, ap=[[EW, 24], [0, 128], [1, EW]])
            dst = bass.AP(tensor=self.d_Z, offset=0, ap=[[128 * EW, 24], [EW, 128], [1, EW]])
            kb.dma("sp", dst, src, [kb.R("E")], [kb.R("Z")], "bs4")
            kb.barrier()

    def phase_B(self, ag):
        nc, kb, l = self.nc, self.kb, self.l
        with ExitStack() as es:
            qkv = [self.sb(es, "qkv%d" % i, [64, 3, S], BF16) for i in range(2)]
            r_qkv = [Res("qkv%d" % i) for i in range(2)]
            va = [self.sb(es, "va%d" % i, [128, 32, 128], BF16) for i in range(2)]
            r_va = [Res("va%d" % i) for i in range(2)]
            bm = [self.sb(es, "bm%d" % i, [128, 2, 128], F32) for i in range(2)]
            r_bm = [Res("bm%d" % i) for i in range(2)]
            acc = [self.sb(es, "acc%d" % i, [128, S], F32) for i in range(1)] * 2
            r_acc = [Res("acc%d" % i) for i in range(1)] * 2
            tq = [self.sb(es, "tq%d" % i, [128, 2, 128], F32) for i in range(4)]
            r_tq = [Res("tq%d" % i) for i in range(4)]
            NPM = 12
            LAG = 9
            pm = [self.sb(es, "pm%d" % i, [128, 2, 128], BF16) for i in range(NPM)]
            r_pm = [Res("pm%d" % i) for i in range(NPM)]
            yo = [self.sb(es, "yo%d" % i, [64, S], BF16) for i in range(1)] * 2
            r_yo = [Res("yo%d" % i) for i in range(1)] * 2
            rd = [self.sb(es, "rd%d" % i, [64, TT], F32) for i in range(2)]
            r_rd = [Res("rd%d" % i) for i in range(2)]
            for i in range(2):
                self.V(lambda: nc.vector.memset(va[i][:, :, 64:128], 1.0), [], [r_va[i]])

            def load(it):
                hl, g = divmod(it, 3)
                head = g * 8 + hl
                sl = it % 2
                for j, off in enumerate((Q_OFF, K_OFF, V_OFF)):
                    row = off + head * 64
                    kb.dma("sp", qkv[sl][:, j, :], self.d_P.ap()[row:row + 64, :], [kb.R("P", row // 128)], [r_qkv[sl]],
                           "qkvld%d" % sl)
                src = bass.AP(tensor=self.d_Z, offset=head * 128 * EW + 127, ap=[[EW - 1, 128], [128, 2], [1, 128]])
                kb.dma("sp", bm[sl][:], src, [kb.R("Z")], [r_bm[sl]], "bmld%d" % sl)

            load(0)
            blk_i = 0
            for it in range(24):
                hl, g = divmod(it, 3)
                r = DILS[g]
                nb = 32 // r
                sl = it % 2
                if it + 1 < 24:
                    load(it + 1)
                q = qkv[sl]
                a = acc[hl % 2]
                r_a = r_acc[hl % 2]

                def tok(c, n):
                    st = 128 * n * r + c
                    return slice(st, st + 127 * r + 1, r)

                for b8 in range(4):
                    ps, r_ps = self.psum()
                    psb = ps[:].bitcast(BF16)
                    for j in range(8):
                        b = b8 * 8 + j
                        c, n = divmod(b, nb)
                        kb.tr(psb[:, j * 64:(j + 1) * 64], q[:, 2, tok(c, n)], self.identB[0:64, 0:64],
                              [r_qkv[sl], self.r_ib], [r_ps])
                    self.V(lambda: nc.vector.tensor_copy(va[sl][:, b8 * 8:(b8 + 1) * 8, 0:64],
                                                         psb[:, 0:512].rearrange("p (j d) -> p j d", d=64)),
                           [r_ps], [r_va[sl]])
                pend = []

                def emit_pv(b, pi):
                    c, n = divmod(b, nb)
                    ps2, r_ps2 = self.psum()
                    pairs = [(va[sl][:, b, :], pm[pi][:, 0, :])]
                    if n > 0:
                        pairs.append((va[sl][:, b - 1, :], pm[pi][:, 1, :]))
                    kb.mm(ps2[:, 0:128], pairs, [r_va[sl], r_pm[pi]], [r_ps2])
                    if g == 0:
                        self.V(lambda: nc.vector.tensor_copy(a[:, tok(c, n)], ps2[:, 0:128]), [r_ps2], [r_a])
                    else:
                        self.V(lambda: nc.vector.tensor_tensor(out=a[:, tok(c, n)], in0=a[:, tok(c, n)], in1=ps2[:, 0:128],
                                                               op=ALU.add), [r_ps2, r_a], [r_a])

                for b in range(32):
                    if blk_i % 20 == 0:
                        self.pumpA(ag, 2)
                    c, n = divmod(b, nb)
                    np_ = 1 if n == 0 else 2
                    ps, r_ps = self.psum()
                    sc = ps[:, 0:256].rearrange("p (a q) -> p a q", q=128)
                    kb.mm(sc[:, 0, :], [(q[:, 1, tok(c, n)], q[:, 0, tok(c, n)])], [r_qkv[sl]], [r_ps])
                    if n > 0:
                        kb.mm(sc[:, 1, :], [(q[:, 1, tok(c, n - 1)], q[:, 0, tok(c, n)])], [r_qkv[sl]], [r_ps])
                    ti = blk_i % 4
                    pi = blk_i % NPM
                    blk_i += 1
                    self.V(lambda: nc.vector.scalar_tensor_tensor(out=tq[ti][:, 0:np_, :], in0=sc[:, 0:np_, :], scalar=0.125,
                                                                  in1=bm[sl][:, 0:np_, :], op0=ALU.mult, op1=ALU.add),
                           [r_ps, r_bm[sl]], [r_tq[ti]])
                    self.A(lambda: nc.scalar.activation(out=pm[pi][:, 0:np_, :], in_=tq[ti][:, 0:np_, :], func=AF.Exp),
                           [r_tq[ti]], [r_pm[pi]])
                    pend.append((b, pi))
                    if len(pend) > LAG:
                        emit_pv(*pend.pop(0))
                while pend:
                    emit_pv(*pend.pop(0))
                if g == 2:
                    ysl = hl % 2
                    for t in range(NTT):
                        ps, r_ps = self.psum()
                        kb.mm(ps[:, :], [(self.sel, a[:, t * TT:(t + 1) * TT])], [self.r_cf, r_a], [r_ps])
                        di = t % 2
                        self.A(lambda: nc.scalar.activation(out=rd[di][:], in_=ps[0:64, :], func=AF.Ln), [r_ps], [r_rd[di]])
                        self.A(lambda: nc.scalar.activation(out=rd[di][:], in_=rd[di][:], func=AF.Exp, scale=-1.0), [r_rd[di]], [r_rd[di]])
                        self.V(lambda: nc.vector.tensor_tensor(out=yo[ysl][:, t * TT:(t + 1) * TT], in0=a[0:64, t * TT:(t + 1) * TT],
                                                               in1=rd[di][:], op=ALU.mult), [r_a, r_rd[di]], [r_yo[ysl]])
                    kb.dma("pool", self.d_YA.ap()[hl * 64:(hl + 1) * 64, :], yo[ysl][:], [r_yo[ysl]], [kb.R("YA", hl // 2)],
                           "yast")

    def ln_stats(self, es_tiles, src_chunks, r_src, nfeat, bf_chunks=None, r_bf=None, sq_chunks=None, r_sqc=None):
        nc, kb = self.nc, self.kb
        sq, r_sq, st, r_st = es_tiles
        ps1, r_ps1 = self.psum()
        kb.mm(ps1[:], [(self.onesB[:], c) for c in bf_chunks], [self.r_ib] + r_bf, [r_ps1])
        ps2, r_ps2 = self.psum()
        n = len(src_chunks)
        if sq_chunks is not None:
            kb.mm(ps2[:], [(self.onesB[:], c) for c in sq_chunks], [self.r_ib] + r_sqc, [r_ps2])
            src_chunks = []
        else:
            kb._wait("pe", kb._deps([self.r_ib], [r_ps2]))
        for i, c in enumerate(src_chunks):
            k = i % len(sq)
            self.A(lambda: nc.scalar.activation(out=sq[k][:], in_=c, func=AF.Square), r_src, [r_sq[k]])
            kb._wait("pe", kb._deps([r_sq[k]], []))
            ins = nc.tensor.matmul(ps2[:], lhsT=self.onesB[:], rhs=sq[k][:], start=(i == 0), stop=(i == n - 1))
            kb.cnt["pe"] += 1
            ins.then_inc(kb.sem["pe"], 1)
            kb._commit(("pe", kb.cnt["pe"]), [r_sq[k]], [r_ps2] if i == n - 1 else [])
        mean, msq, rstd, mr = st
        inv = 1.0 / nfeat
        self.V(lambda: nc.vector.tensor_scalar(out=mean[:], in0=ps1[:], scalar1=inv, scalar2=None, op0=ALU.mult), [r_ps1], [r_st[0]])
        self.V(lambda: nc.vector.tensor_tensor(out=msq[:], in0=mean[:], in1=mean[:], op=ALU.mult), [r_st[0]], [r_st[1]])
        self.V(lambda: nc.vector.scalar_tensor_tensor(out=msq[:], in0=ps2[:], scalar=inv, in1=msq[:], op0=ALU.mult, op1=ALU.subtract),
               [r_ps2, r_st[1]], [r_st[1]])
        self.V(lambda: nc.vector.tensor_scalar(out=msq[:], in0=msq[:], scalar1=1e-5, scalar2=None, op0=ALU.add), [r_st[1]], [r_st[1]])
        self.A(lambda: nc.scalar.activation(out=msq[:], in_=msq[:], func=AF.Sqrt), [r_st[1]], [r_st[1]])
        self.V(lambda: nc.vector.reciprocal(out=rstd[:], in_=msq[:]), [r_st[1]], [r_st[2]])
        self.V(lambda: nc.vector.tensor_tensor(out=mr[:], in0=mean[:], in1=rstd[:], op=ALU.mult), [r_st[0], r_st[2]], [r_st[3]])
        return rstd, r_st[2], mr, r_st[3]

    def ln_tiles(self, es, pfx):
        sq = [self.sb(es, pfx + "sq%d" % i, [128, TT], BF16) for i in range(4)]
        r_sq = [Res(pfx + "sq%d" % i) for i in range(4)]
        st = [self.sb(es, pfx + "st%d" % i, [128, TT], F32) for i in range(4)]
        r_st = [Res(pfx + "st%d" % i) for i in range(4)]
        return sq, r_sq, st, r_st

    def phase_C(self, ag):
        nc, kb, l = self.nc, self.kb, self.l
        with ExitStack() as es:
            wsl = self.sb(es, "wsl", [128, 6, 128], F32)
            wsm = self.sb(es, "wsm", [128, 6, 128], BF16)
            wsT = self.sb(es, "wsT", [128, 6, 128], BF16)
            bsb = self.sb(es, "bsb", [128, 768], F32)
            r_wsl, r_wsm, r_wsT, r_bsb = Res("wsl"), Res("wsm"), Res("wsT"), Res("bsb")
            kb.dma("sp", wsl[:], self.d_w_s.ap()[l].rearrange("g t s -> t g s"), [], [r_wsl], "cws")
            kb.dma("sp", bsb[:], self.d_b_s.ap()[l].partition_broadcast(128), [], [r_bsb], "cbs")
            self.V(lambda: nc.vector.tensor_tensor(out=wsm[:], in0=wsl[:], in1=bc_mid(self.tril, 6), op=ALU.mult),
                   [r_wsl, self.r_cf], [r_wsm])
            ps, r_ps = self.psum()
            psb = ps[:].bitcast(BF16)
            for g in range(6):
                kb.tr(psb[:, g * 128:(g + 1) * 128], wsm[:, g, :], self.identB[:], [r_wsm, self.r_ib], [r_ps])
            self.V(lambda: nc.vector.tensor_copy(wsT[:], psb[:, 0:768].rearrange("p (g t) -> p g t", t=128)), [r_ps], [r_wsT])

            uv = [self.sb(es, "uv%d" % i, [128, 12, TT], BF16) for i in range(2)]
            r_uv = [Res("uv%d" % i) for i in range(2)]
            vf = self.sb(es, "vf", [128, 6, TT], F32)
            r_vf = Res("vf")
            vn = self.sb(es, "vn", [128, 6, TT], BF16)
            r_vn = Res("vn")
            vnT = self.sb(es, "vnT", [128, 6, 4, 128], BF16)
            r_vnT = Res("vnT")
            tmp = [self.sb(es, "ctmp%d" % i, [128, TT], F32) for i in range(2)]
            r_tmp = [Res("ctmp%d" % i) for i in range(2)]
            yb = [self.sb(es, "ybo%d" % i, [128, 6, TT], BF16) for i in range(2)]
            r_yb = [Res("ybo%d" % i) for i in range(2)]
            lnt = self.ln_tiles(es, "c")

            def load(tt):
                sl = tt % 2
                src = self.d_P.ap()[U_OFF:U_OFF + 1536, tt * TT:(tt + 1) * TT].rearrange("(c p) s -> p c s", p=128)
                kb.dma("sp", uv[sl][:], src, [kb.R("P", U_OFF // 128 + c) for c in range(12)], [r_uv[sl]], "uvld%d" % sl)

            load(0)
            for tt in range(NTT):
                sl = tt % 2
                if tt + 1 < NTT:
                    load(tt + 1)
                self.pumpA(ag, 3)
                self.V(lambda: nc.vector.tensor_copy(vf[:], uv[sl][:, 6:12, :]), [r_uv[sl]], [r_vf])
                rstd, r_rstd, mr, r_mr = self.ln_stats(lnt, [vf[:, c, :] for c in range(6)], [r_vf], 768.0,
                                                       [uv[sl][:, 6 + c, :] for c in range(6)], [r_uv[sl]])
                for c in range(6):
                    k = c % 2
                    self.V(lambda: nc.vector.tensor_tensor(out=tmp[k][:], in0=vf[:, c, :], in1=rstd[:], op=ALU.mult),
                           [r_vf, r_rstd], [r_tmp[k]])
                    self.V(lambda: nc.vector.tensor_tensor(out=tmp[k][:], in0=tmp[k][:], in1=mr[:], op=ALU.subtract),
                           [r_tmp[k], r_mr], [r_tmp[k]])
                    self.A(lambda: nc.scalar.activation(out=vn[:, c, :], in_=tmp[k][:], func=AF.Identity,
                                                        scale=self.sp[:, O_SG + c:O_SG + c + 1], bias=self.sp[:, O_SB + c:O_SB + c + 1]),
                           [r_tmp[k], self.r_sp], [r_vn])
                for c in range(6):
                    ps, r_ps = self.psum()
                    psb = ps[:].bitcast(BF16)
                    for j in range(4):
                        kb.tr(psb[:, j * 128:(j + 1) * 128], vn[:, c, j * 128:(j + 1) * 128], self.identB[:], [r_vn, self.r_ib], [r_ps])
                    self.V(lambda: nc.vector.tensor_copy(vnT[:, c, :, :], psb[:, 0:512].rearrange("p (j d) -> p j d", d=128)),
                           [r_ps], [r_vnT])
                for c in range(6):
                    ps, r_ps = self.psum()
                    for j in range(4):
                        kb.mm(ps[:, j * 128:(j + 1) * 128], [(vnT[:, c, j, :], wsT[:, c, :])], [r_vnT, r_wsT], [r_ps])
                    k = c % 2
                    self.V(lambda: nc.vector.tensor_tensor(out=tmp[k][:].rearrange("p (j t) -> p j t", t=128),
                                                           in0=ps[:].rearrange("p (j t) -> p j t", t=128),
                                                           in1=bc_mid(bsb[:, c * 128:(c + 1) * 128], 4), op=ALU.add),
                           [r_ps, r_bsb], [r_tmp[k]])
                    self.V(lambda: nc.vector.tensor_tensor(out=yb[sl][:, c, :], in0=tmp[k][:], in1=uv[sl][:, c, :], op=ALU.mult),
                           [r_tmp[k], r_uv[sl]], [r_yb[sl]])
                dst = self.d_YB.ap()[:, tt * TT:(tt + 1) * TT].rearrange("(c p) s -> p c s", p=128)
                kb.dma("pool", dst, yb[sl][:], [r_yb[sl]], [kb.R("YB", tt)], "ybst%d" % sl)

    def range_reduce(self, x, r_x, tmpf, tmpi, r_t, shape_ap=None):
        nc = self.nc
        C1 = 6.28125
        C2 = TWO_PI - C1
        self.V(lambda: nc.vector.tensor_scalar(out=tmpf, in0=x, scalar1=1.0 / TWO_PI, scalar2=None, op0=ALU.mult), [r_x], [r_t])
        self.V(lambda: nc.vector.tensor_copy(tmpi, tmpf), [r_t], [r_t])
        self.V(lambda: nc.vector.tensor_copy(tmpf, tmpi), [r_t], [r_t])
        self.V(lambda: nc.vector.scalar_tensor_tensor(out=x, in0=tmpf, scalar=-C1, in1=x, op0=ALU.mult, op1=ALU.add), [r_t, r_x], [r_x])
        self.V(lambda: nc.vector.scalar_tensor_tensor(out=x, in0=tmpf, scalar=-C2, in1=x, op0=ALU.mult, op1=ALU.add), [r_t, r_x], [r_x])
        self.V(lambda: nc.vector.tensor_scalar(out=tmpf, in0=x, scalar1=math.pi, scalar2=-TWO_PI, op0=ALU.is_gt, op1=ALU.mult), [r_x], [r_t])
        self.V(lambda: nc.vector.tensor_tensor(out=x, in0=x, in1=tmpf, op=ALU.add), [r_t, r_x], [r_x])
        self.V(lambda: nc.vector.tensor_scalar(out=tmpf, in0=x, scalar1=-math.pi, scalar2=TWO_PI, op0=ALU.is_lt, op1=ALU.mult), [r_x], [r_t])
        self.V(lambda: nc.vector.tensor_tensor(out=x, in0=x, in1=tmpf, op=ALU.add), [r_t, r_x], [r_x])
        self.V(lambda: nc.vector.tensor_scalar(out=x, in0=x, scalar1=math.pi, scalar2=-math.pi, op0=ALU.min, op1=ALU.max), [r_x], [r_x])

    def phase_D(self, ag):
        nc, kb, l = self.nc, self.kb, self.l
        sp = self.sp
        with ExitStack() as es:
            NS = 24
            pp = self.sb(es, "s5p", [128, 20, NS], F32)
            ppi = self.sb(es, "s5pi", [128, 2, NS], I32)
            r_pp = Res("s5p")
            lr, li, ld = sp[:, O_LR:O_LR + NS], sp[:, O_LI:O_LI + NS], sp[:, O_LD:O_LD + NS]
            (DT, MAG, TH, SN, CS, T0, T1, SH, EM1, ABI, AR1, INV, CR, CI, TH2, C5, S5, T2, T3, T4) = [pp[:, i, :] for i in range(20)]
            RS = [self.r_sp, r_pp]

            def v(fn):
                self.V(fn, RS, [r_pp])

            def a(fn):
                self.A(fn, RS, [r_pp])

            a(lambda: nc.scalar.activation(out=DT, in_=ld, func=AF.Exp))
            v(lambda: nc.vector.tensor_tensor(out=T0, in0=lr, in1=DT, op=ALU.mult))
            a(lambda: nc.scalar.activation(out=MAG, in_=T0, func=AF.Exp))
            v(lambda: nc.vector.tensor_scalar(out=EM1, in0=T0, scalar1=1.0 / 6.0, scalar2=1.0, op0=ALU.mult, op1=ALU.add))
            for dv in (5.0, 4.0, 3.0, 2.0):
                v(lambda: nc.vector.tensor_tensor(out=EM1, in0=EM1, in1=T0, op=ALU.mult))
                v(lambda: nc.vector.tensor_scalar(out=EM1, in0=EM1, scalar1=1.0 / dv, scalar2=1.0, op0=ALU.mult, op1=ALU.add))
            v(lambda: nc.vector.tensor_tensor(out=EM1, in0=EM1, in1=T0, op=ALU.mult))
            v(lambda: nc.vector.tensor_tensor(out=TH, in0=li, in1=DT, op=ALU.mult))
            v(lambda: nc.vector.tensor_copy(T1, TH))
            self.range_reduce(T1, r_pp, T2, ppi[:, 0, :], r_pp)
            a(lambda: nc.scalar.activation(out=SN, in_=T1, func=AF.Sin))
            v(lambda: nc.vector.tensor_scalar(out=T1, in0=TH, scalar1=math.pi / 2, scalar2=None, op0=ALU.add))
            self.range_reduce(T1, r_pp, T2, ppi[:, 0, :], r_pp)
            a(lambda: nc.scalar.activation(out=CS, in_=T1, func=AF.Sin))
            v(lambda: nc.vector.tensor_scalar(out=T1, in0=TH, scalar1=0.5, scalar2=None, op0=ALU.mult))
            self.range_reduce(T1, r_pp, T2, ppi[:, 0, :], r_pp)
            a(lambda: nc.scalar.activation(out=SH, in_=T1, func=AF.Sin))
            v(lambda: nc.vector.tensor_tensor(out=ABI, in0=MAG, in1=SN, op=ALU.mult))
            v(lambda: nc.vector.tensor_tensor(out=AR1, in0=EM1, in1=CS, op=ALU.mult))
            v(lambda: nc.vector.tensor_tensor(out=T1, in0=SH, in1=SH, op=ALU.mult))
            v(lambda: nc.vector.scalar_tensor_tensor(out=AR1, in0=T1, scalar=-2.0, in1=AR1, op0=ALU.mult, op1=ALU.add))
            v(lambda: nc.vector.tensor_tensor(out=T1, in0=lr, in1=lr, op=ALU.mult))
            v(lambda: nc.vector.tensor_tensor(out=T2, in0=li, in1=li, op=ALU.mult))
            v(lambda: nc.vector.tensor_tensor(out=T1, in0=T1, in1=T2, op=ALU.add))
            v(lambda: nc.vector.reciprocal(out=INV, in_=T1))
            v(lambda: nc.vector.tensor_tensor(out=T1, in0=AR1, in1=lr, op=ALU.mult))
            v(lambda: nc.vector.tensor_tensor(out=T2, in0=ABI, in1=li, op=ALU.mult))
            v(lambda: nc.vector.tensor_tensor(out=T1, in0=T1, in1=T2, op=ALU.add))
            v(lambda: nc.vector.tensor_tensor(out=CR, in0=T1, in1=INV, op=ALU.mult))
            v(lambda: nc.vector.tensor_tensor(out=T1, in0=ABI, in1=lr, op=ALU.mult))
            v(lambda: nc.vector.tensor_tensor(out=T2, in0=AR1, in1=li, op=ALU.mult))
            v(lambda: nc.vector.tensor_tensor(out=T1, in0=T1, in1=T2, op=ALU.subtract))
            v(lambda: nc.vector.tensor_tensor(out=CI, in0=T1, in1=INV, op=ALU.mult))
            v(lambda: nc.vector.tensor_scalar(out=TH2, in0=TH, scalar1=float(TT), scalar2=None, op0=ALU.mult))
            self.range_reduce(TH2, r_pp, T2, ppi[:, 0, :], r_pp)
            a(lambda: nc.scalar.activation(out=S5, in_=TH2, func=AF.Sin))
            v(lambda: nc.vector.tensor_scalar(out=T1, in0=TH2, scalar1=math.pi / 2, scalar2=None, op0=ALU.add))
            self.range_reduce(T1, r_pp, T2, ppi[:, 0, :], r_pp)
            a(lambda: nc.scalar.activation(out=C5, in_=T1, func=AF.Sin))

            bbr = self.sb(es, "bbr", [128, NS, 16], F32)
            bbi = self.sb(es, "bbi", [128, NS, 16], F32)
            bt = self.sb(es, "bbt", [128, NS, 16], F32)
            r_bb = Res("bb")
            bre = sp[:, O_BRE:O_BRE + 384].rearrange("p (s h) -> p s h", h=16)
            bim = sp[:, O_BIM:O_BIM + 384].rearrange("p (s h) -> p s h", h=16)
            cre = sp[:, O_CRE:O_CRE + 384].rearrange("p (s h) -> p s h", h=16)
            cim = sp[:, O_CIM:O_CIM + 384].rearrange("p (s h) -> p s h", h=16)
            crb, cib = bc_last(CR, 16), bc_last(CI, 16)
            RB = [self.r_sp, r_pp, r_bb]
            self.V(lambda: nc.vector.tensor_tensor(out=bbr[:], in0=bre, in1=crb, op=ALU.mult), RB, [r_bb])
            self.V(lambda: nc.vector.tensor_tensor(out=bt[:], in0=bim, in1=cib, op=ALU.mult), RB, [r_bb])
            self.V(lambda: nc.vector.tensor_tensor(out=bbr[:], in0=bbr[:], in1=bt[:], op=ALU.subtract), RB, [r_bb])
            self.V(lambda: nc.vector.tensor_tensor(out=bbi[:], in0=bim, in1=crb, op=ALU.mult), RB, [r_bb])
            self.V(lambda: nc.vector.tensor_tensor(out=bt[:], in0=bre, in1=cib, op=ALU.mult), RB, [r_bb])
            self.V(lambda: nc.vector.tensor_tensor(out=bbi[:], in0=bbi[:], in1=bt[:], op=ALU.add), RB, [r_bb])

            bwr = self.sb(es, "bwr", [128, NS, 128], BF16)
            bwi = self.sb(es, "bwi", [128, NS, 128], BF16)
            cwr = self.sb(es, "cwr", [128, NS, 128], BF16)
            cwi = self.sb(es, "cwi", [128, NS, 128], BF16)
            r_bw, r_cw = Res("bw"), Res("cw")
            stg = [self.sb(es, "stg%d" % i, [128, 128], F32) for i in range(2)]
            r_stg = [Res("stg%d" % i) for i in range(2)]
            self.V(lambda: nc.vector.memset(cwr[:], 0.0), [], [r_cw])
            self.V(lambda: nc.vector.memset(cwi[:], 0.0), [], [r_cw])
            for sc in range(NS):
                c0 = (sc % 4) * 32
                for hf in range(2):
                    ps_ = slice(hf * 64, hf * 64 + 64)
                    cs_ = slice(c0 + hf * 16, c0 + hf * 16 + 16)
                    self.V(lambda: nc.vector.tensor_copy(cwr[ps_, sc, cs_], cre[ps_, sc, :]), [self.r_sp, r_cw], [r_cw])
                    self.V(lambda: nc.vector.tensor_scalar(out=cwi[ps_, sc, cs_], in0=cim[ps_, sc, :], scalar1=-1.0, scalar2=None,
                                                           op0=ALU.mult), [self.r_sp, r_cw], [r_cw])
            ti = 0
            for sc in range(NS):
                c0 = (sc % 4) * 32
                for src, dstw in ((bbr, bwr), (bbi, bwi)):
                    k = ti % 2
                    ti += 1
                    self.V(lambda: nc.vector.memset(stg[k][:], 0.0), [], [r_stg[k]])
                    for hf in range(2):
                        ps_ = slice(hf * 64, hf * 64 + 64)
                        cs_ = slice(c0 + hf * 16, c0 + hf * 16 + 16)
                        self.V(lambda: nc.vector.tensor_copy(stg[k][ps_, cs_], src[ps_, sc, :]), [r_bb, r_stg[k]], [r_stg[k]])
                    ps, r_ps = self.psum()
                    kb.tr(ps[:, 0:128], stg[k][:], self.identF, [r_stg[k], self.r_cf], [r_ps])
                    self.V(lambda: nc.vector.tensor_copy(dstw[:, sc, :], ps[:, 0:128]), [r_ps], [r_bw])

            cosT = self.sb(es, "cosT", [128, 4, TT], F32)
            sinT = self.sb(es, "sinT", [128, 4, TT], F32)
            rho = self.sb(es, "rho", [128, 4, TT], F32)
            r_tab = Res("tab")
            phs = self.sb(es, "phs", [128, TT], F32)
            phf = self.sb(es, "phf", [128, TT], F32)
            phi = self.sb(es, "phi", [128, TT], I32)
            r_ph, r_pht = Res("phs"), Res("pht")
            ut = [self.sb(es, "ut%d" % i, [128, TT], BF16) for i in range(4)]
            r_ut = [Res("ut%d" % i) for i in range(4)]
            ut2, r_ut2 = ut, r_ut
            pend_y = []
            NT = 4
            tmp = [self.sb(es, "dtmp%d" % i, [128, TT], F32) for i in range(NT)]
            r_tmp = [Res("dtmp%d" % i) for i in range(NT)]
            dre = [self.sb(es, "dre%d" % i, [128, TT], F32) for i in range(2)]
            dim = [self.sb(es, "dim%d" % i, [128, TT], F32) for i in range(2)]
            wre = [self.sb(es, "wre%d" % i, [128, TT], F32) for i in range(3)]
            wim = [self.sb(es, "wim%d" % i, [128, TT], F32) for i in range(3)]
            r_d = [Res("dd%d" % i) for i in range(2)]
            r_w = [Res("ww%d" % i) for i in range(3)]
            xre = [self.sb(es, "xre%d" % i, [128, 4, TT], BF16) for i in range(2)]
            xim = [self.sb(es, "xim%d" % i, [128, 4, TT], BF16) for i in range(2)]
            r_x = [Res("xx%d" % i) for i in range(2)]
            car = self.sb(es, "car", [128, 4, 4], F32)
            r_car4 = [Res("car%d" % i) for i in range(4)]
            so = [self.sb(es, "so%d" % i, [128, TT], F32) for i in range(2)]
            r_so = [Res("so%d" % i) for i in range(2)]
            yo = [self.sb(es, "dyo%d" % i, [128, TT], BF16) for i in range(2)]
            r_yo = [Res("dyo%d" % i) for i in range(2)]
            tix = 0
            wi = 0
            ptix = 0
            ptmp = [self.sb(es, "ptmp%d" % i, [128, TT], F32) for i in range(4)]
            r_ptmp = [Res("ptmp%d" % i) for i in range(4)]

            def load(i):
                uc_, tt_ = divmod(i, NTT)
                sl = i % 4
                kb.dma("sp", ut[sl][:], self.d_P.ap()[UC_OFF + uc_ * 128:UC_OFF + (uc_ + 1) * 128, tt_ * TT:(tt_ + 1) * TT],
                       [kb.R("P", UC_OFF // 128 + uc_)], [r_ut[sl]], "utld%d" % (sl % 2))

            load(0)
            for uc in range(6):
                for s4 in range(4):
                    sc = uc * 4 + s4
                    th_ap = TH[:, sc:sc + 1]
                    self.V(lambda: nc.vector.tensor_scalar(out=phs[:], in0=self.iota, scalar1=th_ap, scalar2=None, op0=ALU.mult),
                           [self.r_cf, r_pp], [r_ph])
                    self.V(lambda: nc.vector.tensor_copy(phf[:], phs[:]), [r_ph], [r_pht])
                    self.range_reduce(phf[:], r_pht, phs[:], phi[:], r_ph)
                    self.A(lambda: nc.scalar.activation(out=sinT[:, s4, :], in_=phf[:], func=AF.Sin), [r_pht, r_tab], [r_tab])
                    self.V(lambda: nc.vector.tensor_scalar(out=phs[:], in0=self.iota, scalar1=th_ap, scalar2=None, op0=ALU.mult),
                           [self.r_cf, r_pp, r_ph], [r_ph])
                    self.V(lambda: nc.vector.tensor_scalar(out=phf[:], in0=phs[:], scalar1=math.pi / 2, scalar2=None, op0=ALU.add),
                           [r_ph, r_pht], [r_pht])
                    self.range_reduce(phf[:], r_pht, phs[:], phi[:], r_ph)
                    self.A(lambda: nc.scalar.activation(out=cosT[:, s4, :], in_=phf[:], func=AF.Sin), [r_pht, r_tab], [r_tab])
                    self.A(lambda: nc.scalar.activation(out=rho[:, s4, :], in_=self.iota, func=AF.Identity, scale=0.0,
                                                        bias=MAG[:, sc:sc + 1]), [self.r_cf, r_pp, r_tab], [r_tab])
                self.V(lambda: nc.vector.memset(car[:], 0.0), [], r_car4)
                for tt in range(NTT):
                    xs = (uc * NTT + tt) % 2
                    usl = (uc * NTT + tt) % 4
                    if uc * NTT + tt + 1 < 6 * NTT:
                        load(uc * NTT + tt + 1)
                    tsl = slice(tt * TT, (tt + 1) * TT)
                    for s4 in range(4):
                        sc = uc * 4 + s4
                        self.pumpA(ag, 1 + (s4 % 2))
                        if l == 0:
                            self.pump_casts(1)
                        pr, r_pr = self.psum()
                        kb.mm(pr[:], [(bwr[:, sc, :], ut[usl][:])], [r_bw, r_ut[usl]], [r_pr])
                        pi_, r_pi = self.psum()
                        kb.mm(pi_[:], [(bwi[:, sc, :], ut[usl][:])], [r_bw, r_ut[usl]], [r_pi])
                        if s4 == 1 and pend_y:
                            pend_y.pop(0)()
                        c_, s_ = cosT[:, s4, :], sinT[:, s4, :]
                        k = wi % 2
                        kw = wi % 3
                        wi += 1
                        t = [tmp[(tix + i) % NT] for i in range(2)]
                        rt = [r_tmp[(tix + i) % NT] for i in range(2)]
                        tix += 2
                        self.V(lambda: nc.vector.tensor_tensor(out=t[0][:], in0=pr[:], in1=c_, op=ALU.mult), [r_pr, r_tab], [rt[0]])
                        self.V(lambda: nc.vector.tensor_tensor(out=t[1][:], in0=pi_[:], in1=s_, op=ALU.mult), [r_pi, r_tab], [rt[1]])
                        self.V(lambda: nc.vector.tensor_tensor(out=dre[k][:], in0=t[0][:], in1=t[1][:], op=ALU.add), [rt[0], rt[1]], [r_d[k]])
                        self.V(lambda: nc.vector.tensor_tensor(out=t[0][:], in0=pi_[:], in1=c_, op=ALU.mult), [r_pi, r_tab, rt[0]], [rt[0]])
                        self.V(lambda: nc.vector.tensor_tensor(out=t[1][:], in0=pr[:], in1=s_, op=ALU.mult), [r_pr, r_tab, rt[1]], [rt[1]])
                        self.V(lambda: nc.vector.tensor_tensor(out=dim[k][:], in0=t[0][:], in1=t[1][:], op=ALU.subtract), [rt[0], rt[1]], [r_d[k]])
                        self.V(lambda: nc.vector.tensor_tensor_scan(out=wre[kw][:], data0=rho[:, s4, :], data1=dre[k][:],
                                                                    initial=car[:, s4, 0:1], op0=ALU.mult, op1=ALU.add),
                               [r_tab, r_d[k], r_car4[s4]], [r_w[kw]])
                        self.V(lambda: nc.vector.tensor_tensor_scan(out=wim[kw][:], data0=rho[:, s4, :], data1=dim[k][:],
                                                                    initial=car[:, s4, 1:2], op0=ALU.mult, op1=ALU.add),
                               [r_tab, r_d[k], r_car4[s4]], [r_w[kw]])
                        if tt + 1 < NTT:
                            c5, s5 = C5[:, sc:sc + 1], S5[:, sc:sc + 1]
                            wl_r, wl_i = wre[kw][:, TT - 1:TT], wim[kw][:, TT - 1:TT]
                            self.V(lambda: nc.vector.tensor_scalar(out=car[:, s4, 2:3], in0=wl_i, scalar1=s5, scalar2=None, op0=ALU.mult),
                                   [r_w[kw], r_pp, r_car4[s4]], [r_car4[s4]])
                            self.V(lambda: nc.vector.tensor_scalar(out=car[:, s4, 3:4], in0=wl_r, scalar1=s5, scalar2=None, op0=ALU.mult),
                                   [r_w[kw], r_pp, r_car4[s4]], [r_car4[s4]])
                            self.V(lambda: nc.vector.scalar_tensor_tensor(out=car[:, s4, 0:1], in0=wl_r, scalar=c5, in1=car[:, s4, 2:3],
                                                                          op0=ALU.mult, op1=ALU.subtract), [r_w[kw], r_pp, r_car4[s4]], [r_car4[s4]])
                            self.V(lambda: nc.vector.scalar_tensor_tensor(out=car[:, s4, 1:2], in0=wl_i, scalar=c5, in1=car[:, s4, 3:4],
                                                                          op0=ALU.mult, op1=ALU.add), [r_w[kw], r_pp, r_car4[s4]], [r_car4[s4]])
                        t = [ptmp[(ptix + i) % 4] for i in range(2)]
                        rt = [r_ptmp[(ptix + i) % 4] for i in range(2)]
                        ptix += 2
                        P_ = lambda fn, rd, wr: kb.op("pool", fn, rd, wr)
                        self.V(lambda: nc.vector.tensor_tensor(out=t[0][:], in0=wre[kw][:], in1=c_, op=ALU.mult), [r_w[kw], r_tab], [rt[0]])
                        P_(lambda: nc.gpsimd.tensor_tensor(out=t[1][:], in0=wim[kw][:], in1=s_, op=ALU.mult), [r_w[kw], r_tab], [rt[1]])
                        P_(lambda: nc.gpsimd.tensor_tensor(out=xre[xs][:, s4, :], in0=t[0][:], in1=t[1][:], op=ALU.subtract),
                           [rt[0], rt[1]], [r_x[xs]])
                        P_(lambda: nc.gpsimd.tensor_tensor(out=t[0][:], in0=wre[kw][:], in1=s_, op=ALU.mult), [r_w[kw], r_tab, rt[0]], [rt[0]])
                        P_(lambda: nc.gpsimd.tensor_tensor(out=t[1][:], in0=wim[kw][:], in1=c_, op=ALU.mult), [r_w[kw], r_tab, rt[1]], [rt[1]])
                        P_(lambda: nc.gpsimd.tensor_tensor(out=xim[xs][:, s4, :], in0=t[0][:], in1=t[1][:], op=ALU.add),
                           [rt[0], rt[1]], [r_x[xs]])
                    def emit_y(uc=uc, tt=tt, xs=xs, usl=usl, tsl=tsl):
                        py, r_py = self.psum()
                        pairs = []
                        for s4 in range(4):
                            sc = uc * 4 + s4
                            pairs.append((cwr[:, sc, :], xre[xs][:, s4, :]))
                            pairs.append((cwi[:, sc, :], xim[xs][:, s4, :]))
                        kb.mm(py[:], pairs, [r_cw, r_x[xs]], [r_py])
                        os_ = tt % 2
                        self.V(lambda: nc.vector.scalar_tensor_tensor(out=so[os_][:], in0=ut2[usl][:], scalar=sp[:, O_DSK + uc:O_DSK + uc + 1],
                                                                      in1=py[:], op0=ALU.mult, op1=ALU.add),
                               [r_ut2[usl], self.r_sp, r_py], [r_so[os_]])
                        self.A(lambda: nc.scalar.activation(out=yo[os_][:], in_=so[os_][:], func=AF.Gelu_apprx_tanh), [r_so[os_]], [r_yo[os_]])
                        kb.dma("pool", self.d_YC0.ap()[uc * 128:(uc + 1) * 128, tsl], yo[os_][:], [r_yo[os_]], [kb.R("YC0", tt)],
                               "ycst%d" % os_)
                    pend_y.append(emit_y)
            while pend_y:
                pend_y.pop(0)()

    def phase_EF(self):
        nc, kb, l = self.nc, self.kb, self.l
        sp = self.sp
        with ExitStack() as es:
            arena = self.sb(es, "arena", [128, 44 * TT], BF16)
            hT = arena[:, :].rearrange("p (k t) -> p k t", t=TT)
            yaT = hT[:, 0:4, :]
            ybT = hT[:, 4:10, :]
            y0T = hT[:, 10:16, :]
            ycT = hT[:, 16:22, :]
            mT = hT[:, 22:38, :]
            r_in = Res("ef_in")
            r_yc, r_mT, r_hT = Res("ycT"), Res("mT"), Res("hT")
            rr = self.sb(es, "rr", [128, 16, TT], F32)
            r_rrc = [Res("rr%d" % i) for i in range(16)]
            x1b = self.sb(es, "x1b", [128, 16, TT], BF16)
            r_x1b = Res("x1b")
            sqb = self.sb(es, "sqb", [128, 16, TT], BF16)
            r_sqb = Res("sqb")
            gt = [self.sb(es, "gt%d" % i, [128, 3, TT], BF16) for i in range(2)]
            r_gt = [Res("gt%d" % i) for i in range(2)]
            xr = [self.sb(es, "xr%d" % i, [128, TT], F32) for i in range(2)]
            r_xr = [Res("xr%d" % i) for i in range(2)]
            tmp = [self.sb(es, "etmp%d" % i, [128, TT], F32) for i in range(6)]
            r_tmp = [Res("etmp%d" % i) for i in range(6)]
            lnt = self.ln_tiles(es, "e")
            tix = 0
            X_in = self.x_in.ap()
            X_out = self.x_out.ap()

            def layer_norm(goff, boff, final_store, tt):
                nonlocal tix
                rstd, r_rstd, mr, r_mr = self.ln_stats(lnt, [rr[:, c, :] for c in range(16)], r_rrc, float(D),
                                                       [x1b[:, c, :] for c in range(16)], [r_x1b],
                                                       [sqb[:, c, :] for c in range(16)], [r_sqb])
                for c in range(16):
                    k = tix % 6
                    tix += 1
                    self.V(lambda: nc.vector.tensor_tensor(out=tmp[k][:], in0=rr[:, c, :], in1=rstd[:], op=ALU.mult),
                           [r_rrc[c], r_rstd], [r_tmp[k]])
                    self.V(lambda: nc.vector.tensor_tensor(out=tmp[k][:], in0=tmp[k][:], in1=mr[:], op=ALU.subtract),
                           [r_tmp[k], r_mr], [r_tmp[k]])
                    self.A(lambda: nc.scalar.activation(out=rr[:, c, :], in_=tmp[k][:], func=AF.Identity,
                                                        scale=sp[:, goff + c:goff + c + 1], bias=sp[:, boff + c:boff + c + 1]),
                           [r_tmp[k], self.r_sp], [r_rrc[c]])
                    if not final_store:
                        self.A(lambda: nc.scalar.activation(out=x1b[:, c, :], in_=tmp[k][:], func=AF.Identity,
                                                            scale=sp[:, goff + c:goff + c + 1], bias=sp[:, boff + c:boff + c + 1]),
                               [r_tmp[k], self.r_sp], [r_x1b])
                if final_store:
                    dst = X_out[:, tt * TT:(tt + 1) * TT].rearrange("(c p) s -> p c s", p=128)
                    kb.dma("pool", dst, rr[:], r_rrc, [kb.R("X", l + 1, tt)], "xst")

            for tt in range(NTT):
                tsl = slice(tt * TT, (tt + 1) * TT)
                kb.dma("sp", yaT, self.d_YA.ap()[:, tsl].rearrange("(c p) s -> p c s", p=128), [kb.R("YA", i) for i in range(4)],
                       [r_in, r_hT], "efld")
                kb.dma("sp", ybT, self.d_YB.ap()[:, tsl].rearrange("(c p) s -> p c s", p=128), [kb.R("YB", tt)], [r_in, r_hT], "efld")
                kb.dma("sp", y0T, self.d_YC0.ap()[:, tsl].rearrange("(c p) s -> p c s", p=128), [kb.R("YC0", tt)], [r_in, r_hT], "efld")
                wg = []
                for h in range(2):
                    wt, r_w = self.ws.get(("G", l, tt, h))
                    wg.append((wt[:, 0:2304].rearrange("p (k c) -> p k c", c=768), r_w))
                for oc in range(6):
                    ps, r_ps = self.psum()
                    pairs = [(wg[k // 3][0][:, k % 3, oc * 128:(oc + 1) * 128], y0T[:, k, :]) for k in range(6)]
                    kb.mm(ps[:], pairs, [wg[0][1], wg[1][1], r_in], [r_ps])
                    k = tix % 6
                    tix += 1
                    self.A(lambda: nc.scalar.activation(out=tmp[k][:], in_=ps[:], func=AF.Sigmoid, bias=sp[:, O_BGLU + oc:O_BGLU + oc + 1],
                                                        scale=1.0), [r_ps, self.r_sp], [r_tmp[k]])
                    self.V(lambda: nc.vector.tensor_tensor(out=ycT[:, oc, :], in0=y0T[:, oc, :], in1=tmp[k][:], op=ALU.mult),
                           [r_in, r_tmp[k]], [r_yc])
                for dc in range(16):
                    gs = dc % 2
                    src = bass.AP(tensor=self.d_P, offset=(GL_OFF + dc * 128) * S + tt * TT, ap=[[S, 128], [D * S, 3], [1, TT]])
                    kb.dma("sp", gt[gs][:], src, [kb.R("P", GL_OFF // 128 + br * 16 + dc) for br in range(3)], [r_gt[gs]], "gtld%d" % gs)
                    wt, r_w = self.ws.get(("M", l, tt, dc))
                    w3 = wt[:, 0:2048].rearrange("p (k c) -> p k c", c=128)
                    pa, r_pa = self.psum()
                    kb.mm(pa[:], [(w3[:, k, :], yaT[:, k, :]) for k in range(4)], [r_w, r_in], [r_pa])
                    pb, r_pb = self.psum()
                    kb.mm(pb[:], [(w3[:, 4 + k, :], ybT[:, k, :]) for k in range(6)], [r_w, r_in], [r_pb])
                    pc, r_pc = self.psum()
                    kb.mm(pc[:], [(w3[:, 10 + k, :], ycT[:, k, :]) for k in range(6)], [r_w, r_yc], [r_pc])
                    k0, k1 = tix % 6, (tix + 1) % 6
                    tix += 2
                    self.V(lambda: nc.vector.tensor_tensor(out=tmp[k0][:], in0=pa[:], in1=gt[gs][:, 0, :], op=ALU.mult), [r_pa, r_gt[gs]], [r_tmp[k0]])
                    self.V(lambda: nc.vector.tensor_tensor(out=tmp[k1][:], in0=pb[:], in1=gt[gs][:, 1, :], op=ALU.mult), [r_pb, r_gt[gs]], [r_tmp[k1]])
                    k2 = tix % 6
                    tix += 1
                    self.V(lambda: nc.vector.tensor_tensor(out=tmp[k2][:], in0=pc[:], in1=gt[gs][:, 2, :], op=ALU.mult), [r_pc, r_gt[gs]], [r_tmp[k2]])
                    kb.op("pool", lambda: nc.gpsimd.tensor_tensor(out=tmp[k0][:], in0=tmp[k0][:], in1=tmp[k1][:], op=ALU.add), [r_tmp[k0], r_tmp[k1]], [r_tmp[k0]])
                    kb.op("pool", lambda: nc.gpsimd.tensor_tensor(out=mT[:, dc, :], in0=tmp[k0][:], in1=tmp[k2][:], op=ALU.add), [r_tmp[k0], r_tmp[k2]], [r_mT])
                for dc in range(16):
                    xs = dc % 2
                    kb.dma("sp", xr[xs][:], X_in[dc * 128:(dc + 1) * 128, tsl], [kb.R("X", l, tt)], [r_xr[xs]], "xrld%d" % xs)
                    wt, r_w = self.ws.get(("O", l, tt, dc))
                    w3 = wt[:, 0:2048].rearrange("p (k c) -> p k c", c=128)
                    ps, r_ps = self.psum()
                    kb.mm(ps[:], [(w3[:, k, :], mT[:, k, :]) for k in range(16)], [r_w, r_mT], [r_ps])
                    self.V(lambda: nc.vector.scalar_tensor_tensor(out=rr[:, dc, :], in0=xr[xs][:], scalar=float(ALPHA), in1=ps[:],
                                                                  op0=ALU.mult, op1=ALU.add), [r_xr[xs], r_ps], [r_rrc[dc]])
                    kb.op("pool", lambda: nc.gpsimd.tensor_copy(x1b[:, dc, :], rr[:, dc, :]), [r_rrc[dc]], [r_x1b])
                    self.A(lambda: nc.scalar.activation(out=sqb[:, dc, :], in_=rr[:, dc, :], func=AF.Square), [r_rrc[dc]], [r_sqb])
                layer_norm(O_L1G, O_L1B, False, tt)
                for j in range(44):
                    self.pump_casts(1)
                    wtg, r_wg = self.ws.get(("F1g", l, tt, j))
                    wtu, r_wu = self.ws.get(("F1u", l, tt, j))
                    g3 = wtg[:, 0:2048].rearrange("p (k c) -> p k c", c=128)
                    u3 = wtu[:, 0:2048].rearrange("p (k c) -> p k c", c=128)
                    pg, r_pg = self.psum()
                    kb.mm(pg[:], [(g3[:, k, :], x1b[:, k, :]) for k in range(16)], [r_wg, r_x1b], [r_pg])
                    pu, r_pu = self.psum()
                    kb.mm(pu[:], [(u3[:, k, :], x1b[:, k, :]) for k in range(16)], [r_wu, r_x1b], [r_pu])
                    k = tix % 6
                    tix += 1
                    self.A(lambda: nc.scalar.activation(out=tmp[k][:], in_=pg[:], func=AF.Silu), [r_pg], [r_tmp[k]])
                    self.V(lambda: nc.vector.tensor_tensor(out=hT[:, j, :], in0=pu[:], in1=tmp[k][:], op=ALU.mult),
                           [r_pu, r_tmp[k]], [r_hT, r_in, r_yc, r_mT])
                for dc in range(16):
                    ps, r_ps = self.psum()
                    wa, r_wa = self.ws.get(("F2", l, tt, dc, 0))
                    wb, r_wb = self.ws.get(("F2", l, tt, dc, 1))
                    a3 = wa[:, 0:2816].rearrange("p (k c) -> p k c", c=128)
                    b3 = wb[:, 0:2816].rearrange("p (k c) -> p k c", c=128)
                    pairs = [(a3[:, k, :], hT[:, k, :]) for k in range(22)] + [(b3[:, k, :], hT[:, 22 + k, :]) for k in range(22)]
                    kb.mm(ps[:], pairs, [r_wa, r_wb, r_hT], [r_ps])
                    self.V(lambda: nc.vector.scalar_tensor_tensor(out=rr[:, dc, :], in0=rr[:, dc, :], scalar=float(ALPHA), in1=ps[:],
                                                                  op0=ALU.mult, op1=ALU.add), [r_ps], [r_rrc[dc]])
                    kb.op("pool", lambda: nc.gpsimd.tensor_copy(x1b[:, dc, :], rr[:, dc, :]), [r_rrc[dc]], [r_x1b])
                    self.A(lambda: nc.scalar.activation(out=sqb[:, dc, :], in_=rr[:, dc, :], func=AF.Square), [r_rrc[dc]], [r_sqb])
                layer_norm(O_L2G, O_L2B, True, tt)


def bc_row(ap, n):
    a = ap.ap
    return bass.AP(tensor=ap.tensor, offset=ap.offset, ap=[list(a[0]), [0, n]])


def _t5_bucket(dist):
    max_exact = 16
    d = np.maximum(dist, 1).astype(np.float32)
    scale = (32 - max_exact) / math.log(2048 / max_exact)
    large = max_exact + (np.log(d / max_exact) * scale).astype(np.int32)
    large = np.minimum(large, 31)
    return np.where(dist < max_exact, dist, large).astype(np.int32)


def host_consts():
    cf = np.zeros((128, 1664), np.float32)
    cf[:, 0:128] = np.eye(128, dtype=np.float32)
    cf[:, 128:256] = np.tril(np.ones((128, 128), np.float32))
    cf[:, 256:768] = np.arange(512, dtype=np.float32)[None, :]
    cf[64, 768:896] = 1.0
    cf[:, 1024:1152] = 1.0
    oh = np.zeros((3, 33, EW), np.float32)
    for g, dil in enumerate(DILS):
        for s in range(EW):
            st = s - 127
            if 0 <= st <= 128:
                b = int(_t5_bucket(np.array([st * dil]))[0])
                oh[g, b, s] = 1.0
            else:
                oh[g, 32, s] = NEG
    return cf, oh


def pack_small(inp):
    sp = np.zeros((DEPTH, 128, NSP), np.float32)

    def fm(v, n):
        return np.ascontiguousarray(v.reshape(DEPTH, n, 128).transpose(0, 2, 1))

    def st(v):
        sh = v.shape
        v = v.reshape((DEPTH, 24, 2, 64) + sh[3:])
        v = np.moveaxis(v, 1, 3)
        return np.ascontiguousarray(v.reshape((DEPTH, 128, 24) + sh[3:]))

    sp[:, :, O_BA:O_BA + 102] = fm(inp["b_in"], 102)
    sp[:, :, O_SG:O_SG + 6] = fm(inp["sgu_ln_g"], 6)
    sp[:, :, O_SB:O_SB + 6] = fm(inp["sgu_ln_b"], 6)
    sp[:, :, O_LR:O_LR + 24] = st(inp["lam_re"])
    sp[:, :, O_LI:O_LI + 24] = st(inp["lam_im"])
    sp[:, :, O_LD:O_LD + 24] = st(np.repeat(inp["log_dt"][:, :, None], 64, axis=2))
    sp[:, :, O_BRE:O_BRE + 384] = st(inp["b_re"]).reshape(DEPTH, 128, 384)
    sp[:, :, O_BIM:O_BIM + 384] = st(inp["b_im"]).reshape(DEPTH, 128, 384)
    sp[:, :, O_CRE:O_CRE + 384] = st(np.swapaxes(inp["c_re"], 2, 3)).reshape(DEPTH, 128, 384)
    sp[:, :, O_CIM:O_CIM + 384] = st(np.swapaxes(inp["c_im"], 2, 3)).reshape(DEPTH, 128, 384)
    sp[:, :, O_DSK:O_DSK + 6] = fm(inp["d_skip"], 6)
    sp[:, :, O_BGLU:O_BGLU + 6] = fm(inp["b_glu"], 6)
    sp[:, :, O_L1G:O_L1G + 16] = fm(inp["ln1_g"], 16)
    sp[:, :, O_L1B:O_L1B + 16] = fm(inp["ln1_b"], 16)
    sp[:, :, O_L2G:O_L2G + 16] = fm(inp["ln2_g"], 16)
    sp[:, :, O_L2B:O_L2B + 16] = fm(inp["ln2_b"], 16)
    return sp


_PROG = {}


def make_in_maps(inp, cores):
    cf, oh = host_consts()
    sp = pack_small(inp)
    shared = {
        "w_in": np.ascontiguousarray(inp["w_in"], dtype=np.float32),
        "w_pa": np.ascontiguousarray(inp["w_pa"], dtype=np.float32),
        "w_pb": np.ascontiguousarray(inp["w_pb"], dtype=np.float32),
        "w_pc": np.ascontiguousarray(inp["w_pc"], dtype=np.float32),
        "w_o": np.ascontiguousarray(inp["w_o"], dtype=np.float32),
        "w_glu": np.ascontiguousarray(inp["w_glu"], dtype=np.float32),
        "w_ffn_in": np.ascontiguousarray(inp["w_ffn_in"], dtype=np.float32),
        "w_ffn_out": np.ascontiguousarray(inp["w_ffn_out"], dtype=np.float32),
        "w_s": np.ascontiguousarray(inp["w_s"], dtype=np.float32),
        "b_s": np.ascontiguousarray(inp["b_s"].reshape(DEPTH, 768), dtype=np.float32),
        "smallp": sp,
        "rel_bias": np.ascontiguousarray(inp["rel_bias"], dtype=np.float32),
        "constf": cf,
        "onehot": oh,
    }
    maps = []
    for b in cores:
        m = dict(shared)
        m["xT"] = np.ascontiguousarray(inp["x"][b].T, dtype=np.float32)
        maps.append(m)
    return maps


def kernel(**inputs):
    inp = {k: np.asarray(v) for k, v in inputs.items()}
    if "full" not in _PROG:
        _PROG["full"] = Prog()
    prog = _PROG["full"]
    maps = make_in_maps(inp, list(range(8)))
    res = run_bass_kernel_spmd(prog.nc, maps, core_ids=list(range(8)))
    out = np.stack([np.ascontiguousarray(res.results[b]["outT"].T) for b in range(8)], axis=0)
    return out.astype(np.float32)
```
